# Optimizing a Trainium2 kernel written in Bass

```python
import math
import jax, jax.numpy as jnp
from jax import lax
import numpy as np

D_MODEL = 1024
BATCH = 8
SEQ = 4096
DEPTH = 4

HEAD_DIM = 64
D_ATTN = D_MODEL // 2
D_RWKV = D_MODEL - D_ATTN
D_MIX = D_ATTN + D_RWKV
N_ATTN_HEADS = D_ATTN // HEAD_DIM
N_RWKV_HEADS = D_RWKV // HEAD_DIM
DILATED_BRANCHES = ((128, 1), (512, 4), (2048, 16))
Q_BLOCK = 128
LORA_DECAY = 32
LORA_AAA = 32
LORA_MV = 32
LORA_GATE = 96
D_FF = 4 * D_MODEL
NORM_EPS = 1e-6
GN_EPS = 64e-5
N_SHIFT = 3 * D_RWKV + LORA_DECAY + LORA_AAA + LORA_GATE
N_COLS_FIRST = 3 * D_ATTN + N_SHIFT
N_COLS_REST = N_COLS_FIRST + LORA_MV

kernel_name = "hybrid_dilated_attn_rwkv7_sandwich"


def _rms_norm(x, g):
    xf = x.astype(jnp.float32)
    y = xf * lax.rsqrt(jnp.mean(xf * xf, axis=-1, keepdims=True) + NORM_EPS)
    return (y * g.astype(jnp.float32)).astype(x.dtype)


def _token_shift(z, mu):
    prev = jnp.pad(z, ((0, 0), (1, 0), (0, 0)))[:, :-1]
    return z + (prev - z) * mu


def _dilated_branch(q, k, v, window, dilation):
    B, S, H, Dh = q.shape
    L = S // dilation
    K = window // dilation
    qb = math.gcd(L, Q_BLOCK)
    nb = L // qb

    def phase(t):
        return t.reshape(B, L, dilation, H, Dh).transpose(0, 2, 3, 1, 4)

    qs, ks, vs = phase(q), phase(k), phase(v)
    pad = ((0, 0), (0, 0), (0, 0), (K, 0), (0, 0))
    kp, vp = jnp.pad(ks, pad), jnp.pad(vs, pad)
    idx = jnp.arange(nb)[:, None] * qb + jnp.arange(qb + K)[None, :]
    kb = kp[:, :, :, idx, :]
    vb = vp[:, :, :, idx, :]
    qblk = qs.reshape(B, dilation, H, nb, qb, Dh)
    s = jnp.einsum('bdhnqc,bdhnkc->bdhnqk', qblk, kb) * (1.0 / math.sqrt(Dh))
    j = jnp.arange(qb)[:, None]
    m_idx = jnp.arange(qb + K)[None, :]
    dist = j + K - m_idx
    blk = jnp.arange(nb)[:, None, None]
    valid = (dist >= 0) & (dist <= K) & (blk * qb + m_idx - K >= 0)
    s = jnp.where(valid, s, -jnp.inf)
    m = jnp.max(s, axis=-1)
    p = jnp.exp(s - m[..., None])
    l = jnp.sum(p, axis=-1)
    acc = jnp.einsum('bdhnqk,bdhnkc->bdhnqc', p, vb)
    acc = acc.reshape(B, dilation, H, L, Dh).transpose(0, 3, 1, 2, 4).reshape(B, S, H, Dh)
    m = m.reshape(B, dilation, H, L).transpose(0, 3, 1, 2).reshape(B, S, H)
    l = l.reshape(B, dilation, H, L).transpose(0, 3, 1, 2).reshape(B, S, H)
    return m, l, acc


def _dilated_attention(q, k, v):
    B, S, _ = q.shape
    heads = lambda t: t.astype(jnp.float32).reshape(B, S, N_ATTN_HEADS, HEAD_DIM)
    qh, kh, vh = heads(q), heads(k), heads(v)
    outs = [_dilated_branch(qh, kh, vh, w, d) for (w, d) in DILATED_BRANCHES]
    m_all = jnp.maximum(jnp.maximum(outs[0][0], outs[1][0]), outs[2][0])
    num = sum(jnp.exp(m - m_all)[..., None] * acc for (m, l, acc) in outs)
    den = sum(jnp.exp(m - m_all) * l for (m, l, acc) in outs)
    o = num / den[..., None]
    return o.reshape(B, S, D_ATTN).astype(q.dtype)


def _rwkv7_scan(r, w, k, v, kk, b):
    B, S, H, N = r.shape

    def step(state, inp):
        r_t, w_t, k_t, v_t, kk_t, b_t = inp
        sa = jnp.einsum('bhij,bhj->bhi', state, kk_t)
        state = (state * w_t[:, :, None, :]
                 - sa[..., None] * b_t[:, :, None, :]
                 + v_t[..., None] * k_t[:, :, None, :])
        y = jnp.einsum('bhij,bhj->bhi', state, r_t)
        return state, y

    xs = tuple(jnp.moveaxis(t, 1, 0) for t in (r, w, k, v, kk, b))
    state0 = jnp.zeros((B, H, N, N), jnp.float32)
    _, ys = lax.scan(step, state0, xs)
    return jnp.moveaxis(ys, 0, 1)


def _rwkv7(r, k, v, xw, xa, xg, w0, w_up, a0, a_up, g_up, k_k, k_a, r_k, gn_w, gn_b):
    B, S, _ = r.shape
    H, N = N_RWKV_HEADS, HEAD_DIM
    f32 = jnp.float32
    heads = lambda t: t.astype(f32).reshape(B, S, H, N)
    logw = -jax.nn.softplus(-(w0 + jnp.tanh(xw) @ w_up).astype(f32)) - 0.5
    decay = jnp.exp(-jnp.exp(logw))
    a = jax.nn.sigmoid((a0 + xa @ a_up).astype(f32))
    g = (jax.nn.sigmoid(xg) @ g_up).astype(f32)
    kk = heads(k * k_k)
    kk = kk / jnp.maximum(jnp.sqrt(jnp.sum(kk * kk, axis=-1, keepdims=True)), 1e-12)
    k_mod = k.astype(f32) * (1.0 + (a - 1.0) * k_a.astype(f32))
    rh, kh, vh, ah, wh = heads(r), heads(k_mod), heads(v), heads(a), heads(decay)
    y = _rwkv7_scan(rh, wh, kh, vh, kk, kk * ah)
    mean = jnp.mean(y, axis=-1, keepdims=True)
    var = jnp.mean(jnp.square(y - mean), axis=-1, keepdims=True)
    yn = ((y - mean) * lax.rsqrt(var + GN_EPS)).reshape(B, S, D_RWKV)
    yn = yn * gn_w.astype(f32) + gn_b.astype(f32)
    bonus = jnp.sum(rh * kh * r_k.astype(f32), axis=-1, keepdims=True) * vh
    out = (yn + bonus.reshape(B, S, D_RWKV)) * g
    return out.astype(r.dtype)


def setup_inputs(seed: int = 0) -> dict:
    key = jax.random.key(seed)
    ks = iter(jax.random.split(key, 40))
    f32 = jnp.float32
    nrm = lambda shape, scale: jax.random.normal(next(ks), shape, f32) * scale
    gain = lambda shape: 1.0 + nrm(shape, 0.05)
    L1 = DEPTH - 1
    return {
        "x": nrm((BATCH, SEQ, D_MODEL), 1.0),
        "norm_mix_pre": gain((DEPTH, D_MODEL)),
        "norm_mix_post": gain((DEPTH, D_MODEL)),
        "norm_ffn_pre": gain((DEPTH, D_MODEL)),
        "norm_ffn_post": gain((DEPTH, D_MODEL)),
        "w_in_first": nrm((D_MODEL, N_COLS_FIRST), D_MODEL ** -0.5),
        "w_in_rest": nrm((L1, D_MODEL, N_COLS_REST), D_MODEL ** -0.5),
        "mu_shift": jax.random.uniform(next(ks), (DEPTH, N_SHIFT), f32),
        "mu_shift_mv": jax.random.uniform(next(ks), (L1, LORA_MV), f32),
        "attn_out_gain": gain((DEPTH, D_ATTN)),
        "decay_w0": jax.random.uniform(next(ks), (DEPTH, D_RWKV), f32, -5.0, 0.0),
        "decay_up": nrm((DEPTH, LORA_DECAY, D_RWKV), 0.5 * LORA_DECAY ** -0.5),
        "aaa_a0": nrm((DEPTH, D_RWKV), 0.1),
        "aaa_up": nrm((DEPTH, LORA_AAA, D_RWKV), 0.5 * LORA_AAA ** -0.5),
        "mv_v0": nrm((L1, D_RWKV), 0.5),
        "mv_up": nrm((L1, LORA_MV, D_RWKV), 0.5 * LORA_MV ** -0.5),
        "gate_up": nrm((DEPTH, LORA_GATE, D_RWKV), LORA_GATE ** -0.5),
        "k_k": 0.85 + nrm((DEPTH, D_RWKV), 0.1),
        "k_a": 1.0 + nrm((DEPTH, D_RWKV), 0.1),
        "r_k": nrm((DEPTH, N_RWKV_HEADS, HEAD_DIM), 0.1),
        "gn_w": gain((DEPTH, D_RWKV)),
        "gn_b": nrm((DEPTH, D_RWKV), 0.02),
        "w_out": nrm((DEPTH, D_MIX, D_MODEL), D_MIX ** -0.5),
        "w_ffn_up": nrm((DEPTH, D_MODEL, D_FF), D_MODEL ** -0.5),
        "w_ffn_down": nrm((DEPTH, D_FF, D_MODEL), D_FF ** -0.5),
    }


def reference(x, norm_mix_pre, norm_mix_post, norm_ffn_pre, norm_ffn_post,
              w_in_first, w_in_rest, mu_shift, mu_shift_mv, attn_out_gain,
              decay_w0, decay_up, aaa_a0, aaa_up, mv_v0, mv_up, gate_up,
              k_k, k_a, r_k, gn_w, gn_b, w_out, w_ffn_up, w_ffn_down):
    v_first = None
    o0, o1, o2 = 3 * D_RWKV, 3 * D_RWKV + LORA_DECAY, 3 * D_RWKV + LORA_DECAY + LORA_AAA
    for i in range(DEPTH):
        h = _rms_norm(x, norm_mix_pre[i])
        z = h @ (w_in_first if i == 0 else w_in_rest[i - 1])
        q = z[..., 0:D_ATTN]
        k = z[..., D_ATTN:2 * D_ATTN]
        v = z[..., 2 * D_ATTN:3 * D_ATTN]
        zs = _token_shift(z[..., 3 * D_ATTN:3 * D_ATTN + N_SHIFT], mu_shift[i])
        r_r = zs[..., 0:D_RWKV]
        k_r = zs[..., D_RWKV:2 * D_RWKV]
        v_r = zs[..., 2 * D_RWKV:3 * D_RWKV]
        xw = zs[..., o0:o1]
        xa = zs[..., o1:o2]
        xg = zs[..., o2:N_SHIFT]
        if i == 0:
            v_first = v_r
        else:
            xmv = _token_shift(z[..., N_COLS_FIRST:], mu_shift_mv[i - 1])
            vgate = jax.nn.sigmoid(mv_v0[i - 1] + xmv @ mv_up[i - 1])
            v_r = v_r + (v_first - v_r) * vgate
        attn = _rms_norm(_dilated_attention(q, k, v), attn_out_gain[i])
        rw = _rwkv7(r_r, k_r, v_r, xw, xa, xg, decay_w0[i], decay_up[i], aaa_a0[i],
                    aaa_up[i], gate_up[i], k_k[i], k_a[i], r_k[i], gn_w[i], gn_b[i])
        mixed = jnp.concatenate([attn, rw], axis=-1) @ w_out[i]
        x = x + _rms_norm(mixed, norm_mix_post[i])
        h = _rms_norm(x, norm_ffn_pre[i])
        f = jnp.square(jax.nn.relu(h @ w_ffn_up[i])) @ w_ffn_down[i]
        x = x + _rms_norm(f, norm_ffn_post[i])
    return x
```

```python
import contextlib
import numpy as np
import ml_dtypes
import concourse.bass as bass
import concourse.mybir as mybir
from concourse.bass_utils import run_bass_kernel_spmd

F32 = mybir.dt.float32
BF16 = mybir.dt.bfloat16
ALU = mybir.AluOpType
AF = mybir.ActivationFunctionType
AX = mybir.AxisListType

D = 1024
DA = 512
DR = 512
NH = 8
HD = 64
NCOLS = 3264
NZ = 1728
DFF = 4096
EPS = 1e-6
GN_EPS = 64e-5
C0 = float(np.exp(-0.5))
MBW = 2944
N_CORES = 8

ENGS = ("pe", "act", "dve", "pool", "sp")
N_DMA_SEMS = 40
import os
P2STOP = int(os.environ.get('P2STOP', '9'))


class Buf:
    __slots__ = ("name", "last_w", "readers", "excl")

    def __init__(self, name):
        self.name = name
        self.last_w = None
        self.readers = {}
        self.excl = False


class Op:
    __slots__ = ("eng", "fn", "deps", "sem", "val", "is_dma")

    def __init__(self, eng, fn, is_dma):
        self.eng = eng
        self.fn = fn
        self.deps = set()
        self.sem = None
        self.val = None
        self.is_dma = is_dma


class Sched:
    def __init__(self, nc):
        self.nc = nc
        self.ops = []
        self.bufs = {}
        self.dma_rr = 0
        self.dma_last = [None] * N_DMA_SEMS
        self.dma_cnt = [0] * N_DMA_SEMS
        self.last_op = {e: None for e in ENGS}
        self.dma_since = []
        self.sync_same_engine = True

    def B(self, *key):
        b = self.bufs.get(key)
        if b is None:
            b = Buf(key)
            self.bufs[key] = b
        return b

    def add(self, eng, fn, reads=(), writes=(), dma=False):
        op = Op(eng, fn, dma)
        deps = op.deps
        for b in reads:
            if b.last_w is not None:
                deps.add(b.last_w)
            if b.excl:
                for k, r in b.readers.items():
                    if k != eng:
                        deps.add(r)
        for b in writes:
            if b.last_w is not None:
                deps.add(b.last_w)
            for r in b.readers.values():
                deps.add(r)
        for b in reads:
            b.readers[id(op) if dma else eng] = op
        for b in writes:
            b.last_w = op
            b.readers = {}
        deps.discard(op)
        if dma:
            k = self.dma_rr
            self.dma_rr = (k + 1) % N_DMA_SEMS
            prev = self.dma_last[k]
            if prev is not None:
                deps.add(prev)
            self.dma_cnt[k] += 16
            op.sem = k
            op.val = self.dma_cnt[k]
            self.dma_last[k] = op
            self.dma_since.append(op)
        else:
            self.last_op[eng] = op
        self.ops.append(op)
        return op

    def barrier(self):
        lasts = [o for o in self.last_op.values() if o is not None] + list(self.dma_since)
        for e in ENGS:
            op = Op(e, None, False)
            op.deps = set(lasts)
            self.ops.append(op)
        self.dma_since = []
        for b in self.bufs.values():
            b.last_w = None
            b.readers = {}

    def _skip(self, d, op):
        if d.is_dma or op.is_dma or d.eng != op.eng:
            return False
        return d.eng == "pe" or not self.sync_same_engine

    def emit(self):
        nc = self.nc
        with contextlib.ExitStack() as st:
            esem = {e: st.enter_context(nc.semaphore("s_" + e)) for e in ENGS}
            dsem = [st.enter_context(nc.semaphore("d%d" % i)) for i in range(N_DMA_SEMS)]
            needed = set()
            for op in self.ops:
                for d in op.deps:
                    if d.is_dma or self._skip(d, op):
                        continue
                    needed.add(d)
            cnt = {e: 0 for e in ENGS}
            for op in self.ops:
                if op.is_dma:
                    op.sem = dsem[op.sem]
                else:
                    op.sem = esem[op.eng]
                    if op in needed:
                        cnt[op.eng] += 1
                        op.val = cnt[op.eng]
            per = {e: [op for op in self.ops if op.eng == e] for e in ENGS}
            block = st.enter_context(nc.Block())

            def run(engname, eng):
                waited = {}
                for op in per[engname]:
                    for d in op.deps:
                        if self._skip(d, op):
                            continue
                        key = id(d.sem)
                        if waited.get(key, 0) >= d.val:
                            continue
                        eng.wait_ge(d.sem, d.val)
                        waited[key] = d.val
                    if op.fn is None:
                        continue
                    ins = op.fn(eng)
                    if op.is_dma:
                        ins.then_inc(op.sem, 16)
                    elif op in needed:
                        ins.then_inc(op.sem, 1)

            @block.tensor
            def _(e):
                run("pe", e)

            @block.scalar
            def _(e):
                run("act", e)

            @block.vector
            def _(e):
                run("dve", e)

            @block.gpsimd
            def _(e):
                run("pool", e)

            @block.sync
            def _(e):
                run("sp", e)


class Arena:
    def __init__(self, tens, nwords):
        self.t = tens
        self.n = nwords
        self.off = 0

    def alloc(self, free_shape, dtype):
        n = int(np.prod(free_shape))
        words = n if dtype == F32 else (n + 1) // 2
        words = (words + 7) // 8 * 8
        assert self.off + words <= self.n, ("arena overflow", self.off, words, self.n)
        ap = self.t[:, self.off:self.off + words]
        self.off += words
        if dtype != F32:
            ap = ap.bitcast(dtype)
        ap = ap[:, 0:n]
        if len(free_shape) == 2:
            ap = ap.rearrange("p (a b) -> p a b", b=free_shape[1])
        elif len(free_shape) == 3:
            ap = ap.rearrange("p (a b c) -> p a b c", b=free_shape[1], c=free_shape[2])
        return ap


def build_program(S, depth, debug=False, phases=None):
    NT = S // 128
    nc = bass.Bass("TRN2", target_bir_lowering=False)
    okind = "ExternalOutput" if debug else "Internal"

    def din(name, shape, dt=F32):
        return nc.dram_tensor(name, list(shape), dt, kind="ExternalInput").ap()

    def dscr(name, shape, dt=F32):
        return nc.dram_tensor(name, list(shape), dt, kind=okind).ap()

    x_in = din("x", [S, D])
    w_in = din("w_in", [depth, D, NCOLS])
    w_out = din("w_out", [depth, D, D])
    w_up = din("w_up", [depth, D, DFF])
    w_dn = din("w_dn", [depth, DFF, D])
    gpre_d = din("gpre", [depth, 2, 128, 8])
    gpost_d = din("gpost", [depth, 2, D])
    mu_d = din("mu", [depth, NZ])
    rowp_d = din("rowp", [depth, 9 * 512])
    lora_d = din("lora", [depth, 4, 96, 512])
    cmask_d = din("cmask", [128, 2048])
    mbig_d = din("mbig", [128, MBW])
    out_d = nc.dram_tensor("out", [S, D], F32, kind="ExternalOutput").ap()

    qkT_d = dscr("qkT", [D, S], BF16)
    vbf_d = dscr("vbf", [S, DA], BF16)
    zr_d = dscr("zr", [S, NZ])
    vfirst_d = dscr("vfirst", [S, DR])
    atto_d = dscr("atto", [S, DA])
    rwo_d = dscr("rwo", [S, DR], BF16)
    xmid_d = dscr("xmid", [S, D])
    xres_d = dscr("xres", [S, D])

    S_ = Sched(nc)
    add = S_.add
    B = S_.B
    taps = {}

    def tap(name, ap, bufs):
        if not debug or name in taps:
            return
        tdt = nc.dram_tensor("tap_" + name, list(ap.shape), ap.dtype, kind="ExternalOutput").ap()
        taps[name] = tdt
        add("sp", lambda e: e.dma_start(out=tdt, in_=ap), reads=bufs, dma=True)

    with contextlib.ExitStack() as st:
        ARENA_WORDS = 50 * 1024
        arena_t = st.enter_context(nc.sbuf_tensor("arena", [128, ARENA_WORDS], F32))
        A = Arena(arena_t, ARENA_WORDS)
        pbank = [st.enter_context(nc.psum_tensor("pb%d" % i, [128, 512], F32)) for i in range(8)]
        Bbank = [B("bank", i) for i in range(8)]
        for b_ in Bbank:
            b_.excl = True

        def bank_bf(i):
            return pbank[i][:, :].bitcast(BF16)

        cmask = A.alloc((2048,), F32)
        ident_f = cmask[:, 0:128]
        ut_f = cmask[:, 128:256]
        mask4 = cmask[:, 256:768]
        mlow = cmask[:, 768:896]
        ones_f = cmask[:, 896:1024]
        bd16 = cmask[:, 1024:1152]
        lowbd = cmask[:, 1152:1280]
        ident_b = A.alloc((128,), BF16)
        Bc = B("consts")
        add("sp", lambda e: e.dma_start(out=cmask, in_=cmask_d), writes=[Bc], dma=True)
        add("dve", lambda e: e.tensor_copy(ident_b, ident_f), reads=[Bc], writes=[B("identb")])
        P0 = A.off
        S_.barrier()

        rot = {}

        def nb(lo=0, hi=8):
            i = rot.get((lo, hi), 0)
            rot[(lo, hi)] = i + 1
            return lo + i % (hi - lo)

        def rms_rstd(eng_sq_in, width, stat, Bin, Bstat, tag):
            pass

        def phase1(L, xsrc):
            A.off = P0
            NB = S // 512
            win = A.alloc((8, NCOLS), BF16)
            gpre = A.alloc((8,), F32)
            xt = [A.alloc((D,), F32) for _ in range(2)]
            junk = A.alloc((D,), BF16)
            stt_ = [A.alloc((4,), F32) for _ in range(2)]
            xn = [A.alloc((D,), BF16) for _ in range(2)]
            hT = [A.alloc((8, 512), BF16) for _ in range(2)]
            qks = [A.alloc((512,), BF16) for _ in range(3)]
            zt = [A.alloc((NZ,), F32) for _ in range(2)]
            vt = [A.alloc((512,), BF16) for _ in range(2)]
            for kc in range(8):
                add("pool", lambda e, kc=kc: e.dma_start(out=win[:, kc, :], in_=w_in[L, kc * 128:(kc + 1) * 128, :]),
                    writes=[B("win", kc)], dma=True)
            add("sp", lambda e: e.dma_start(out=gpre, in_=gpre_d[L, 0]), writes=[B("gpre")], dma=True)
            Bwin = [B("win", kc) for kc in range(8)]
            ev = {"i": 0}

            def evac(out, in_, reads, writes):
                ev["i"] += 1
                if ev["i"] % 2:
                    add("act", lambda e: e.copy(out, in_), reads=reads, writes=writes)
                else:
                    add("dve", lambda e: e.tensor_copy(out, in_), reads=reads, writes=writes)

            for b in range(NB):
                hs = b % 2
                BhT = B("hT", hs)
                for ts in range(4):
                    t = 4 * b + ts
                    s = t % 2
                    add("sp", lambda e, s=s, t=t: e.dma_start(out=xt[s], in_=xsrc[t * 128:(t + 1) * 128, :]),
                        writes=[B("xt", s)], dma=True)
                    add("act", lambda e, s=s: e.activation(out=junk, in_=xt[s], func=AF.Square, accum_out=stt_[s][:, 0:1]),
                        reads=[B("xt", s)], writes=[B("st0", s)])
                    add("act", lambda e, s=s: e.activation(out=stt_[s][:, 1:2], in_=stt_[s][:, 0:1], func=AF.Sqrt, bias=EPS, scale=1.0 / D),
                        reads=[B("st0", s)], writes=[B("st1", s)])
                    add("dve", lambda e, s=s: e.reciprocal(stt_[s][:, 2:3], stt_[s][:, 1:2]),
                        reads=[B("st1", s)], writes=[B("st2", s)])
                    add("dve", lambda e, s=s: e.tensor_scalar(xn[s], xt[s], stt_[s][:, 2:3], None, ALU.mult),
                        reads=[B("xt", s), B("st2", s)], writes=[B("xn", s)])
                    tb = nb(0, 2)

                    def tr(e, s=s, tb=tb):
                        pv = bank_bf(tb)
                        for kc in range(8):
                            ins = e.transpose(pv[:, kc * 128:(kc + 1) * 128], xn[s][:, kc * 128:(kc + 1) * 128], ident_b)
                        return ins
                    add("pe", tr, reads=[B("xn", s), B("identb")], writes=[Bbank[tb]])
                    add("dve", lambda e, tb=tb, hs=hs, ts=ts: e.tensor_tensor(
                        out=hT[hs][:, :, ts * 128:(ts + 1) * 128],
                        in0=bank_bf(tb).rearrange("p (k t) -> p k t", t=128),
                        in1=gpre.unsqueeze(2).broadcast_to([128, 8, 128]), op=ALU.mult),
                        reads=[Bbank[tb], B("gpre")], writes=[BhT])
                for m in range(8):
                    bk = nb(2, 8)

                    def mmf(e, m=m, bk=bk, hs=hs):
                        for kc in range(8):
                            ins = e.matmul(pbank[bk][:, :], win[:, kc, m * 128:(m + 1) * 128], hT[hs][:, kc, :],
                                           start=(kc == 0), stop=(kc == 7))
                        return ins
                    add("pe", mmf, reads=Bwin + [BhT], writes=[Bbank[bk]])
                    qs = (b * 8 + m) % 3
                    evac(qks[qs], pbank[bk][:, :], [Bbank[bk]], [B("qks", qs)])
                    add("sp", lambda e, m=m, b=b, qs=qs: e.dma_start(
                        out=qkT_d[m * 128:(m + 1) * 128, b * 512:(b + 1) * 512], in_=qks[qs]),
                        reads=[B("qks", qs)], dma=True)
                for ts in range(4):
                    t = 4 * b + ts
                    zs = t % 2
                    for (c0, c1) in ((1024, 1536), (1536, 2048), (2048, 2560), (2560, 3072), (3072, 3264)):
                        bk = nb(2, 8)
                        w = c1 - c0

                        def mmt(e, bk=bk, hs=hs, ts=ts, c0=c0, c1=c1, w=w):
                            for kc in range(8):
                                ins = e.matmul(pbank[bk][:, 0:w], hT[hs][:, kc, ts * 128:(ts + 1) * 128], win[:, kc, c0:c1],
                                               start=(kc == 0), stop=(kc == 7))
                            return ins
                        add("pe", mmt, reads=Bwin + [BhT], writes=[Bbank[bk]])
                        if c0 == 1024:
                            evac(vt[zs], pbank[bk][:, 0:512], [Bbank[bk]], [B("vt", zs)])
                        else:
                            evac(zt[zs][:, c0 - 1536:c1 - 1536], pbank[bk][:, 0:w], [Bbank[bk]], [B("zt", zs)])
                    add("sp", lambda e, t=t, zs=zs: e.dma_start(out=vbf_d[t * 128:(t + 1) * 128, :], in_=vt[zs]),
                        reads=[B("vt", zs)], dma=True)
                    add("sp", lambda e, t=t, zs=zs: e.dma_start(out=zr_d[t * 128:(t + 1) * 128, :], in_=zt[zs]),
                        reads=[B("zt", zs)], dma=True)
            S_.barrier()

        def phase2(L):
            A.off = P0
            mu = A.alloc((NZ,), F32)
            rowp = A.alloc((9, 512), F32)
            lora = A.alloc((4, 512), BF16)
            ST = [A.alloc((NH, HD), F32) for _ in range(2)]
            STb = [A.alloc((NH, HD), BF16) for _ in range(2)]
            cur = [A.alloc((NZ,), F32) for _ in range(2)]
            prv = [A.alloc((NZ,), F32) for _ in range(2)]
            vf = [A.alloc((512,), F32) for _ in range(2)]
            lo = A.alloc((192,), BF16)
            loT = A.alloc((4, 128), BF16)
            tmpa = A.alloc((512,), F32)
            tmpb = A.alloc((512,), F32)
            sg = A.alloc((512,), F32)
            cs = A.alloc((512,), F32)
            g_in = A.alloc((512,), F32)
            g_inv = A.alloc((512,), F32)
            g_prev = A.alloc((512,), F32)
            g_end = A.alloc((512,), F32)
            gCT = A.alloc((NH,), F32)
            a_sb = A.alloc((512,), F32)
            gate = [A.alloc((512,), F32) for _ in range(2)]
            v2 = [A.alloc((512,), F32) for _ in range(2)]
            kk = A.alloc((512,), F32)
            kap = A.alloc((512,), F32)
            kmod = A.alloc((512,), F32)
            b_sb = A.alloc((512,), F32)
            sm = A.alloc((64,), F32)
            rkc = [A.alloc((NH,), F32) for _ in range(2)]
            tm = [A.alloc((7, 512), BF16) for _ in range(2)]
            XT = [A.alloc((NH, 4, 128), BF16) for _ in range(2)]
            NG = 4
            AT = [A.alloc((512,), BF16) for _ in range(NG)]
            Q = [[A.alloc((4, 128), BF16) for _ in range(2)] for _ in range(NG)]
            ZT = [A.alloc((2, 128), BF16) for _ in range(NG)]
            GG = [A.alloc((256,), BF16) for _ in range(NG)]
            Lm = [A.alloc((128,), BF16) for _ in range(NG)]
            tmpm = [A.alloc((256,), F32) for _ in range(NG)]
            nW1 = [A.alloc((HD,), BF16) for _ in range(NG)]
            KU = [A.alloc((128,), BF16) for _ in range(NG)]
            Mc = [A.alloc((HD,), BF16) for _ in range(NG)]
            RhT = [A.alloc((128,), BF16) for _ in range(NG)]
            ysb = A.alloc((512,), F32)
            ysq = A.alloc((512,), F32)
            rwo = [A.alloc((512,), BF16) for _ in range(2)]

            add("sp", lambda e: e.dma_start(out=mu, in_=mu_d[L:L + 1, :].partition_broadcast(128)), writes=[B("mu")], dma=True)
            add("sp", lambda e: e.dma_start(out=rowp.rearrange("p a b -> p (a b)"), in_=rowp_d[L:L + 1, :].partition_broadcast(128)),
                writes=[B("rowp")], dma=True)
            for q in range(4):
                add("pool", lambda e, q=q: e.dma_start(out=lora[0:96, q, :], in_=lora_d[L, q]), writes=[B("lora", q)], dma=True)
            add("pool", lambda e: e.memset(ST[0].rearrange("p a b -> p (a b)"), 0.0), writes=[B("ST", 0)])
            add("pool", lambda e: e.memset(STb[0].rearrange("p a b -> p (a b)"), 0.0), writes=[B("STb", 0)])
            add("pool", lambda e: e.memset(STb[1].rearrange("p a b -> p (a b)"), 0.0), writes=[B("STb", 1)])
            for s_ in range(2):
                add("pool", lambda e, s_=s_: e.memset(XT[s_].rearrange("p a b c -> p (a b c)"), 0.0), writes=[B("XT", s_)])
            for i_ in range(NG):
                add("pool", lambda e, i_=i_: e.memset(Mc[i_], 0.0), writes=[B("Mc", i_)])
                add("pool", lambda e, i_=i_: e.memset(RhT[i_], 0.0), writes=[B("RhT", i_)])
            w0_bc, a0_bc, mv0_bc, kk_bc, ka_bc, rk_bc, gnw_bc, gnb_bc = [rowp[:, i, :] for i in range(8)]
            Brow = B("rowp")

            h3 = lambda ap: ap.rearrange("p (h j) -> p h j", j=HD)

            for c in range(NT):
                s = c % 2
                Bcur, Bprv = B("cur", s), B("prv", s)
                add("sp", lambda e, s=s, c=c: e.dma_start(out=cur[s], in_=zr_d[c * 128:(c + 1) * 128, :]), writes=[Bcur], dma=True)
                if c == 0:
                    add("pool", lambda e, s=s: e.memset(prv[s][0:1, :], 0.0), writes=[Bprv])
                    add("sp", lambda e, s=s: e.dma_start(out=prv[s][1:128, :], in_=zr_d[0:127, :]), writes=[Bprv], dma=True)
                else:
                    add("sp", lambda e, s=s, c=c: e.dma_start(out=prv[s], in_=zr_d[c * 128 - 1:c * 128 + 127, :]), writes=[Bprv], dma=True)
                if L > 0:
                    add("sp", lambda e, s=s, c=c: e.dma_start(out=vf[s], in_=vfirst_d[c * 128:(c + 1) * 128, :]), writes=[B("vf", s)], dma=True)
                add("pool", lambda e, s=s: e.tensor_tensor(out=prv[s], in0=prv[s], in1=cur[s], op=ALU.subtract), reads=[Bcur, Bprv], writes=[Bprv])
                add("dve", lambda e, s=s: e.tensor_tensor(out=prv[s], in0=prv[s], in1=mu, op=ALU.mult), reads=[Bprv, B("mu")], writes=[Bprv])
                add("dve", lambda e, s=s: e.tensor_tensor(out=prv[s], in0=prv[s], in1=cur[s], op=ALU.add), reads=[Bprv, Bcur], writes=[Bprv])
                zs = prv[s]
                r_, k_, v_ = zs[:, 0:512], zs[:, 512:1024], zs[:, 1024:1536]
                Blo = B("lo")
                add("act", lambda e, zs=zs: e.activation(out=lo[:, 0:32], in_=zs[:, 1536:1568], func=AF.Tanh), reads=[Bprv], writes=[Blo])
                add("act", lambda e, zs=zs: e.activation(out=lo[:, 64:160], in_=zs[:, 1600:1696], func=AF.Sigmoid), reads=[Bprv], writes=[Blo])
                add("dve", lambda e, zs=zs: e.tensor_copy(lo[:, 32:64], zs[:, 1568:1600]), reads=[Bprv], writes=[Blo])
                add("dve", lambda e, zs=zs: e.tensor_copy(lo[:, 160:192], zs[:, 1696:1728]), reads=[Bprv], writes=[Blo])
                bk = nb()

                def trlo(e, bk=bk):
                    pv = bank_bf(bk)
                    e.transpose(pv[0:32, 0:128], lo[:, 0:32], ident_b)
                    e.transpose(pv[0:32, 128:256], lo[:, 32:64], ident_b)
                    e.transpose(pv[0:96, 256:384], lo[:, 64:160], ident_b)
                    return e.transpose(pv[0:32, 384:512], lo[:, 160:192], ident_b)
                add("pe", trlo, reads=[Blo, B("identb")], writes=[Bbank[bk]])
                BloT = B("loT")
                for (q, r0) in ((0, 32), (1, 32), (2, 96), (3, 32)):
                    add("act", lambda e, bk=bk, q=q, r0=r0: e.copy(loT[0:r0, q, :], bank_bf(bk)[0:r0, q * 128:(q + 1) * 128]),
                        reads=[Bbank[bk]], writes=[BloT])
                bk = nb()
                add("pe", lambda e, bk=bk: e.matmul(pbank[bk][:, :], loT[0:32, 0, :], lora[0:32, 0, :], start=True, stop=True),
                    reads=[BloT, B("lora", 0)], writes=[Bbank[bk]])
                add("dve", lambda e, bk=bk: e.tensor_tensor(out=tmpa, in0=pbank[bk][:, :], in1=w0_bc, op=ALU.add),
                    reads=[Bbank[bk], Brow], writes=[B("tmpa")])
                add("act", lambda e: e.activation(out=sg, in_=tmpa, func=AF.Sigmoid), reads=[B("tmpa")], writes=[B("sg")])
                bk = nb()
                add("pe", lambda e, bk=bk: e.matmul(pbank[bk][:, :], loT[0:32, 1, :], lora[0:32, 1, :], start=True, stop=True),
                    reads=[BloT, B("lora", 1)], writes=[Bbank[bk]])
                add("dve", lambda e, bk=bk: e.tensor_tensor(out=tmpb, in0=pbank[bk][:, :], in1=a0_bc, op=ALU.add),
                    reads=[Bbank[bk], Brow], writes=[B("tmpb")])
                add("act", lambda e: e.activation(out=a_sb, in_=tmpb, func=AF.Sigmoid), reads=[B("tmpb")], writes=[B("a")])
                bk = nb()
                add("pe", lambda e, bk=bk: e.matmul(pbank[bk][:, :], loT[0:96, 2, :], lora[0:96, 3, :], start=True, stop=True),
                    reads=[BloT, B("lora", 3)], writes=[Bbank[bk]])
                add("act", lambda e, bk=bk, s=s: e.copy(gate[s], pbank[bk][:, :]), reads=[Bbank[bk]], writes=[B("gate", s)])
                Bv2 = B("v2", s)
                if L == 0:
                    add("pool", lambda e, s=s, v_=v_: e.tensor_copy(v2[s], v_), reads=[Bprv], writes=[Bv2])
                    add("sp", lambda e, s=s, c=c: e.dma_start(out=vfirst_d[c * 128:(c + 1) * 128, :], in_=v2[s]), reads=[Bv2], dma=True)
                else:
                    bk = nb()
                    add("pe", lambda e, bk=bk: e.matmul(pbank[bk][:, :], loT[0:32, 3, :], lora[0:32, 2, :], start=True, stop=True),
                        reads=[BloT, B("lora", 2)], writes=[Bbank[bk]])
                    add("dve", lambda e, bk=bk: e.tensor_tensor(out=tmpb, in0=pbank[bk][:, :], in1=mv0_bc, op=ALU.add),
                        reads=[Bbank[bk], Brow], writes=[B("tmpb")])
                    add("act", lambda e: e.activation(out=tmpb, in_=tmpb, func=AF.Sigmoid), reads=[B("tmpb")], writes=[B("tmpb")])
                    add("pool", lambda e, s=s, v_=v_: e.tensor_tensor(out=v2[s], in0=vf[s], in1=v_, op=ALU.subtract),
                        reads=[B("vf", s), Bprv], writes=[Bv2])
                    add("dve", lambda e, s=s: e.tensor_tensor(out=v2[s], in0=v2[s], in1=tmpb, op=ALU.mult), reads=[Bv2, B("tmpb")], writes=[Bv2])
                    add("dve", lambda e, s=s, v_=v_: e.tensor_tensor(out=v2[s], in0=v2[s], in1=v_, op=ALU.add), reads=[Bv2, Bprv], writes=[Bv2])
                bk_cs = nb()
                add("pe", lambda e, bk=bk_cs: e.matmul(pbank[bk][:, :], ut_f, sg, start=True, stop=True), reads=[B("sg"), Bc], writes=[Bbank[bk]])
                add("act", lambda e, bk=bk_cs: e.copy(cs, pbank[bk][:, :]), reads=[Bbank[bk]], writes=[B("cs")])
                bk_tot = nb()
                add("pe", lambda e, bk=bk_tot: e.matmul(pbank[bk][:, :], ones_f, sg, start=True, stop=True), reads=[B("sg"), Bc], writes=[Bbank[bk]])
                add("act", lambda e: e.activation(out=g_in, in_=cs, func=AF.Exp, scale=-C0), reads=[B("cs")], writes=[B("g_in")])
                add("act", lambda e: e.activation(out=g_inv, in_=cs, func=AF.Exp, scale=C0), reads=[B("cs")], writes=[B("g_inv")])
                add("pool", lambda e: e.tensor_tensor(out=tmpa, in0=cs, in1=sg, op=ALU.subtract), reads=[B("cs"), B("sg")], writes=[B("tmpa")])
                add("act", lambda e: e.activation(out=g_prev, in_=tmpa, func=AF.Exp, scale=-C0), reads=[B("tmpa")], writes=[B("g_prev")])
                add("dve", lambda e, bk=bk_tot: e.tensor_tensor(out=tmpb, in0=pbank[bk][:, :], in1=cs, op=ALU.subtract),
                    reads=[Bbank[bk], B("cs")], writes=[B("tmpb")])
                add("act", lambda e: e.activation(out=g_end, in_=tmpb, func=AF.Exp, scale=-C0), reads=[B("tmpb")], writes=[B("g_end")])
                bk = nb()

                def mmgc(e, bk=bk):
                    for h in range(NH):
                        ins = e.matmul(pbank[bk][0:64, h:h + 1], sg[:, h * 64:(h + 1) * 64], ones_f[:, 0:1], start=True, stop=True)
                    return ins
                add("pe", mmgc, reads=[B("sg"), Bc], writes=[Bbank[bk]])
                add("act", lambda e, bk=bk: e.activation(out=gCT[0:64, :], in_=pbank[bk][0:64, 0:NH], func=AF.Exp, scale=-C0),
                    reads=[Bbank[bk]], writes=[B("gCT")])
                Bsm = B("sm")
                add("dve", lambda e, k_=k_: e.tensor_tensor(out=kk, in0=k_, in1=kk_bc, op=ALU.mult), reads=[Bprv, Brow], writes=[B("kk")])
                add("pool", lambda e: e.tensor_tensor(out=tmpa, in0=kk, in1=kk, op=ALU.mult), reads=[B("kk")], writes=[B("tmpa")])
                add("dve", lambda e: e.tensor_reduce(out=sm[:, 0:8], in_=h3(tmpa), axis=AX.X, op=ALU.add), reads=[B("tmpa")], writes=[Bsm])
                add("act", lambda e: e.activation(out=sm[:, 8:16], in_=sm[:, 0:8], func=AF.Sqrt), reads=[Bsm], writes=[Bsm])
                add("dve", lambda e: e.tensor_scalar(sm[:, 8:16], sm[:, 8:16], 1e-12, None, ALU.max), reads=[Bsm], writes=[Bsm])
                add("dve", lambda e: e.reciprocal(sm[:, 16:24], sm[:, 8:16]), reads=[Bsm], writes=[Bsm])
                add("dve", lambda e: e.tensor_tensor(out=h3(kap), in0=h3(kk), in1=sm[:, 16:24].unsqueeze(2).broadcast_to([128, NH, HD]), op=ALU.mult),
                    reads=[B("kk"), Bsm], writes=[B("kap")])
                add("dve", lambda e: e.scalar_tensor_tensor(out=kmod, in0=a_sb, scalar=-1.0, in1=ka_bc, op0=ALU.add, op1=ALU.mult),
                    reads=[B("a"), Brow], writes=[B("kmod")])
                add("dve", lambda e, k_=k_: e.scalar_tensor_tensor(out=kmod, in0=kmod, scalar=1.0, in1=k_, op0=ALU.add, op1=ALU.mult),
                    reads=[B("kmod"), Bprv], writes=[B("kmod")])
                add("pool", lambda e: e.tensor_tensor(out=b_sb, in0=kap, in1=a_sb, op=ALU.mult), reads=[B("kap"), B("a")], writes=[B("b")])
                add("pool", lambda e, r_=r_: e.tensor_tensor(out=tmpa, in0=r_, in1=kmod, op=ALU.mult), reads=[Bprv, B("kmod")], writes=[B("tmpa")])
                add("dve", lambda e: e.tensor_tensor(out=tmpa, in0=tmpa, in1=rk_bc, op=ALU.mult), reads=[B("tmpa"), Brow], writes=[B("tmpa")])
                add("dve", lambda e, s=s: e.tensor_reduce(out=rkc[s], in_=h3(tmpa), axis=AX.X, op=ALU.add), reads=[B("tmpa")], writes=[B("rkc", s)])
                Btm = B("tm", s)
                T = tm[s]
                add("dve", lambda e, T=T, r_=r_: e.tensor_tensor(out=T[:, 0, :], in0=r_, in1=g_in, op=ALU.mult), reads=[Bprv, B("g_in")], writes=[B("tm0", s)])
                add("pool", lambda e, T=T: e.tensor_tensor(out=T[:, 1, :], in0=kap, in1=g_prev, op=ALU.mult), reads=[B("kap"), B("g_prev")], writes=[B("tm1", s)])
                add("dve", lambda e, T=T: e.tensor_tensor(out=T[:, 2, :], in0=b_sb, in1=g_inv, op=ALU.mult), reads=[B("b"), B("g_inv")], writes=[B("tm2", s)])
                add("pool", lambda e, T=T: e.tensor_tensor(out=T[:, 3, :], in0=kmod, in1=g_inv, op=ALU.mult), reads=[B("kmod"), B("g_inv")], writes=[B("tm3", s)])
                add("dve", lambda e, T=T: e.tensor_tensor(out=T[:, 4, :], in0=b_sb, in1=g_end, op=ALU.mult), reads=[B("b"), B("g_end")], writes=[B("tm4", s)])
                add("pool", lambda e, T=T: e.tensor_tensor(out=T[:, 5, :], in0=kmod, in1=g_end, op=ALU.mult), reads=[B("kmod"), B("g_end")], writes=[B("tm5", s)])
                add("act", lambda e, T=T, s=s: e.copy(T[:, 6, :], v2[s]), reads=[Bv2], writes=[B("tm6", s)])
                if c == 0 and L == depth - 1:
                    tap("zs", prv[s], [Bprv]); tap("sg", sg, [B("sg")]); tap("a", a_sb, [B("a")]); tap("cs", cs, [B("cs")])
                    tap("g_in", g_in, [B("g_in")]); tap("g_inv", g_inv, [B("g_inv")]); tap("g_prev", g_prev, [B("g_prev")]); tap("g_end", g_end, [B("g_end")])
                    tap("kap", kap, [B("kap")]); tap("kmod", kmod, [B("kmod")]); tap("b", b_sb, [B("b")]); tap("v2", v2[s], [Bv2])
                    tap("tm", tm[s], [B("tm%d" % q, s) for q in range(7)]); tap("gCT", gCT, [B("gCT")]); tap("gate", gate[s], [B("gate", s)])
                if P2STOP == 0:
                    continue
                BXT = B("XT", s)
                for (kind, src) in ((0, 1), (1, 0), (2, 2), (3, 3)):
                    bk = nb()

                    def trx(e, bk=bk, T=T, src=src):
                        pv = bank_bf(bk)
                        for h in range(NH):
                            ins = e.transpose(pv[0:64, h * 128:(h + 1) * 128], T[:, src, h * 64:(h + 1) * 64], ident_b)
                        return ins
                    add("pe", trx, reads=[B("tm%d" % src, s), B("identb")], writes=[Bbank[bk]])
                    eng = "act" if kind % 2 else "dve"
                    if eng == "act":
                        add("act", lambda e, bk=bk, kind=kind, s=s: e.copy(XT[s][0:64, :, kind, :], bank_bf(bk)[0:64, :].rearrange("p (h t) -> p h t", t=128)),
                            reads=[Bbank[bk]], writes=[BXT])
                    else:
                        add("dve", lambda e, bk=bk, kind=kind, s=s: e.tensor_copy(XT[s][0:64, :, kind, :], bank_bf(bk)[0:64, :].rearrange("p (h t) -> p h t", t=128)),
                            reads=[Bbank[bk]], writes=[BXT])
                if c == 0 and L == depth - 1:
                    tap("XT", XT[s][0:64], [BXT])
                if P2STOP == 1:
                    continue
                X = XT[s]
                STo, STn = ST[c % 2], ST[(c + 1) % 2]
                BSTo, BSTn = B("ST", c % 2), B("ST", (c + 1) % 2)
                STbo, STbn = STb[c % 2], STb[(c + 1) % 2]
                BSTbo, BSTbn = B("STb", c % 2), B("STb", (c + 1) % 2)
                bkY = 7
                for hg in range(NH // NG):
                    heads = [hg * NG + i for i in range(NG)]
                    bA = {}
                    for i, h in enumerate(heads):
                        bk = nb(0, 7)
                        bA[h] = bk

                        def mma(e, bk=bk, h=h, X=X):
                            e.matmul(pbank[bk][:, 0:256], X[:, h, 2, :], X[:, h, 0:2, :], start=True, stop=True)
                            return e.matmul(pbank[bk][:, 256:512], X[:, h, 3, :], X[:, h, 0:2, :], start=True, stop=True)
                        add("pe", mma, reads=[BXT], writes=[Bbank[bk]])
                        add("dve", lambda e, bk=bk, i=i: e.tensor_tensor(out=AT[i], in0=pbank[bk][:, :], in1=mask4, op=ALU.mult),
                            reads=[Bbank[bk], Bc], writes=[B("AT", i)])
                        bk2 = nb(0, 7)
                        add("pe", lambda e, bk2=bk2, h=h, X=X: e.matmul(pbank[bk2][:, 0:128], X[:, h, 0, :], X[:, h, 2, :], start=True, stop=True),
                            reads=[BXT], writes=[Bbank[bk2]])
                        add("dve", lambda e, bk2=bk2, i=i: e.tensor_tensor(out=Q[i][0][:, 2, :], in0=pbank[bk2][:, 0:128], in1=lowbd, op=ALU.mult),
                            reads=[Bbank[bk2], Bc], writes=[B("Qw", i, 0)])
                        add("dve", lambda e, bk2=bk2, i=i: e.tensor_tensor(out=Lm[i], in0=pbank[bk2][:, 0:128], in1=mlow, op=ALU.mult),
                            reads=[Bbank[bk2], Bc], writes=[B("Lm", i)])
                        add("pool", lambda e, i=i: e.tensor_tensor(out=Q[i][0][:, 0, :], in0=AT[i][:, 0:128], in1=bd16, op=ALU.mult),
                            reads=[B("AT", i), Bc], writes=[B("Qy", i, 0)])
                        add("pool", lambda e, i=i: e.tensor_tensor(out=Q[i][1][:, 1:4:2, :], in0=Q[i][0][:, 0:3:2, :], in1=ident_f.unsqueeze(1).broadcast_to([128, 2, 128]), op=ALU.add),
                            reads=[B("Qy", i, 0), B("Qw", i, 0), Bc], writes=[B("Qzt", i, 1)])
                    if P2STOP == 2:
                        continue
                    for lev in range(0, 4):
                        p_, n_ = lev % 2, (lev + 1) % 2
                        for i, h in enumerate(heads):
                            bk = nb(0, 7)
                            Qp, Qn = Q[i][p_], Q[i][n_]
                            if lev == 0:
                                def mml(e, bk=bk, Qp=Qp):
                                    e.matmul(pbank[bk][:, 0:128], Qp[:, 2, :], Qp[:, 0, :], start=True, stop=True)
                                    return e.matmul(pbank[bk][:, 128:256], Qp[:, 0, :], Qp[:, 2, :], start=True, stop=True)
                                add("pe", mml, reads=[B("Qy", i, p_), B("Qw", i, p_)], writes=[Bbank[bk]])
                                add("act", lambda e, bk=bk, Qn=Qn: e.copy(Qn[:, 0:3:2, :], pbank[bk][:, 0:256].rearrange("p (a b) -> p a b", b=128)),
                                    reads=[Bbank[bk]], writes=[B("Qy", i, n_), B("Qw", i, n_)])
                            elif lev < 3:
                                def mml(e, bk=bk, Qp=Qp):
                                    e.matmul(pbank[bk][:, 0:256], Qp[:, 2, :], Qp[:, 0:2, :], start=True, stop=True)
                                    return e.matmul(pbank[bk][:, 256:512], Qp[:, 0, :], Qp[:, 2:4, :], start=True, stop=True)
                                add("pe", mml, reads=[B("Qy", i, p_), B("Qw", i, p_), B("Qzt", i, p_)], writes=[Bbank[bk]])
                                pv4 = pbank[bk][:, :].rearrange("p (a b) -> p a b", b=128)
                                add("act", lambda e, pv4=pv4, Qn=Qn: e.copy(Qn[:, 0:3:2, :], pv4[:, 0:3:2, :]),
                                    reads=[Bbank[bk]], writes=[B("Qy", i, n_), B("Qw", i, n_)])
                                add("dve", lambda e, pv4=pv4, Qn=Qn, Qp=Qp: e.tensor_tensor(out=Qn[:, 1:4:2, :], in0=pv4[:, 1:4:2, :], in1=Qp[:, 1:4:2, :], op=ALU.add),
                                    reads=[Bbank[bk], B("Qzt", i, p_)], writes=[B("Qzt", i, n_)])
                            else:
                                def mml(e, bk=bk, Qp=Qp):
                                    e.matmul(pbank[bk][:, 0:128], Qp[:, 2, :], Qp[:, 1, :], start=True, stop=True)
                                    return e.matmul(pbank[bk][:, 128:256], Qp[:, 0, :], Qp[:, 3, :], start=True, stop=True)
                                add("pe", mml, reads=[B("Qy", i, p_), B("Qw", i, p_), B("Qzt", i, p_)], writes=[Bbank[bk]])
                                add("dve", lambda e, bk=bk, i=i, Qp=Qp: e.tensor_tensor(out=ZT[i], in0=pbank[bk][:, 0:256].rearrange("p (a b) -> p a b", b=128),
                                                                                      in1=Qp[:, 1:4:2, :], op=ALU.add),
                                    reads=[Bbank[bk], B("Qzt", i, p_)], writes=[B("ZT", i)])
                    for mi in range(3):
                        MMm = cmask[:, 1280 + 256 * mi:1536 + 256 * mi]
                        for i, h in enumerate(heads):
                            bk = nb(0, 7)

                            def mmg(e, bk=bk, i=i):
                                e.matmul(pbank[bk][:, 0:128], Lm[i], ZT[i][:, 0, :], start=True, stop=True)
                                return e.matmul(pbank[bk][:, 128:256], AT[i][:, 0:128], ZT[i][:, 1, :], start=True, stop=True)
                            add("pe", mmg, reads=[B("Lm", i), B("AT", i), B("ZT", i)], writes=[Bbank[bk]])
                            add("act", lambda e, bk=bk, i=i: e.copy(GG[i], pbank[bk][:, 0:256]), reads=[Bbank[bk]], writes=[B("GG", i)])
                            bk = nb(0, 7)

                            def mmh(e, bk=bk, i=i):
                                e.matmul(pbank[bk][:, 0:128], ZT[i][:, 1, :], GG[i][:, 0:128], start=True, stop=True)
                                return e.matmul(pbank[bk][:, 128:256], ZT[i][:, 0, :], GG[i][:, 128:256], start=True, stop=True)
                            add("pe", mmh, reads=[B("GG", i), B("ZT", i)], writes=[Bbank[bk]])
                            add("dve", lambda e, bk=bk, i=i, MMm=MMm: e.tensor_tensor(out=tmpm[i], in0=pbank[bk][:, 0:256], in1=MMm, op=ALU.mult),
                                reads=[Bbank[bk], Bc], writes=[B("tmpm", i)])
                            add("pool", lambda e, i=i: e.tensor_tensor(out=ZT[i].rearrange("p a b -> p (a b)"), in0=ZT[i].rearrange("p a b -> p (a b)"), in1=tmpm[i], op=ALU.subtract),
                                reads=[B("ZT", i), B("tmpm", i)], writes=[B("ZT", i)])
                    if c == 0 and hg == 0 and L == depth - 1:
                        tap("AT", AT[0], [B("AT", 0)]); tap("Zf", ZT[0][:, 0, :], [B("ZT", 0)])
                    if P2STOP == 3:
                        continue
                    for i, h in enumerate(heads):
                        Zf = ZT[i][:, 0, :]
                        BZf = B("ZT", i)
                        vb_h = T[:, 6, h * 64:(h + 1) * 64]
                        bk = nb(0, 7)
                        add("pe", lambda e, bk=bk, i=i, vb_h=vb_h: e.matmul(pbank[bk][:, 0:64], AT[i][:, 256:384], vb_h, start=True, stop=True),
                            reads=[B("AT", i), B("tm6", s)], writes=[Bbank[bk]])
                        add("act", lambda e, bk=bk, i=i: e.mul(nW1[i], pbank[bk][:, 0:64], -1.0), reads=[Bbank[bk]], writes=[B("nW1", i)])
                        bk = nb(0, 7)

                        def mmku(e, bk=bk, i=i, h=h, Zf=Zf, T=T):
                            e.matmul(pbank[bk][:, 0:64], Zf, T[:, 1, h * 64:(h + 1) * 64], start=True, stop=True)
                            return e.matmul(pbank[bk][:, 64:128], Zf, nW1[i], start=True, stop=True)
                        add("pe", mmku, reads=[BZf, B("tm1", s), B("nW1", i)], writes=[Bbank[bk]])
                        add("dve", lambda e, bk=bk, i=i: e.tensor_copy(KU[i], pbank[bk][:, 0:128]), reads=[Bbank[bk]], writes=[B("KU", i)])
                        bk = nb(0, 7)

                        def mmmr(e, bk=bk, i=i, h=h, T=T):
                            e.matmul(pbank[bk][0:64, 0:64], KU[i][:, 0:64], T[:, 4, h * 64:(h + 1) * 64], start=True, stop=True)
                            return e.matmul(pbank[bk][0:64, 64:192], KU[i][:, 0:64], AT[i][:, 128:256], start=True, stop=True)
                        add("pe", mmmr, reads=[B("KU", i), B("tm4", s), B("AT", i)], writes=[Bbank[bk]])
                        add("act", lambda e, bk=bk, i=i: e.mul(Mc[i][0:64, :], pbank[bk][0:64, 0:64], -1.0), reads=[Bbank[bk]], writes=[B("Mc", i)])
                        add("dve", lambda e, bk=bk, i=i, h=h, X=X: e.tensor_tensor(out=RhT[i][0:64, :], in0=X[0:64, h, 1, :], in1=pbank[bk][0:64, 64:192], op=ALU.subtract),
                            reads=[Bbank[bk], BXT], writes=[B("RhT", i)])
                        def mmy(e, i=i, h=h, vb_h=vb_h, STbo=STbo):
                            o = pbank[bkY][:, h * 64:(h + 1) * 64]
                            e.matmul(o, RhT[i][:, :], STbo[:, h, :], start=True, stop=False)
                            e.matmul(o, AT[i][:, 128:256], KU[i][:, 64:128], start=False, stop=False)
                            return e.matmul(o, AT[i][:, 384:512], vb_h, start=False, stop=True)
                        add("pe", mmy, reads=[B("RhT", i), BSTbo, B("AT", i), B("KU", i), B("tm6", s)], writes=[Bbank[bkY]])
                        bk = nb(0, 7)

                        def mms(e, bk=bk, i=i, h=h, vb_h=vb_h, T=T, STbo=STbo):
                            o = pbank[bk][0:64, 0:64]
                            e.matmul(o, T[:, 4, h * 64:(h + 1) * 64], KU[i][:, 64:128], start=True, stop=False)
                            e.matmul(o, T[:, 5, h * 64:(h + 1) * 64], vb_h, start=False, stop=False)
                            return e.matmul(o, Mc[i][:, :], STbo[:, h, :], start=False, stop=True)
                        add("pe", mms, reads=[B("tm4", s), B("tm5", s), B("tm6", s), B("KU", i), B("Mc", i), BSTbo], writes=[Bbank[bk]])
                        add("dve", lambda e, bk=bk, h=h, STn=STn, STo=STo: e.scalar_tensor_tensor(out=STn[0:64, h, :], in0=STo[0:64, h, :], scalar=gCT[0:64, h:h + 1],
                                                                                               in1=pbank[bk][0:64, 0:64], op0=ALU.mult, op1=ALU.add),
                            reads=[Bbank[bk], BSTo, B("gCT")], writes=[B("STh", (c + 1) % 2, h)])
                        add("act", lambda e, h=h, STn=STn, STbn=STbn: e.copy(STbn[0:64, h, :], STn[0:64, h, :]), reads=[B("STh", (c + 1) % 2, h)], writes=[BSTbn, BSTn])
                if c == 0 and L == depth - 1:
                    tap("KU3", KU[3], [B("KU", 3)]); tap("Mc3", Mc[3][0:64], [B("Mc", 3)]); tap("RhT3", RhT[3][0:64], [B("RhT", 3)])
                    tap("STn", STn[0:64], [BSTn]); tap("STo", STo[0:64], [BSTo])
                if P2STOP <= 3:
                    continue
                if c == 0 and debug and False:
                    add("act", lambda e: e.copy(ysq, pbank[bkY][:, :]), reads=[Bbank[bkY]], writes=[B("ysq")])
                    tap("yraw", ysq, [B("ysq")])
                    add("pe", lambda e, X=X: e.matmul(pbank[0][:, 0:128], X[:, 0, 0, :], X[:, 0, 2, :], start=True, stop=True), reads=[BXT], writes=[Bbank[0]])
                    add("act", lambda e: e.copy(tmpa[:, 0:128], pbank[0][:, 0:128]), reads=[Bbank[0]], writes=[B("tmpa")])
                    tap("Pl_late", tmpa[:, 0:128], [B("tmpa")])
                    add("pe", lambda e, X=X: e.matmul(pbank[1][:, 0:128], X[:, 0, 0, :], ident_b, start=True, stop=True), reads=[BXT], writes=[Bbank[1]])
                    add("act", lambda e: e.copy(tmpb[:, 0:128], pbank[1][:, 0:128]), reads=[Bbank[1]], writes=[B("tmpb")])
                    tap("seeL", tmpb[:, 0:128], [B("tmpb")])
                    add("pe", lambda e, X=X: e.matmul(pbank[2][:, 0:128], ident_b, X[:, 0, 2, :], start=True, stop=True), reads=[BXT], writes=[Bbank[2]])
                    add("act", lambda e: e.copy(g_in[:, 0:128], pbank[2][:, 0:128]), reads=[Bbank[2]], writes=[B("g_in")])
                    tap("seeR", g_in[:, 0:128], [B("g_in")])
                add("act", lambda e: e.copy(ysb, pbank[bkY][:, :]), reads=[Bbank[bkY]], writes=[B("ysb")])
                add("pool", lambda e: e.tensor_tensor(out=ysq, in0=ysb, in1=ysb, op=ALU.mult), reads=[B("ysb")], writes=[B("ysq")])
                add("dve", lambda e: e.tensor_reduce(out=sm[:, 24:32], in_=h3(ysb), axis=AX.X, op=ALU.add), reads=[B("ysb")], writes=[Bsm])
                add("dve", lambda e: e.tensor_reduce(out=sm[:, 32:40], in_=h3(ysq), axis=AX.X, op=ALU.add), reads=[B("ysq")], writes=[Bsm])
                add("dve", lambda e: e.tensor_scalar(sm[:, 24:32], sm[:, 24:32], 1.0 / HD, None, ALU.mult), reads=[Bsm], writes=[Bsm])
                add("dve", lambda e: e.tensor_tensor(out=sm[:, 40:48], in0=sm[:, 24:32], in1=sm[:, 24:32], op=ALU.mult), reads=[Bsm], writes=[Bsm])
                add("dve", lambda e: e.scalar_tensor_tensor(out=sm[:, 32:40], in0=sm[:, 32:40], scalar=1.0 / HD, in1=sm[:, 40:48], op0=ALU.mult, op1=ALU.subtract),
                    reads=[Bsm], writes=[Bsm])
                add("act", lambda e: e.activation(out=sm[:, 40:48], in_=sm[:, 32:40], func=AF.Sqrt, bias=GN_EPS, scale=1.0), reads=[Bsm], writes=[Bsm])
                add("dve", lambda e: e.reciprocal(sm[:, 48:56], sm[:, 40:48]), reads=[Bsm], writes=[Bsm])
                add("dve", lambda e: e.tensor_tensor(out=h3(ysb), in0=h3(ysb), in1=sm[:, 24:32].unsqueeze(2).broadcast_to([128, NH, HD]), op=ALU.subtract),
                    reads=[B("ysb"), Bsm], writes=[B("ysb")])
                add("dve", lambda e: e.tensor_tensor(out=h3(ysb), in0=h3(ysb), in1=sm[:, 48:56].unsqueeze(2).broadcast_to([128, NH, HD]), op=ALU.mult),
                    reads=[B("ysb"), Bsm], writes=[B("ysb")])
                add("pool", lambda e: e.tensor_tensor(out=ysb, in0=ysb, in1=gnw_bc, op=ALU.mult), reads=[B("ysb"), Brow], writes=[B("ysb")])
                add("pool", lambda e: e.tensor_tensor(out=ysb, in0=ysb, in1=gnb_bc, op=ALU.add), reads=[B("ysb"), Brow], writes=[B("ysb")])
                add("dve", lambda e, s=s: e.tensor_tensor(out=h3(ysq), in0=h3(v2[s]), in1=rkc[s].unsqueeze(2).broadcast_to([128, NH, HD]), op=ALU.mult),
                    reads=[Bv2, B("rkc", s)], writes=[B("ysq")])
                add("pool", lambda e: e.tensor_tensor(out=ysb, in0=ysb, in1=ysq, op=ALU.add), reads=[B("ysb"), B("ysq")], writes=[B("ysb")])
                add("dve", lambda e, s=s: e.tensor_tensor(out=rwo[s], in0=ysb, in1=gate[s], op=ALU.mult), reads=[B("ysb"), B("gate", s)], writes=[B("rwo", s)])
                add("sp", lambda e, s=s, c=c: e.dma_start(out=rwo_d[c * 128:(c + 1) * 128, :], in_=rwo[s]), reads=[B("rwo", s)], dma=True)
            S_.barrier()

        def phase3(L):
            A.off = P0
            NSL = 6
            PD = 3
            qk = A.alloc((8, S), BF16)
            vaug = A.alloc((NT, NH, HD + 1), BF16)
            mbig = A.alloc((MBW,), BF16)
            esb = [A.alloc((512,), BF16) for _ in range(NSL)]
            psb = [A.alloc((512,), BF16) for _ in range(NSL)]
            osb = [A.alloc((4, 512), F32) for _ in range(2)]
            rec = [A.alloc((4,), F32) for _ in range(2)]
            for m in range(8):
                add("sp", lambda e, m=m: e.dma_start(out=qk[:, m, :], in_=qkT_d[m * 128:(m + 1) * 128, :]), writes=[B("qk", m)], dma=True)
            for t in range(NT):
                add("sp", lambda e, t=t: e.dma_start(out=vaug[:, t, :, 0:HD], in_=vbf_d[t * 128:(t + 1) * 128, :].rearrange("p (h c) -> p h c", c=HD)),
                    writes=[B("vaug", t)], dma=True)
            add("pool", lambda e: e.dma_start(out=mbig, in_=mbig_d), writes=[B("mbig")], dma=True)
            add("pool", lambda e: e.memset(vaug[:, :, :, HD:HD + 1], 1.0), writes=[B("vones")])
            Bqk = [B("qk", m) for m in range(8)]
            units = []
            for sb in range(S // 512):
                q0 = sb * 512
                kt_lo = max(0, (q0 - 2048) // 128)
                kt_hi = (q0 + 511) // 128
                for h in range(NH):
                    for kt in range(kt_lo, kt_hi + 1):
                        units.append((sb, h, kt, kt == kt_lo, kt == kt_hi))

            def front(u, idx):
                sb, h, kt, first, last = u
                q0 = sb * 512
                ph = (h % 2) * 64
                Dd = q0 - kt * 128
                bs = nb(0, 6)
                es = idx % NSL
                add("pe", lambda e: e.matmul(pbank[bs][:, :], qk[ph:ph + 64, 4 + h // 2, kt * 128:(kt + 1) * 128], qk[ph:ph + 64, h // 2, q0:q0 + 512],
                                             start=True, stop=True),
                    reads=[Bqk[h // 2], Bqk[4 + h // 2]], writes=[Bbank[bs]])
                add("act", lambda e: e.activation(out=esb[es], in_=pbank[bs][:, :], func=AF.Exp, scale=1.0 / 8.0),
                    reads=[Bbank[bs]], writes=[B("esb", es)])
                add("dve", lambda e: e.tensor_tensor(out=psb[es], in0=esb[es], in1=mbig[:, Dd + 384:Dd + 384 + 512], op=ALU.mult),
                    reads=[B("esb", es), B("mbig")], writes=[B("psb", es)])

            def back(u, idx):
                sb, h, kt, first, last = u
                q0 = sb * 512
                Dd = q0 - kt * 128
                es = idx % NSL
                bacc = 6 + (h % 2)
                os_ = sb % 2
                qss = [qs for qs in range(4) if (Dd + qs * 128 + 127 >= 0) and (Dd + qs * 128 - 127 <= 2048)]

                def mmpv(e):
                    ins = None
                    for qs in qss:
                        ins = e.matmul(pbank[bacc][:, qs * 65:(qs + 1) * 65], psb[es][:, qs * 128:(qs + 1) * 128], vaug[:, kt, h, :],
                                       start=(first and qs == qss[0]), stop=last, skip_group_check=True)
                    return ins
                add("pe", mmpv, reads=[B("psb", es), B("vaug", kt), B("vones")], writes=[Bbank[bacc]])
                if last:
                    accv = pbank[bacc][:, 0:260].rearrange("p (q c) -> p q c", c=65)
                    add("dve", lambda e: e.reciprocal(rec[os_].unsqueeze(2), accv[:, :, 64:65]), reads=[Bbank[bacc]], writes=[B("rec", os_)])
                    add("dve", lambda e: e.tensor_tensor(out=osb[os_][:, :, h * 64:(h + 1) * 64], in0=accv[:, :, 0:64],
                                                         in1=rec[os_].unsqueeze(2).broadcast_to([128, 4, 64]), op=ALU.mult),
                        reads=[Bbank[bacc], B("rec", os_)], writes=[B("osb", os_)])
                    if h == NH - 1:
                        add("sp", lambda e: e.dma_start(out=atto_d[q0:q0 + 512, :].rearrange("(q p) c -> p q c", p=128), in_=osb[os_]),
                            reads=[B("osb", os_)], dma=True)

            n = len(units)
            for idx in range(n + PD):
                if idx < n:
                    front(units[idx], idx)
                if idx >= PD:
                    back(units[idx - PD], idx - PD)
            S_.barrier()

        def phase4(L, xsrc):
            A.off = P0
            wo = A.alloc((8, D), BF16)
            gpo = A.alloc((D,), F32)
            aog = A.alloc((512,), F32)
            at = [A.alloc((512,), F32) for _ in range(2)]
            cat = [A.alloc((D,), BF16) for _ in range(2)]
            catT = [A.alloc((8, 128), BF16) for _ in range(2)]
            xt = [A.alloc((D,), F32) for _ in range(2)]
            xo = [A.alloc((D,), F32) for _ in range(2)]
            junk = A.alloc((D,), BF16)
            st4 = [A.alloc((8,), F32) for _ in range(2)]
            for kc in range(8):
                add("pool", lambda e, kc=kc: e.dma_start(out=wo[:, kc, :], in_=w_out[L, kc * 128:(kc + 1) * 128, :]), writes=[B("wo", kc)], dma=True)
            add("sp", lambda e: e.dma_start(out=gpo, in_=gpost_d[L, 0:1, :].partition_broadcast(128)), writes=[B("gpo")], dma=True)
            add("sp", lambda e: e.dma_start(out=aog, in_=rowp_d[L:L + 1, 8 * 512:9 * 512].partition_broadcast(128)), writes=[B("aog")], dma=True)
            Bwo = [B("wo", kc) for kc in range(8)]
            for t in range(NT):
                s = t % 2
                add("sp", lambda e, s=s, t=t: e.dma_start(out=at[s], in_=atto_d[t * 128:(t + 1) * 128, :]), writes=[B("at", s)], dma=True)
                add("sp", lambda e, s=s, t=t: e.dma_start(out=cat[s][:, 512:1024], in_=rwo_d[t * 128:(t + 1) * 128, :]), writes=[B("catr", s)], dma=True)
                add("sp", lambda e, s=s, t=t: e.dma_start(out=xt[s], in_=xsrc[t * 128:(t + 1) * 128, :]), writes=[B("xt", s)], dma=True)
                Bst = B("st4", s)
                add("act", lambda e, s=s: e.activation(out=junk[:, 0:512], in_=at[s], func=AF.Square, accum_out=st4[s][:, 0:1]), reads=[B("at", s)], writes=[Bst])
                add("act", lambda e, s=s: e.activation(out=st4[s][:, 1:2], in_=st4[s][:, 0:1], func=AF.Sqrt, bias=EPS, scale=1.0 / DA), reads=[Bst], writes=[Bst])
                add("dve", lambda e, s=s: e.reciprocal(st4[s][:, 2:3], st4[s][:, 1:2]), reads=[Bst], writes=[Bst])
                add("dve", lambda e, s=s: e.scalar_tensor_tensor(out=cat[s][:, 0:512], in0=at[s], scalar=st4[s][:, 2:3], in1=aog, op0=ALU.mult, op1=ALU.mult),
                    reads=[B("at", s), Bst, B("aog")], writes=[B("cata", s)])
                tb = nb(0, 2)

                def tr(e, s=s, tb=tb):
                    pv = bank_bf(tb)
                    for kc in range(8):
                        ins = e.transpose(pv[:, kc * 128:(kc + 1) * 128], cat[s][:, kc * 128:(kc + 1) * 128], ident_b)
                    return ins
                add("pe", tr, reads=[B("cata", s), B("catr", s), B("identb")], writes=[Bbank[tb]])
                add("act", lambda e, s=s, tb=tb: e.copy(catT[s].rearrange("p k t -> p (k t)"), bank_bf(tb)), reads=[Bbank[tb]], writes=[B("catT", s)])
                bks = []
                for n in range(2):
                    bk = nb(2, 8)
                    bks.append(bk)

                    def mmo(e, s=s, bk=bk, n=n):
                        for kc in range(8):
                            ins = e.matmul(pbank[bk][:, :], catT[s][:, kc, :], wo[:, kc, n * 512:(n + 1) * 512], start=(kc == 0), stop=(kc == 7))
                        return ins
                    add("pe", mmo, reads=Bwo + [B("catT", s)], writes=[Bbank[bk]])
                    add("act", lambda e, s=s, bk=bk, n=n: e.activation(out=junk[:, 0:512], in_=pbank[bk][:, :], func=AF.Square, accum_out=st4[s][:, 3 + n:4 + n]),
                        reads=[Bbank[bk]], writes=[B("st4b", s, n)])
                Bst2 = B("st4c", s)
                add("dve", lambda e, s=s: e.tensor_tensor(out=st4[s][:, 5:6], in0=st4[s][:, 3:4], in1=st4[s][:, 4:5], op=ALU.add),
                    reads=[B("st4b", s, 0), B("st4b", s, 1)], writes=[Bst2])
                add("act", lambda e, s=s: e.activation(out=st4[s][:, 6:7], in_=st4[s][:, 5:6], func=AF.Sqrt, bias=EPS, scale=1.0 / D), reads=[Bst2], writes=[Bst2])
                add("dve", lambda e, s=s: e.reciprocal(st4[s][:, 7:8], st4[s][:, 6:7]), reads=[Bst2], writes=[Bst2])
                for n in range(2):
                    bk = bks[n]
                    add("dve", lambda e, s=s, bk=bk, n=n: e.scalar_tensor_tensor(out=xo[s][:, n * 512:(n + 1) * 512], in0=pbank[bk][:, :], scalar=st4[s][:, 7:8],
                                                                              in1=gpo[:, n * 512:(n + 1) * 512], op0=ALU.mult, op1=ALU.mult),
                        reads=[Bbank[bk], Bst2, B("gpo")], writes=[B("xo", s, n)])
                add("pool", lambda e, s=s: e.tensor_tensor(out=xo[s], in0=xo[s], in1=xt[s], op=ALU.add),
                    reads=[B("xo", s, 0), B("xo", s, 1), B("xt", s)], writes=[B("xo", s, 0), B("xo", s, 1)])
                add("sp", lambda e, s=s, t=t: e.dma_start(out=xmid_d[t * 128:(t + 1) * 128, :], in_=xo[s]), reads=[B("xo", s, 0), B("xo", s, 1)], dma=True)
            S_.barrier()

        def phase5(L, xdst, outbufs):
            A.off = P0
            TB = 256
            NB5 = S // TB
            wu = A.alloc((8, DFF), BF16)
            wd = A.alloc((32, D), BF16)
            gpre = A.alloc((8,), F32)
            gpo = A.alloc((D,), F32)
            xt = [A.alloc((2, D), F32) for _ in range(2)]
            xn = [A.alloc((D,), BF16) for _ in range(2)]
            hT = [A.alloc((8, TB), BF16) for _ in range(2)]
            aT = A.alloc((32, TB), BF16)
            rl = [A.alloc((TB,), BF16) for _ in range(3)]
            xo = [A.alloc((D,), F32) for _ in range(2)]
            junk = A.alloc((D,), BF16)
            st5 = [A.alloc((8,), F32) for _ in range(2)]
            for kc in range(8):
                add("pool", lambda e, kc=kc: e.dma_start(out=wu[:, kc, :], in_=w_up[L, kc * 128:(kc + 1) * 128, :]), writes=[B("wu", kc)], dma=True)
            for g in range(4):
                add("pool", lambda e, g=g: e.dma_start(out=wd[:, g * 8:(g + 1) * 8, :], in_=w_dn[L, g * 1024:(g + 1) * 1024, :].rearrange("(m p) d -> p m d", p=128)),
                    writes=[B("wd", g)], dma=True)
            add("sp", lambda e: e.dma_start(out=gpre, in_=gpre_d[L, 1]), writes=[B("gpre")], dma=True)
            add("sp", lambda e: e.dma_start(out=gpo, in_=gpost_d[L, 1:2, :].partition_broadcast(128)), writes=[B("gpo")], dma=True)
            Bwu = [B("wu", kc) for kc in range(8)]
            Bwd = [B("wd", g) for g in range(4)]
            cnt = 0
            for b in range(NB5):
                bs = b % 2
                BhT = B("hT", bs)
                for ts in range(2):
                    t = 2 * b + ts
                    s = t % 2
                    Bx = B("xt", bs, ts)
                    add("sp", lambda e, bs=bs, ts=ts, t=t: e.dma_start(out=xt[bs][:, ts, :], in_=xmid_d[t * 128:(t + 1) * 128, :]), writes=[Bx], dma=True)
                    Bst = B("st5", s)
                    add("act", lambda e, bs=bs, ts=ts, s=s: e.activation(out=junk, in_=xt[bs][:, ts, :], func=AF.Square, accum_out=st5[s][:, 0:1]), reads=[Bx], writes=[Bst])
                    add("act", lambda e, s=s: e.activation(out=st5[s][:, 1:2], in_=st5[s][:, 0:1], func=AF.Sqrt, bias=EPS, scale=1.0 / D), reads=[Bst], writes=[Bst])
                    add("dve", lambda e, s=s: e.reciprocal(st5[s][:, 2:3], st5[s][:, 1:2]), reads=[Bst], writes=[Bst])
                    add("dve", lambda e, bs=bs, ts=ts, s=s: e.tensor_scalar(xn[s], xt[bs][:, ts, :], st5[s][:, 2:3], None, ALU.mult), reads=[Bx, Bst], writes=[B("xn", s)])
                    tb = nb(0, 2)

                    def tr(e, s=s, tb=tb):
                        pv = bank_bf(tb)
                        for kc in range(8):
                            ins = e.transpose(pv[:, kc * 128:(kc + 1) * 128], xn[s][:, kc * 128:(kc + 1) * 128], ident_b)
                        return ins
                    add("pe", tr, reads=[B("xn", s), B("identb")], writes=[Bbank[tb]])
                    add("dve", lambda e, tb=tb, bs=bs, ts=ts: e.tensor_tensor(
                        out=hT[bs][:, :, ts * 128:(ts + 1) * 128], in0=bank_bf(tb).rearrange("p (k t) -> p k t", t=128),
                        in1=gpre.unsqueeze(2).broadcast_to([128, 8, 128]), op=ALU.mult),
                        reads=[Bbank[tb], B("gpre")], writes=[BhT])
                for m2 in range(16):
                    bk = nb(2, 6)

                    def mmu(e, bk=bk, m2=m2, bs=bs):
                        for j in range(2):
                            m = 2 * m2 + j
                            for kc in range(8):
                                ins = e.matmul(pbank[bk][:, j * TB:(j + 1) * TB], wu[:, kc, m * 128:(m + 1) * 128], hT[bs][:, kc, :], start=(kc == 0), stop=(kc == 7))
                        return ins
                    add("pe", mmu, reads=Bwu + [BhT], writes=[Bbank[bk]])
                    for j in range(2):
                        m = 2 * m2 + j
                        rs = cnt % 3
                        cnt += 1
                        add("act", lambda e, bk=bk, j=j, rs=rs: e.activation(out=rl[rs], in_=pbank[bk][:, j * TB:(j + 1) * TB], func=AF.Relu),
                            reads=[Bbank[bk]], writes=[B("rl", rs)])
                        eng = "pool" if m % 2 else "dve"
                        add(eng, lambda e, m=m, rs=rs: e.tensor_tensor(out=aT[:, m, :], in0=rl[rs], in1=rl[rs], op=ALU.mult),
                            reads=[B("rl", rs)], writes=[B("aT", m)])
                BaT = [B("aT", m) for m in range(32)]
                for ts in range(2):
                    t = 2 * b + ts
                    s = t % 2
                    bks = []
                    for n in range(2):
                        bk = nb(6, 8) if False else (6 + n)
                        bks.append(bk)

                        def mmd(e, bk=bk, ts=ts, n=n):
                            for m in range(32):
                                ins = e.matmul(pbank[bk][:, :], aT[:, m, ts * 128:(ts + 1) * 128], wd[:, m, n * 512:(n + 1) * 512], start=(m == 0), stop=(m == 31))
                            return ins
                        add("pe", mmd, reads=Bwd + BaT, writes=[Bbank[bk]])
                        add("act", lambda e, s=s, bk=bk, n=n: e.activation(out=junk[:, 0:512], in_=pbank[bk][:, :], func=AF.Square, accum_out=st5[s][:, 3 + n:4 + n]),
                            reads=[Bbank[bk]], writes=[B("st5b", s, n)])
                    Bst2 = B("st5c", s)
                    add("dve", lambda e, s=s: e.tensor_tensor(out=st5[s][:, 5:6], in0=st5[s][:, 3:4], in1=st5[s][:, 4:5], op=ALU.add),
                        reads=[B("st5b", s, 0), B("st5b", s, 1)], writes=[Bst2])
                    add("act", lambda e, s=s: e.activation(out=st5[s][:, 6:7], in_=st5[s][:, 5:6], func=AF.Sqrt, bias=EPS, scale=1.0 / D), reads=[Bst2], writes=[Bst2])
                    add("dve", lambda e, s=s: e.reciprocal(st5[s][:, 7:8], st5[s][:, 6:7]), reads=[Bst2], writes=[Bst2])
                    for n in range(2):
                        bk = bks[n]
                        add("dve", lambda e, s=s, bk=bk, n=n: e.scalar_tensor_tensor(out=xo[s][:, n * 512:(n + 1) * 512], in0=pbank[bk][:, :], scalar=st5[s][:, 7:8],
                                                                                  in1=gpo[:, n * 512:(n + 1) * 512], op0=ALU.mult, op1=ALU.mult),
                            reads=[Bbank[bk], Bst2, B("gpo")], writes=[B("xo", s, n)])
                    add("pool", lambda e, s=s, bs=bs, ts=ts: e.tensor_tensor(out=xo[s], in0=xo[s], in1=xt[bs][:, ts, :], op=ALU.add),
                        reads=[B("xo", s, 0), B("xo", s, 1), B("xt", bs, ts)], writes=[B("xo", s, 0), B("xo", s, 1)])
                    ob = B("outd", L, t)
                    outbufs.append(ob)
                    add("sp", lambda e, s=s, t=t: e.dma_start(out=xdst[t * 128:(t + 1) * 128, :], in_=xo[s]), reads=[B("xo", s, 0), B("xo", s, 1)], writes=[ob], dma=True)
            S_.barrier()

        outbufs = []
        for L in range(depth):
            xsrc = x_in if L == 0 else xres_d
            xdst = out_d if L == depth - 1 else xres_d
            if phases is None or 1 in phases:
                phase1(L, xsrc)
            if phases is None or 2 in phases:
                phase2(L)
            if phases is None or 3 in phases:
                phase3(L)
            if phases is None or 4 in phases:
                phase4(L, xsrc)
            if phases is None or 5 in phases:
                phase5(L, xdst, outbufs)
        S_.emit()
    return nc


def _count_mask():
    ki = np.arange(128)[:, None]
    xx = np.arange(MBW)[None, :]
    d = xx - 384 - ki
    c = ((d >= 0) & (d <= 128)).astype(np.float32)
    c += ((d >= 0) & (d <= 512) & (d % 4 == 0)).astype(np.float32)
    c += ((d >= 0) & (d <= 2048) & (d % 16 == 0)).astype(np.float32)
    return np.ascontiguousarray(c, dtype=np.float32)


def _const_masks():
    p = np.arange(128)[:, None]
    f = np.arange(128)[None, :]
    ident = (p == f).astype(np.float32)
    ut = (p <= f).astype(np.float32)
    strict = (p < f).astype(np.float32)
    incl = (p <= f).astype(np.float32)
    low = (f < p).astype(np.float32)
    ones = np.ones((128, 128), np.float32)
    bd16 = ((p // 16) == (f // 16)).astype(np.float32)
    lowbd = low * bd16
    mms = []
    for bs in (16, 32, 64):
        mu_ = (((p // bs) % 2 == 0) & ((f // bs) == (p // bs) + 1)).astype(np.float32)
        mms += [mu_, np.ascontiguousarray(mu_.T)]
    return np.ascontiguousarray(np.concatenate([ident, ut, strict, incl, strict, incl, low, ones, -bd16, -lowbd] + mms, axis=1))


def pack_inputs(depth, norm_mix_pre, norm_mix_post, norm_ffn_pre, norm_ffn_post, w_in_first, w_in_rest,
                mu_shift, mu_shift_mv, attn_out_gain, decay_w0, decay_up, aaa_a0, aaa_up, mv_v0, mv_up,
                gate_up, k_k, k_a, r_k, gn_w, gn_b, w_out, w_ffn_up, w_ffn_down):
    f = np.float32
    w_in = np.zeros((depth, D, NCOLS), f)
    w_in[0, :, :w_in_first.shape[1]] = w_in_first
    for i in range(1, depth):
        w_in[i] = w_in_rest[i - 1]
    gpre = np.zeros((depth, 2, 128, 8), f)
    gpost = np.zeros((depth, 2, D), f)
    mu = np.zeros((depth, NZ), f)
    rowp = np.zeros((depth, 9, 512), f)
    lora = np.zeros((depth, 4, 96, 512), f)
    for i in range(depth):
        gpre[i, 0] = np.asarray(norm_mix_pre[i]).reshape(8, 128).T
        gpre[i, 1] = np.asarray(norm_ffn_pre[i]).reshape(8, 128).T
        gpost[i, 0] = norm_mix_post[i]
        gpost[i, 1] = norm_ffn_post[i]
        mu[i, :1696] = mu_shift[i]
        rowp[i, 0] = decay_w0[i]
        rowp[i, 1] = aaa_a0[i]
        rowp[i, 3] = k_k[i]
        rowp[i, 4] = k_a[i]
        rowp[i, 5] = np.asarray(r_k[i]).reshape(512)
        rowp[i, 6] = gn_w[i]
        rowp[i, 7] = gn_b[i]
        rowp[i, 8] = attn_out_gain[i]
        lora[i, 0, :32] = decay_up[i]
        lora[i, 1, :32] = aaa_up[i]
        lora[i, 3, :96] = gate_up[i]
        if i > 0:
            mu[i, 1696:] = mu_shift_mv[i - 1]
            rowp[i, 2] = mv_v0[i - 1]
            lora[i, 2, :32] = mv_up[i - 1]
    return {
        "w_in": w_in, "w_out": np.ascontiguousarray(w_out[:depth], f), "w_up": np.ascontiguousarray(w_ffn_up[:depth], f),
        "w_dn": np.ascontiguousarray(w_ffn_down[:depth], f), "gpre": gpre, "gpost": gpost, "mu": mu,
        "rowp": rowp.reshape(depth, 9 * 512), "lora": lora, "cmask": _const_masks(), "mbig": _count_mask(),
    }


_CACHE = {}


def kernel(x, **params):
    x = np.asarray(x, np.float32)
    Bn, S, _ = x.shape
    depth = 4
    params = {k: np.asarray(v, np.float32) for k, v in params.items()}
    shared = pack_inputs(depth, **params)
    key = (S, depth)
    if key not in _CACHE:
        _CACHE[key] = build_program(S, depth)
    nc = _CACHE[key]
    in_maps = []
    for b in range(Bn):
        m = dict(shared)
        m["x"] = np.ascontiguousarray(x[b])
        in_maps.append(m)
    res = run_bass_kernel_spmd(nc, in_maps, core_ids=list(range(Bn)))
    return np.stack([np.asarray(r["out"], np.float32) for r in res.results], axis=0)
```

```python
import contextlib
import numpy as np
import ml_dtypes
import concourse.bass as bass
import concourse.mybir as mybir
from concourse.bass_utils import run_bass_kernel_spmd

F32 = mybir.dt.float32
BF16 = mybir.dt.bfloat16
ALU = mybir.AluOpType
AF = mybir.ActivationFunctionType
AX = mybir.AxisListType

D = 1024
DA = 512
DR = 512
NH = 8
HD = 64
NCOLS = 3264
NZ = 1728
DFF = 4096
EPS = 1e-6
GN_EPS = 64e-5
C0 = float(np.exp(-0.5))
MBW = 2944
N_CORES = 8

ENGS = ("pe", "act", "dve", "pool", "sp")
N_DMA_SEMS = 40
import os
P2STOP = int(os.environ.get('P2STOP', '9'))


class Buf:
    __slots__ = ("name", "last_w", "readers", "excl")

    def __init__(self, name):
        self.name = name
        self.last_w = None
        self.readers = {}
        self.excl = False


class Op:
    __slots__ = ("eng", "fn", "deps", "sem", "val", "is_dma")

    def __init__(self, eng, fn, is_dma):
        self.eng = eng
        self.fn = fn
        self.deps = set()
        self.sem = None
        self.val = None
        self.is_dma = is_dma


class Sched:
    def __init__(self, nc):
        self.nc = nc
        self.ops = []
        self.bufs = {}
        self.dma_rr = 0
        self.dma_last = [None] * N_DMA_SEMS
        self.dma_cnt = [0] * N_DMA_SEMS
        self.last_op = {e: None for e in ENGS}
        self.dma_since = []
        self.sync_same_engine = True

    def B(self, *key):
        b = self.bufs.get(key)
        if b is None:
            b = Buf(key)
            self.bufs[key] = b
        return b

    def add(self, eng, fn, reads=(), writes=(), dma=False):
        op = Op(eng, fn, dma)
        deps = op.deps
        for b in reads:
            if b.last_w is not None:
                deps.add(b.last_w)
            if b.excl:
                for k, r in b.readers.items():
                    if k != eng:
                        deps.add(r)
        for b in writes:
            if b.last_w is not None:
                deps.add(b.last_w)
            for r in b.readers.values():
                deps.add(r)
        for b in reads:
            b.readers[id(op) if dma else eng] = op
        for b in writes:
            b.last_w = op
            b.readers = {}
        deps.discard(op)
        if dma:
            k = self.dma_rr
            self.dma_rr = (k + 1) % N_DMA_SEMS
            prev = self.dma_last[k]
            if prev is not None:
                deps.add(prev)
            self.dma_cnt[k] += 16
            op.sem = k
            op.val = self.dma_cnt[k]
            self.dma_last[k] = op
            self.dma_since.append(op)
        else:
            self.last_op[eng] = op
        self.ops.append(op)
        return op

    def barrier(self):
        lasts = [o for o in self.last_op.values() if o is not None] + list(self.dma_since)
        for e in ENGS:
            op = Op(e, None, False)
            op.deps = set(lasts)
            self.ops.append(op)
        self.dma_since = []
        for b in self.bufs.values():
            b.last_w = None
            b.readers = {}

    def _skip(self, d, op):
        if d.is_dma or op.is_dma or d.eng != op.eng:
            return False
        return d.eng == "pe" or not self.sync_same_engine

    def emit(self):
        nc = self.nc
        with contextlib.ExitStack() as st:
            esem = {e: st.enter_context(nc.semaphore("s_" + e)) for e in ENGS}
            dsem = [st.enter_context(nc.semaphore("d%d" % i)) for i in range(N_DMA_SEMS)]
            needed = set()
            for op in self.ops:
                for d in op.deps:
                    if d.is_dma or self._skip(d, op):
                        continue
                    needed.add(d)
            cnt = {e: 0 for e in ENGS}
            for op in self.ops:
                if op.is_dma:
                    op.sem = dsem[op.sem]
                else:
                    op.sem = esem[op.eng]
                    if op in needed:
                        cnt[op.eng] += 1
                        op.val = cnt[op.eng]
            per = {e: [op for op in self.ops if op.eng == e] for e in ENGS}
            block = st.enter_context(nc.Block())

            def run(engname, eng):
                waited = {}
                for op in per[engname]:
                    for d in op.deps:
                        if self._skip(d, op):
                            continue
                        key = id(d.sem)
                        if waited.get(key, 0) >= d.val:
                            continue
                        eng.wait_ge(d.sem, d.val)
                        waited[key] = d.val
                    if op.fn is None:
                        continue
                    ins = op.fn(eng)
                    if op.is_dma:
                        ins.then_inc(op.sem, 16)
                    elif op in needed:
                        ins.then_inc(op.sem, 1)

            @block.tensor
            def _(e):
                run("pe", e)

            @block.scalar
            def _(e):
                run("act", e)

            @block.vector
            def _(e):
                run("dve", e)

            @block.gpsimd
            def _(e):
                run("pool", e)

            @block.sync
            def _(e):
                run("sp", e)


class Arena:
    def __init__(self, tens, nwords):
        self.t = tens
        self.n = nwords
        self.off = 0

    def alloc(self, free_shape, dtype):
        n = int(np.prod(free_shape))
        words = n if dtype == F32 else (n + 1) // 2
        words = (words + 7) // 8 * 8
        assert self.off + words <= self.n, ("arena overflow", self.off, words, self.n)
        ap = self.t[:, self.off:self.off + words]
        self.off += words
        if dtype != F32:
            ap = ap.bitcast(dtype)
        ap = ap[:, 0:n]
        if len(free_shape) == 2:
            ap = ap.rearrange("p (a b) -> p a b", b=free_shape[1])
        elif len(free_shape) == 3:
            ap = ap.rearrange("p (a b c) -> p a b c", b=free_shape[1], c=free_shape[2])
        return ap


def build_program(S, depth, debug=False, phases=None):
    NT = S // 128
    nc = bass.Bass("TRN2", target_bir_lowering=False)
    okind = "ExternalOutput" if debug else "Internal"

    def din(name, shape, dt=F32):
        return nc.dram_tensor(name, list(shape), dt, kind="ExternalInput").ap()

    def dscr(name, shape, dt=F32):
        return nc.dram_tensor(name, list(shape), dt, kind=okind).ap()

    x_in = din("x", [S, D])
    w_in = din("w_in", [depth, D, NCOLS])
    w_out = din("w_out", [depth, D, D])
    w_up = din("w_up", [depth, D, DFF])
    w_dn = din("w_dn", [depth, DFF, D])
    gpre_d = din("gpre", [depth, 2, 128, 8])
    gpost_d = din("gpost", [depth, 2, D])
    mu_d = din("mu", [depth, NZ])
    rowp_d = din("rowp", [depth, 9 * 512])
    lora_d = din("lora", [depth, 4, 96, 512])
    cmask_d = din("cmask", [128, 2048])
    mbig_d = din("mbig", [128, MBW])
    out_d = nc.dram_tensor("out", [S, D], F32, kind="ExternalOutput").ap()

    qkT_d = dscr("qkT", [D, S], BF16)
    vbf_d = dscr("vbf", [S, DA], BF16)
    zr_d = dscr("zr", [S, NZ])
    vfirst_d = dscr("vfirst", [S, DR])
    atto_d = dscr("atto", [S, DA])
    rwo_d = dscr("rwo", [S, DR], BF16)
    xmid_d = dscr("xmid", [S, D])
    xres_d = dscr("xres", [S, D])

    S_ = Sched(nc)
    add = S_.add
    B = S_.B
    taps = {}

    def tap(name, ap, bufs):
        if not debug or name in taps:
            return
        tdt = nc.dram_tensor("tap_" + name, list(ap.shape), ap.dtype, kind="ExternalOutput").ap()
        taps[name] = tdt
        add("sp", lambda e: e.dma_start(out=tdt, in_=ap), reads=bufs, dma=True)

    with contextlib.ExitStack() as st:
        ARENA_WORDS = 50 * 1024
        arena_t = st.enter_context(nc.sbuf_tensor("arena", [128, ARENA_WORDS], F32))
        A = Arena(arena_t, ARENA_WORDS)
        pbank = [st.enter_context(nc.psum_tensor("pb%d" % i, [128, 512], F32)) for i in range(8)]
        Bbank = [B("bank", i) for i in range(8)]
        for b_ in Bbank:
            b_.excl = True

        def bank_bf(i):
            return pbank[i][:, :].bitcast(BF16)

        cmask = A.alloc((2048,), F32)
        ident_f = cmask[:, 0:128]
        ut_f = cmask[:, 128:256]
        mask4 = cmask[:, 256:768]
        mlow = cmask[:, 768:896]
        ones_f = cmask[:, 896:1024]
        bd16 = cmask[:, 1024:1152]
        lowbd = cmask[:, 1152:1280]
        ident_b = A.alloc((128,), BF16)
        Bc = B("consts")
        add("sp", lambda e: e.dma_start(out=cmask, in_=cmask_d), writes=[Bc], dma=True)
        add("dve", lambda e: e.tensor_copy(ident_b, ident_f), reads=[Bc], writes=[B("identb")])
        ut_b = A.alloc((128,), BF16)
        ones_b = A.alloc((128,), BF16)
        add("dve", lambda e: e.tensor_copy(ut_b, ut_f), reads=[Bc], writes=[B("utb")])
        add("dve", lambda e: e.tensor_copy(ones_b, ones_f), reads=[Bc], writes=[B("onesb")])
        P0 = A.off
        S_.barrier()

        rot = {}

        def nb(lo=0, hi=8):
            i = rot.get((lo, hi), 0)
            rot[(lo, hi)] = i + 1
            return lo + i % (hi - lo)

        def rms_rstd(eng_sq_in, width, stat, Bin, Bstat, tag):
            pass

        def phase1(L, xsrc):
            A.off = P0
            NB = S // 512
            win = A.alloc((8, NCOLS), BF16)
            gpre = A.alloc((8,), F32)
            xt = [A.alloc((D,), F32) for _ in range(2)]
            junk = A.alloc((D,), BF16)
            stt_ = [A.alloc((4,), F32) for _ in range(2)]
            xn = [A.alloc((D,), BF16) for _ in range(2)]
            hT = [A.alloc((8, 512), BF16) for _ in range(2)]
            qks = [A.alloc((512,), BF16) for _ in range(3)]
            zt = [A.alloc((NZ,), F32) for _ in range(2)]
            vt = [A.alloc((512,), BF16) for _ in range(2)]
            for kc in range(8):
                add("pool", lambda e, kc=kc: e.dma_start(out=win[:, kc, :], in_=w_in[L, kc * 128:(kc + 1) * 128, :]),
                    writes=[B("win", kc)], dma=True)
            add("sp", lambda e: e.dma_start(out=gpre, in_=gpre_d[L, 0]), writes=[B("gpre")], dma=True)
            Bwin = [B("win", kc) for kc in range(8)]
            ev = {"i": 0}

            def evac(out, in_, reads, writes):
                ev["i"] += 1
                if ev["i"] % 2:
                    add("act", lambda e: e.copy(out, in_), reads=reads, writes=writes)
                else:
                    add("dve", lambda e: e.tensor_copy(out, in_), reads=reads, writes=writes)

            for b in range(NB):
                hs = b % 2
                BhT = B("hT", hs)
                for ts in range(4):
                    t = 4 * b + ts
                    s = t % 2
                    add("sp", lambda e, s=s, t=t: e.dma_start(out=xt[s], in_=xsrc[t * 128:(t + 1) * 128, :]),
                        writes=[B("xt", s)], dma=True)
                    add("act", lambda e, s=s: e.activation(out=junk, in_=xt[s], func=AF.Square, accum_out=stt_[s][:, 0:1]),
                        reads=[B("xt", s)], writes=[B("st0", s)])
                    add("act", lambda e, s=s: e.activation(out=stt_[s][:, 1:2], in_=stt_[s][:, 0:1], func=AF.Sqrt, bias=EPS, scale=1.0 / D),
                        reads=[B("st0", s)], writes=[B("st1", s)])
                    add("dve", lambda e, s=s: e.reciprocal(stt_[s][:, 2:3], stt_[s][:, 1:2]),
                        reads=[B("st1", s)], writes=[B("st2", s)])
                    add("dve", lambda e, s=s: e.tensor_scalar(xn[s], xt[s], stt_[s][:, 2:3], None, ALU.mult),
                        reads=[B("xt", s), B("st2", s)], writes=[B("xn", s)])
                    tb = nb(0, 2)

                    def tr(e, s=s, tb=tb):
                        pv = bank_bf(tb)
                        for kc in range(8):
                            ins = e.transpose(pv[:, kc * 128:(kc + 1) * 128], xn[s][:, kc * 128:(kc + 1) * 128], ident_b)
                        return ins
                    add("pe", tr, reads=[B("xn", s), B("identb")], writes=[Bbank[tb]])
                    add("dve", lambda e, tb=tb, hs=hs, ts=ts: e.tensor_tensor(
                        out=hT[hs][:, :, ts * 128:(ts + 1) * 128],
                        in0=bank_bf(tb).rearrange("p (k t) -> p k t", t=128),
                        in1=gpre.unsqueeze(2).broadcast_to([128, 8, 128]), op=ALU.mult),
                        reads=[Bbank[tb], B("gpre")], writes=[BhT])
                for m in range(8):
                    bk = nb(2, 8)

                    def mmf(e, m=m, bk=bk, hs=hs):
                        for kc in range(8):
                            ins = e.matmul(pbank[bk][:, :], win[:, kc, m * 128:(m + 1) * 128], hT[hs][:, kc, :],
                                           start=(kc == 0), stop=(kc == 7))
                        return ins
                    add("pe", mmf, reads=Bwin + [BhT], writes=[Bbank[bk]])
                    qs = (b * 8 + m) % 3
                    evac(qks[qs], pbank[bk][:, :], [Bbank[bk]], [B("qks", qs)])
                    add("sp", lambda e, m=m, b=b, qs=qs: e.dma_start(
                        out=qkT_d[m * 128:(m + 1) * 128, b * 512:(b + 1) * 512], in_=qks[qs]),
                        reads=[B("qks", qs)], dma=True)
                for ts in range(4):
                    t = 4 * b + ts
                    zs = t % 2
                    for (c0, c1) in ((1024, 1536), (1536, 2048), (2048, 2560), (2560, 3072), (3072, 3264)):
                        bk = nb(2, 8)
                        w = c1 - c0

                        def mmt(e, bk=bk, hs=hs, ts=ts, c0=c0, c1=c1, w=w):
                            for kc in range(8):
                                ins = e.matmul(pbank[bk][:, 0:w], hT[hs][:, kc, ts * 128:(ts + 1) * 128], win[:, kc, c0:c1],
                                               start=(kc == 0), stop=(kc == 7))
                            return ins
                        add("pe", mmt, reads=Bwin + [BhT], writes=[Bbank[bk]])
                        if c0 == 1024:
                            evac(vt[zs], pbank[bk][:, 0:512], [Bbank[bk]], [B("vt", zs)])
                        else:
                            evac(zt[zs][:, c0 - 1536:c1 - 1536], pbank[bk][:, 0:w], [Bbank[bk]], [B("zt", zs)])
                    add("sp", lambda e, t=t, zs=zs: e.dma_start(out=vbf_d[t * 128:(t + 1) * 128, :], in_=vt[zs]),
                        reads=[B("vt", zs)], dma=True)
                    add("sp", lambda e, t=t, zs=zs: e.dma_start(out=zr_d[t * 128:(t + 1) * 128, :], in_=zt[zs]),
                        reads=[B("zt", zs)], dma=True)
            S_.barrier()

        def phase2(L):
            A.off = P0
            mu = A.alloc((NZ,), F32)
            rowp = A.alloc((9, 512), F32)
            lora = A.alloc((4, 512), BF16)
            ST = [A.alloc((NH, HD), F32) for _ in range(2)]
            STb = [A.alloc((NH, HD), BF16) for _ in range(2)]
            cur = [A.alloc((NZ,), F32) for _ in range(2)]
            prv = [A.alloc((NZ,), F32) for _ in range(2)]
            vf = [A.alloc((512,), F32) for _ in range(2)]
            lo = A.alloc((192,), BF16)
            loT = A.alloc((4, 128), BF16)
            tmpa = A.alloc((512,), F32)
            tmpb = A.alloc((512,), F32)
            sg = A.alloc((512,), F32)
            sgh = A.alloc((512,), BF16)
            sgl = A.alloc((512,), BF16)
            cs = A.alloc((512,), F32)
            g_in = A.alloc((512,), F32)
            g_inv = A.alloc((512,), F32)
            g_prev = A.alloc((512,), F32)
            g_end = A.alloc((512,), F32)
            gCTs = [A.alloc((NH,), F32) for _ in range(2)]
            a_sb = A.alloc((512,), F32)
            gate = [A.alloc((512,), F32) for _ in range(2)]
            v2 = [A.alloc((512,), F32) for _ in range(2)]
            kk = A.alloc((512,), F32)
            kap = A.alloc((512,), F32)
            kmod = A.alloc((512,), F32)
            b_sb = A.alloc((512,), F32)
            sm = A.alloc((64,), F32)
            smt = A.alloc((64,), F32)
            rkc = [A.alloc((NH,), F32) for _ in range(2)]
            tm = [A.alloc((7, 512), BF16) for _ in range(2)]
            XT = [A.alloc((NH, 4, 128), BF16) for _ in range(2)]
            NG = 8
            AT = [A.alloc((512,), BF16) for _ in range(NG)]
            Q = [[A.alloc((4, 128), BF16) for _ in range(2)] for _ in range(NG)]
            ZT = [A.alloc((2, 128), BF16) for _ in range(NG)]
            GG = [A.alloc((256,), BF16) for _ in range(NG)]
            Lm = [A.alloc((128,), BF16) for _ in range(NG)]
            tmpm = [A.alloc((256,), F32) for _ in range(NG)]
            nW1 = [A.alloc((HD,), BF16) for _ in range(NG)]
            KU = [A.alloc((128,), BF16) for _ in range(NG)]
            Mc = [A.alloc((HD,), BF16) for _ in range(NG)]
            RhT = [A.alloc((128,), BF16) for _ in range(NG)]
            ysb = A.alloc((512,), F32)
            ysq = A.alloc((512,), F32)
            rwo = [A.alloc((512,), BF16) for _ in range(2)]

            add("sp", lambda e: e.dma_start(out=mu, in_=mu_d[L:L + 1, :].partition_broadcast(128)), writes=[B("mu")], dma=True)
            add("sp", lambda e: e.dma_start(out=rowp.rearrange("p a b -> p (a b)"), in_=rowp_d[L:L + 1, :].partition_broadcast(128)),
                writes=[B("rowp")], dma=True)
            for q in range(4):
                add("pool", lambda e, q=q: e.dma_start(out=lora[0:96, q, :], in_=lora_d[L, q]), writes=[B("lora", q)], dma=True)
            add("pool", lambda e: e.memset(ST[0].rearrange("p a b -> p (a b)"), 0.0), writes=[B("ST", 0)])
            add("pool", lambda e: e.memset(STb[0].rearrange("p a b -> p (a b)"), 0.0), writes=[B("STb", 0)])
            add("pool", lambda e: e.memset(STb[1].rearrange("p a b -> p (a b)"), 0.0), writes=[B("STb", 1)])
            for s_ in range(2):
                add("pool", lambda e, s_=s_: e.memset(XT[s_].rearrange("p a b c -> p (a b c)"), 0.0), writes=[B("XT", s_)])
            for i_ in range(NG):
                add("pool", lambda e, i_=i_: e.memset(Mc[i_], 0.0), writes=[B("Mc", i_)])
                add("pool", lambda e, i_=i_: e.memset(RhT[i_], 0.0), writes=[B("RhT", i_)])
            w0_bc, a0_bc, mv0_bc, kk_bc, ka_bc, rk_bc, gnw_bc, gnb_bc = [rowp[:, i, :] for i in range(8)]
            Brow = B("rowp")

            h3 = lambda ap: ap.rearrange("p (h j) -> p h j", j=HD)

            LV = 1 if os.environ.get('P2FORCE') else L
            def pro(c, padd):
                s = c % 2
                Bcur, Bprv = B("cur", s), B("prv", s)
                gCT = gCTs[s]
                padd("sp", lambda e, s=s, c=c: e.dma_start(out=cur[s], in_=zr_d[c * 128:(c + 1) * 128, :]), writes=[Bcur], dma=True)
                if c == 0:
                    padd("pool", lambda e, s=s: e.memset(prv[s][0:1, :], 0.0), writes=[Bprv])
                    padd("sp", lambda e, s=s: e.dma_start(out=prv[s][1:128, :], in_=zr_d[0:127, :]), writes=[Bprv], dma=True)
                else:
                    padd("sp", lambda e, s=s, c=c: e.dma_start(out=prv[s], in_=zr_d[c * 128 - 1:c * 128 + 127, :]), writes=[Bprv], dma=True)
                if LV > 0:
                    padd("sp", lambda e, s=s, c=c: e.dma_start(out=vf[s], in_=vfirst_d[c * 128:(c + 1) * 128, :]), writes=[B("vf", s)], dma=True)
                padd("pool", lambda e, s=s: e.tensor_tensor(out=prv[s], in0=prv[s], in1=cur[s], op=ALU.subtract), reads=[Bcur, Bprv], writes=[Bprv])
                padd("dve", lambda e, s=s: e.tensor_tensor(out=prv[s], in0=prv[s], in1=mu, op=ALU.mult), reads=[Bprv, B("mu")], writes=[Bprv])
                padd("dve", lambda e, s=s: e.tensor_tensor(out=prv[s], in0=prv[s], in1=cur[s], op=ALU.add), reads=[Bprv, Bcur], writes=[Bprv])
                zs = prv[s]
                r_, k_, v_ = zs[:, 0:512], zs[:, 512:1024], zs[:, 1024:1536]
                Blo = B("lo")
                padd("act", lambda e, zs=zs: e.activation(out=lo[:, 0:32], in_=zs[:, 1536:1568], func=AF.Tanh), reads=[Bprv], writes=[Blo])
                padd("act", lambda e, zs=zs: e.activation(out=lo[:, 64:160], in_=zs[:, 1600:1696], func=AF.Sigmoid), reads=[Bprv], writes=[Blo])
                padd("dve", lambda e, zs=zs: e.tensor_copy(lo[:, 32:64], zs[:, 1568:1600]), reads=[Bprv], writes=[Blo])
                padd("dve", lambda e, zs=zs: e.tensor_copy(lo[:, 160:192], zs[:, 1696:1728]), reads=[Bprv], writes=[Blo])
                bk = nb(0, 7)

                def trlo(e, bk=bk):
                    pv = bank_bf(bk)
                    e.transpose(pv[0:32, 0:128], lo[:, 0:32], ident_b)
                    e.transpose(pv[0:32, 128:256], lo[:, 32:64], ident_b)
                    e.transpose(pv[0:96, 256:384], lo[:, 64:160], ident_b)
                    return e.transpose(pv[0:32, 384:512], lo[:, 160:192], ident_b)
                padd("pe", trlo, reads=[Blo, B("identb")], writes=[Bbank[bk]])
                BloT = B("loT")
                for (q, r0) in ((0, 32), (1, 32), (2, 96), (3, 32)):
                    padd("act", lambda e, bk=bk, q=q, r0=r0: e.copy(loT[0:r0, q, :], bank_bf(bk)[0:r0, q * 128:(q + 1) * 128]),
                        reads=[Bbank[bk]], writes=[BloT])
                bk = nb(0, 7)
                padd("pe", lambda e, bk=bk: e.matmul(pbank[bk][:, :], loT[0:32, 0, :], lora[0:32, 0, :], start=True, stop=True),
                    reads=[BloT, B("lora", 0)], writes=[Bbank[bk]])
                padd("dve", lambda e, bk=bk: e.tensor_tensor(out=tmpa, in0=pbank[bk][:, :], in1=w0_bc, op=ALU.add),
                    reads=[Bbank[bk], Brow], writes=[B("tmpa")])
                padd("act", lambda e: e.activation(out=sg, in_=tmpa, func=AF.Sigmoid), reads=[B("tmpa")], writes=[B("sg")])
                padd("act", lambda e: e.copy(sgh, sg), reads=[B("sg")], writes=[B("sgh")])
                padd("dve", lambda e: e.tensor_tensor(out=sgl, in0=sg, in1=sgh, op=ALU.subtract), reads=[B("sg"), B("sgh")], writes=[B("sgl")])
                bk = nb(0, 7)
                padd("pe", lambda e, bk=bk: e.matmul(pbank[bk][:, :], loT[0:32, 1, :], lora[0:32, 1, :], start=True, stop=True),
                    reads=[BloT, B("lora", 1)], writes=[Bbank[bk]])
                padd("dve", lambda e, bk=bk: e.tensor_tensor(out=tmpb, in0=pbank[bk][:, :], in1=a0_bc, op=ALU.add),
                    reads=[Bbank[bk], Brow], writes=[B("tmpb")])
                padd("act", lambda e: e.activation(out=a_sb, in_=tmpb, func=AF.Sigmoid), reads=[B("tmpb")], writes=[B("a")])
                bk = nb(0, 7)
                padd("pe", lambda e, bk=bk: e.matmul(pbank[bk][:, :], loT[0:96, 2, :], lora[0:96, 3, :], start=True, stop=True),
                    reads=[BloT, B("lora", 3)], writes=[Bbank[bk]])
                padd("act", lambda e, bk=bk, s=s: e.copy(gate[s], pbank[bk][:, :]), reads=[Bbank[bk]], writes=[B("gate", s)])
                Bv2 = B("v2", s)
                if LV == 0:
                    padd("pool", lambda e, s=s, v_=v_: e.tensor_copy(v2[s], v_), reads=[Bprv], writes=[Bv2])
                    padd("sp", lambda e, s=s, c=c: e.dma_start(out=vfirst_d[c * 128:(c + 1) * 128, :], in_=v2[s]), reads=[Bv2], dma=True)
                else:
                    bk = nb(0, 7)
                    padd("pe", lambda e, bk=bk: e.matmul(pbank[bk][:, :], loT[0:32, 3, :], lora[0:32, 2, :], start=True, stop=True),
                        reads=[BloT, B("lora", 2)], writes=[Bbank[bk]])
                    padd("dve", lambda e, bk=bk: e.tensor_tensor(out=tmpb, in0=pbank[bk][:, :], in1=mv0_bc, op=ALU.add),
                        reads=[Bbank[bk], Brow], writes=[B("tmpb")])
                    padd("act", lambda e: e.activation(out=tmpb, in_=tmpb, func=AF.Sigmoid), reads=[B("tmpb")], writes=[B("tmpb")])
                    padd("pool", lambda e, s=s, v_=v_: e.tensor_tensor(out=v2[s], in0=vf[s], in1=v_, op=ALU.subtract),
                        reads=[B("vf", s), Bprv], writes=[Bv2])
                    padd("dve", lambda e, s=s: e.tensor_tensor(out=v2[s], in0=v2[s], in1=tmpb, op=ALU.mult), reads=[Bv2, B("tmpb")], writes=[Bv2])
                    padd("dve", lambda e, s=s, v_=v_: e.tensor_tensor(out=v2[s], in0=v2[s], in1=v_, op=ALU.add), reads=[Bv2, Bprv], writes=[Bv2])
                bk_cs = nb(0, 7)
                def mmcs(e, bk=bk_cs):
                    e.matmul(pbank[bk][:, :], ut_b, sgh, start=True, stop=False)
                    return e.matmul(pbank[bk][:, :], ut_b, sgl, start=False, stop=True)
                padd("pe", mmcs, reads=[B("sgh"), B("sgl"), B("utb")], writes=[Bbank[bk_cs]])
                padd("act", lambda e, bk=bk_cs: e.copy(cs, pbank[bk][:, :]), reads=[Bbank[bk]], writes=[B("cs")])
                bk_tot = nb(0, 7)
                def mmtot(e, bk=bk_tot):
                    e.matmul(pbank[bk][:, :], ones_b, sgh, start=True, stop=False)
                    return e.matmul(pbank[bk][:, :], ones_b, sgl, start=False, stop=True)
                padd("pe", mmtot, reads=[B("sgh"), B("sgl"), B("onesb")], writes=[Bbank[bk_tot]])
                padd("act", lambda e: e.activation(out=g_in, in_=cs, func=AF.Exp, scale=-C0), reads=[B("cs")], writes=[B("g_in")])
                padd("act", lambda e: e.activation(out=g_inv, in_=cs, func=AF.Exp, scale=C0), reads=[B("cs")], writes=[B("g_inv")])
                padd("pool", lambda e: e.tensor_tensor(out=tmpa, in0=cs, in1=sg, op=ALU.subtract), reads=[B("cs"), B("sg")], writes=[B("tmpa")])
                padd("act", lambda e: e.activation(out=g_prev, in_=tmpa, func=AF.Exp, scale=-C0), reads=[B("tmpa")], writes=[B("g_prev")])
                padd("dve", lambda e, bk=bk_tot: e.tensor_tensor(out=tmpb, in0=pbank[bk][:, :], in1=cs, op=ALU.subtract),
                    reads=[Bbank[bk], B("cs")], writes=[B("tmpb")])
                padd("act", lambda e: e.activation(out=g_end, in_=tmpb, func=AF.Exp, scale=-C0), reads=[B("tmpb")], writes=[B("g_end")])
                bk = nb(0, 7)

                def mmgc(e, bk=bk):
                    for h in range(NH):
                        e.matmul(pbank[bk][0:64, h:h + 1], sgh[:, h * 64:(h + 1) * 64], ones_b[:, 0:1], start=True, stop=False)
                        ins = e.matmul(pbank[bk][0:64, h:h + 1], sgl[:, h * 64:(h + 1) * 64], ones_b[:, 0:1], start=False, stop=True)
                    return ins
                padd("pe", mmgc, reads=[B("sgh"), B("sgl"), B("onesb")], writes=[Bbank[bk]])
                padd("act", lambda e, bk=bk: e.activation(out=gCT[0:64, :], in_=pbank[bk][0:64, 0:NH], func=AF.Exp, scale=-C0),
                    reads=[Bbank[bk]], writes=[B("gCT", s)])
                Bsm = B("sm")
                padd("dve", lambda e, k_=k_: e.tensor_tensor(out=kk, in0=k_, in1=kk_bc, op=ALU.mult), reads=[Bprv, Brow], writes=[B("kk")])
                padd("pool", lambda e: e.tensor_tensor(out=tmpa, in0=kk, in1=kk, op=ALU.mult), reads=[B("kk")], writes=[B("tmpa")])
                padd("dve", lambda e: e.tensor_reduce(out=sm[:, 0:8], in_=h3(tmpa), axis=AX.X, op=ALU.add), reads=[B("tmpa")], writes=[Bsm])
                padd("act", lambda e: e.activation(out=sm[:, 8:16], in_=sm[:, 0:8], func=AF.Sqrt), reads=[Bsm], writes=[Bsm])
                padd("dve", lambda e: e.tensor_scalar(sm[:, 8:16], sm[:, 8:16], 1e-12, None, ALU.max), reads=[Bsm], writes=[Bsm])
                padd("dve", lambda e: e.reciprocal(sm[:, 16:24], sm[:, 8:16]), reads=[Bsm], writes=[Bsm])
                padd("dve", lambda e: e.tensor_tensor(out=h3(kap), in0=h3(kk), in1=sm[:, 16:24].unsqueeze(2).broadcast_to([128, NH, HD]), op=ALU.mult),
                    reads=[B("kk"), Bsm], writes=[B("kap")])
                padd("dve", lambda e: e.scalar_tensor_tensor(out=kmod, in0=a_sb, scalar=-1.0, in1=ka_bc, op0=ALU.add, op1=ALU.mult),
                    reads=[B("a"), Brow], writes=[B("kmod")])
                padd("dve", lambda e, k_=k_: e.scalar_tensor_tensor(out=kmod, in0=kmod, scalar=1.0, in1=k_, op0=ALU.add, op1=ALU.mult),
                    reads=[B("kmod"), Bprv], writes=[B("kmod")])
                padd("pool", lambda e: e.tensor_tensor(out=b_sb, in0=kap, in1=a_sb, op=ALU.mult), reads=[B("kap"), B("a")], writes=[B("b")])
                padd("pool", lambda e, r_=r_: e.tensor_tensor(out=tmpa, in0=r_, in1=kmod, op=ALU.mult), reads=[Bprv, B("kmod")], writes=[B("tmpa")])
                padd("dve", lambda e: e.tensor_tensor(out=tmpa, in0=tmpa, in1=rk_bc, op=ALU.mult), reads=[B("tmpa"), Brow], writes=[B("tmpa")])
                padd("dve", lambda e, s=s: e.tensor_reduce(out=rkc[s], in_=h3(tmpa), axis=AX.X, op=ALU.add), reads=[B("tmpa")], writes=[B("rkc", s)])
                Btm = B("tm", s)
                T = tm[s]
                padd("dve", lambda e, T=T, r_=r_: e.tensor_tensor(out=T[:, 0, :], in0=r_, in1=g_in, op=ALU.mult), reads=[Bprv, B("g_in")], writes=[B("tm0", s)])
                padd("pool", lambda e, T=T: e.tensor_tensor(out=T[:, 1, :], in0=kap, in1=g_prev, op=ALU.mult), reads=[B("kap"), B("g_prev")], writes=[B("tm1", s)])
                padd("dve", lambda e, T=T: e.tensor_tensor(out=T[:, 2, :], in0=b_sb, in1=g_inv, op=ALU.mult), reads=[B("b"), B("g_inv")], writes=[B("tm2", s)])
                padd("pool", lambda e, T=T: e.tensor_tensor(out=T[:, 3, :], in0=kmod, in1=g_inv, op=ALU.mult), reads=[B("kmod"), B("g_inv")], writes=[B("tm3", s)])
                padd("dve", lambda e, T=T: e.tensor_tensor(out=T[:, 4, :], in0=b_sb, in1=g_end, op=ALU.mult), reads=[B("b"), B("g_end")], writes=[B("tm4", s)])
                padd("pool", lambda e, T=T: e.tensor_tensor(out=T[:, 5, :], in0=kmod, in1=g_end, op=ALU.mult), reads=[B("kmod"), B("g_end")], writes=[B("tm5", s)])
                padd("act", lambda e, T=T, s=s: e.copy(T[:, 6, :], v2[s]), reads=[Bv2], writes=[B("tm6", s)])
                BXT = B("XT", s)
                for (kind, src) in ((0, 1), (1, 0), (2, 2), (3, 3)):
                    bk = nb(0, 7)

                    def trx(e, bk=bk, T=T, src=src):
                        pv = bank_bf(bk)
                        for h in range(NH):
                            ins = e.transpose(pv[0:64, h * 128:(h + 1) * 128], T[:, src, h * 64:(h + 1) * 64], ident_b)
                        return ins
                    padd("pe", trx, reads=[B("tm%d" % src, s), B("identb")], writes=[Bbank[bk]])
                    eng = "act" if kind % 2 else "dve"
                    if eng == "act":
                        padd("act", lambda e, bk=bk, kind=kind, s=s: e.copy(XT[s][0:64, :, kind, :], bank_bf(bk)[0:64, :].rearrange("p (h t) -> p h t", t=128)),
                            reads=[Bbank[bk]], writes=[BXT])
                    else:
                        padd("dve", lambda e, bk=bk, kind=kind, s=s: e.tensor_copy(XT[s][0:64, :, kind, :], bank_bf(bk)[0:64, :].rearrange("p (h t) -> p h t", t=128)),
                            reads=[Bbank[bk]], writes=[BXT])
            def post(c, pull):
                s = c % 2
                T = tm[s]
                X = XT[s]
                BXT = B("XT", s)
                Bv2 = B("v2", s)
                gCT = gCTs[s]
                BgCT = B("gCT", s)
                STo, STn = ST[c % 2], ST[(c + 1) % 2]
                BSTo, BSTn = B("ST", c % 2), B("ST", (c + 1) % 2)
                STbo, STbn = STb[c % 2], STb[(c + 1) % 2]
                BSTbo, BSTbn = B("STb", c % 2), B("STb", (c + 1) % 2)
                bkY = 7
                PK = int(os.environ.get('P2PK', '0'))
                heads = list(range(NH))
                for i in heads:
                    h = i
                    bk = nb(0, 7)

                    def mma(e, bk=bk, h=h):
                        e.matmul(pbank[bk][:, 0:256], X[:, h, 2, :], X[:, h, 0:2, :], start=True, stop=True)
                        return e.matmul(pbank[bk][:, 256:512], X[:, h, 3, :], X[:, h, 0:2, :], start=True, stop=True)
                    add("pe", mma, reads=[BXT], writes=[Bbank[bk]])
                    add("dve", lambda e, bk=bk, i=i: e.tensor_tensor(out=AT[i], in0=pbank[bk][:, :], in1=mask4, op=ALU.mult),
                        reads=[Bbank[bk], Bc], writes=[B("AT", i)])
                    bk2 = nb(0, 7)
                    add("pe", lambda e, bk2=bk2, h=h: e.matmul(pbank[bk2][:, 0:128], X[:, h, 0, :], X[:, h, 2, :], start=True, stop=True),
                        reads=[BXT], writes=[Bbank[bk2]])
                    add("dve", lambda e, bk2=bk2, i=i: e.tensor_tensor(out=Q[i][0][:, 2, :], in0=pbank[bk2][:, 0:128], in1=lowbd, op=ALU.mult),
                        reads=[Bbank[bk2], Bc], writes=[B("Qw", i, 0)])
                    add("act", lambda e, bk2=bk2, i=i: e.copy(tmpm[i][:, 0:128], pbank[bk2][:, 0:128]), reads=[Bbank[bk2]], writes=[B("tmpm", i)])
                    add("pool", lambda e, i=i: e.tensor_tensor(out=Lm[i], in0=tmpm[i][:, 0:128], in1=mlow, op=ALU.mult),
                        reads=[B("tmpm", i), Bc], writes=[B("Lm", i)])
                    add("pool", lambda e, i=i: e.tensor_tensor(out=Q[i][0][:, 0, :], in0=AT[i][:, 0:128], in1=bd16, op=ALU.mult),
                        reads=[B("AT", i), Bc], writes=[B("Qy", i, 0)])
                    add("pool", lambda e, i=i: e.tensor_tensor(out=Q[i][1][:, 1:4:2, :], in0=Q[i][0][:, 0:3:2, :], in1=ident_f.unsqueeze(1).broadcast_to([128, 2, 128]), op=ALU.add),
                        reads=[B("Qy", i, 0), B("Qw", i, 0), Bc], writes=[B("Qzt", i, 1)])
                pull(PK)
                for lev in range(0, 4):
                    p_, n_ = lev % 2, (lev + 1) % 2
                    for i in heads:
                        bk = nb(0, 7)
                        Qp, Qn = Q[i][p_], Q[i][n_]
                        if lev == 0:
                            def mml(e, bk=bk, Qp=Qp):
                                e.matmul(pbank[bk][:, 0:128], Qp[:, 2, :], Qp[:, 0, :], start=True, stop=True)
                                return e.matmul(pbank[bk][:, 128:256], Qp[:, 0, :], Qp[:, 2, :], start=True, stop=True)
                            add("pe", mml, reads=[B("Qy", i, p_), B("Qw", i, p_)], writes=[Bbank[bk]])
                            add("act", lambda e, bk=bk, Qn=Qn: e.copy(Qn[:, 0:3:2, :], pbank[bk][:, 0:256].rearrange("p (a b) -> p a b", b=128)),
                                reads=[Bbank[bk]], writes=[B("Qy", i, n_), B("Qw", i, n_)])
                        elif lev < 3:
                            def mml(e, bk=bk, Qp=Qp):
                                e.matmul(pbank[bk][:, 0:256], Qp[:, 2, :], Qp[:, 0:2, :], start=True, stop=True)
                                return e.matmul(pbank[bk][:, 256:512], Qp[:, 0, :], Qp[:, 2:4, :], start=True, stop=True)
                            add("pe", mml, reads=[B("Qy", i, p_), B("Qw", i, p_), B("Qzt", i, p_)], writes=[Bbank[bk]])
                            pv4 = pbank[bk][:, :].rearrange("p (a b) -> p a b", b=128)
                            add("act", lambda e, pv4=pv4, Qn=Qn: e.copy(Qn[:, 0:3:2, :], pv4[:, 0:3:2, :]),
                                reads=[Bbank[bk]], writes=[B("Qy", i, n_), B("Qw", i, n_)])
                            add("dve", lambda e, pv4=pv4, Qn=Qn, Qp=Qp: e.tensor_tensor(out=Qn[:, 1:4:2, :], in0=pv4[:, 1:4:2, :], in1=Qp[:, 1:4:2, :], op=ALU.add),
                                reads=[Bbank[bk], B("Qzt", i, p_)], writes=[B("Qzt", i, n_)])
                        else:
                            def mml(e, bk=bk, Qp=Qp):
                                e.matmul(pbank[bk][:, 0:128], Qp[:, 2, :], Qp[:, 1, :], start=True, stop=True)
                                return e.matmul(pbank[bk][:, 128:256], Qp[:, 0, :], Qp[:, 3, :], start=True, stop=True)
                            add("pe", mml, reads=[B("Qy", i, p_), B("Qw", i, p_), B("Qzt", i, p_)], writes=[Bbank[bk]])
                            add("dve", lambda e, bk=bk, i=i, Qp=Qp: e.tensor_tensor(out=ZT[i], in0=pbank[bk][:, 0:256].rearrange("p (a b) -> p a b", b=128),
                                                                                  in1=Qp[:, 1:4:2, :], op=ALU.add),
                                reads=[Bbank[bk], B("Qzt", i, p_)], writes=[B("ZT", i)])
                    pull(PK)
                for mi in range(3):
                    MMm = cmask[:, 1280 + 256 * mi:1536 + 256 * mi]
                    bks = {}
                    for i in heads:
                        bk = nb(0, 7)

                        def mmg(e, bk=bk, i=i):
                            e.matmul(pbank[bk][:, 0:128], Lm[i], ZT[i][:, 0, :], start=True, stop=True)
                            return e.matmul(pbank[bk][:, 128:256], AT[i][:, 0:128], ZT[i][:, 1, :], start=True, stop=True)
                        add("pe", mmg, reads=[B("Lm", i), B("AT", i), B("ZT", i)], writes=[Bbank[bk]])
                        add("act", lambda e, bk=bk, i=i: e.copy(GG[i], pbank[bk][:, 0:256]), reads=[Bbank[bk]], writes=[B("GG", i)])
                    pull(PK // 2)
                    for i in heads:
                        bk = nb(0, 7)

                        def mmh(e, bk=bk, i=i):
                            e.matmul(pbank[bk][:, 0:128], ZT[i][:, 1, :], GG[i][:, 0:128], start=True, stop=True)
                            return e.matmul(pbank[bk][:, 128:256], ZT[i][:, 0, :], GG[i][:, 128:256], start=True, stop=True)
                        add("pe", mmh, reads=[B("GG", i), B("ZT", i)], writes=[Bbank[bk]])
                        add("dve", lambda e, bk=bk, i=i, MMm=MMm: e.tensor_tensor(out=tmpm[i], in0=pbank[bk][:, 0:256], in1=MMm, op=ALU.mult),
                            reads=[Bbank[bk], Bc], writes=[B("tmpm", i)])
                        add("pool", lambda e, i=i: e.tensor_tensor(out=ZT[i].rearrange("p a b -> p (a b)"), in0=ZT[i].rearrange("p a b -> p (a b)"), in1=tmpm[i], op=ALU.subtract),
                            reads=[B("ZT", i), B("tmpm", i)], writes=[B("ZT", i)])
                    pull(PK // 2)
                for i in heads:
                    h = i
                    vb_h = T[:, 6, h * 64:(h + 1) * 64]
                    bk = nb(0, 7)
                    add("pe", lambda e, bk=bk, i=i, vb_h=vb_h: e.matmul(pbank[bk][:, 0:64], AT[i][:, 256:384], vb_h, start=True, stop=True),
                        reads=[B("AT", i), B("tm6", s)], writes=[Bbank[bk]])
                    add("act", lambda e, bk=bk, i=i: e.mul(nW1[i], pbank[bk][:, 0:64], -1.0), reads=[Bbank[bk]], writes=[B("nW1", i)])
                pull(PK)
                for i in heads:
                    h = i
                    Zf = ZT[i][:, 0, :]
                    bk = nb(0, 7)

                    def mmku(e, bk=bk, i=i, h=h, Zf=Zf):
                        e.matmul(pbank[bk][:, 0:64], Zf, T[:, 1, h * 64:(h + 1) * 64], start=True, stop=True)
                        return e.matmul(pbank[bk][:, 64:128], Zf, nW1[i], start=True, stop=True)
                    add("pe", mmku, reads=[B("ZT", i), B("tm1", s), B("nW1", i)], writes=[Bbank[bk]])
                    add("dve", lambda e, bk=bk, i=i: e.tensor_copy(KU[i], pbank[bk][:, 0:128]), reads=[Bbank[bk]], writes=[B("KU", i)])
                pull(PK)
                for i in heads:
                    h = i
                    bk = nb(0, 7)

                    def mmmr(e, bk=bk, i=i, h=h):
                        e.matmul(pbank[bk][0:64, 0:64], KU[i][:, 0:64], T[:, 4, h * 64:(h + 1) * 64], start=True, stop=True)
                        return e.matmul(pbank[bk][0:64, 64:192], KU[i][:, 0:64], AT[i][:, 128:256], start=True, stop=True)
                    add("pe", mmmr, reads=[B("KU", i), B("tm4", s), B("AT", i)], writes=[Bbank[bk]])
                    add("act", lambda e, bk=bk, i=i: e.mul(Mc[i][0:64, :], pbank[bk][0:64, 0:64], -1.0), reads=[Bbank[bk]], writes=[B("Mc", i)])
                    add("dve", lambda e, bk=bk, i=i, h=h: e.tensor_tensor(out=RhT[i][0:64, :], in0=X[0:64, h, 1, :], in1=pbank[bk][0:64, 64:192], op=ALU.subtract),
                        reads=[Bbank[bk], BXT], writes=[B("RhT", i)])
                pull(PK)
                for i in heads:
                    h = i
                    vb_h = T[:, 6, h * 64:(h + 1) * 64]

                    def mmy(e, i=i, h=h, vb_h=vb_h):
                        o = pbank[bkY][:, h * 64:(h + 1) * 64]
                        e.matmul(o, RhT[i][:, :], STbo[:, h, :], start=True, stop=False)
                        e.matmul(o, AT[i][:, 128:256], KU[i][:, 64:128], start=False, stop=False)
                        return e.matmul(o, AT[i][:, 384:512], vb_h, start=False, stop=True)
                    add("pe", mmy, reads=[B("RhT", i), BSTbo, B("AT", i), B("KU", i), B("tm6", s)], writes=[Bbank[bkY]])
                    bk = nb(0, 7)

                    def mms(e, bk=bk, i=i, h=h, vb_h=vb_h):
                        o = pbank[bk][0:64, 0:64]
                        e.matmul(o, T[:, 4, h * 64:(h + 1) * 64], KU[i][:, 64:128], start=True, stop=False)
                        e.matmul(o, T[:, 5, h * 64:(h + 1) * 64], vb_h, start=False, stop=False)
                        return e.matmul(o, Mc[i][:, :], STbo[:, h, :], start=False, stop=True)
                    add("pe", mms, reads=[B("tm4", s), B("tm5", s), B("tm6", s), B("KU", i), B("Mc", i), BSTbo], writes=[Bbank[bk]])
                    add("dve", lambda e, bk=bk, h=h: e.scalar_tensor_tensor(out=STn[0:64, h, :], in0=STo[0:64, h, :], scalar=gCT[0:64, h:h + 1],
                                                                          in1=pbank[bk][0:64, 0:64], op0=ALU.mult, op1=ALU.add),
                        reads=[Bbank[bk], BSTo, BgCT], writes=[B("STh", (c + 1) % 2, h)])
                    add("act", lambda e, h=h: e.copy(STbn[0:64, h, :], STn[0:64, h, :]), reads=[B("STh", (c + 1) % 2, h)], writes=[BSTbn, BSTn])
                pull(PK)
                Bsm = B("smt")
                sm = smt
                add("act", lambda e: e.copy(ysb, pbank[bkY][:, :]), reads=[Bbank[bkY]], writes=[B("ysb")])
                add("pool", lambda e: e.tensor_tensor(out=ysq, in0=ysb, in1=ysb, op=ALU.mult), reads=[B("ysb")], writes=[B("ysq")])
                add("dve", lambda e: e.tensor_reduce(out=sm[:, 24:32], in_=h3(ysb), axis=AX.X, op=ALU.add), reads=[B("ysb")], writes=[Bsm])
                add("dve", lambda e: e.tensor_reduce(out=sm[:, 32:40], in_=h3(ysq), axis=AX.X, op=ALU.add), reads=[B("ysq")], writes=[Bsm])
                add("dve", lambda e: e.tensor_scalar(sm[:, 24:32], sm[:, 24:32], 1.0 / HD, None, ALU.mult), reads=[Bsm], writes=[Bsm])
                add("dve", lambda e: e.tensor_tensor(out=sm[:, 40:48], in0=sm[:, 24:32], in1=sm[:, 24:32], op=ALU.mult), reads=[Bsm], writes=[Bsm])
                add("dve", lambda e: e.scalar_tensor_tensor(out=sm[:, 32:40], in0=sm[:, 32:40], scalar=1.0 / HD, in1=sm[:, 40:48], op0=ALU.mult, op1=ALU.subtract),
                    reads=[Bsm], writes=[Bsm])
                add("act", lambda e: e.activation(out=sm[:, 40:48], in_=sm[:, 32:40], func=AF.Sqrt, bias=GN_EPS, scale=1.0), reads=[Bsm], writes=[Bsm])
                add("dve", lambda e: e.reciprocal(sm[:, 48:56], sm[:, 40:48]), reads=[Bsm], writes=[Bsm])
                add("dve", lambda e: e.tensor_tensor(out=h3(ysb), in0=h3(ysb), in1=sm[:, 24:32].unsqueeze(2).broadcast_to([128, NH, HD]), op=ALU.subtract),
                    reads=[B("ysb"), Bsm], writes=[B("ysb")])
                add("dve", lambda e: e.tensor_tensor(out=h3(ysb), in0=h3(ysb), in1=sm[:, 48:56].unsqueeze(2).broadcast_to([128, NH, HD]), op=ALU.mult),
                    reads=[B("ysb"), Bsm], writes=[B("ysb")])
                add("pool", lambda e: e.tensor_tensor(out=ysb, in0=ysb, in1=gnw_bc, op=ALU.mult), reads=[B("ysb"), Brow], writes=[B("ysb")])
                add("pool", lambda e: e.tensor_tensor(out=ysb, in0=ysb, in1=gnb_bc, op=ALU.add), reads=[B("ysb"), Brow], writes=[B("ysb")])
                add("dve", lambda e: e.tensor_tensor(out=h3(ysq), in0=h3(v2[s]), in1=rkc[s].unsqueeze(2).broadcast_to([128, NH, HD]), op=ALU.mult),
                    reads=[Bv2, B("rkc", s)], writes=[B("ysq")])
                add("pool", lambda e: e.tensor_tensor(out=ysb, in0=ysb, in1=ysq, op=ALU.add), reads=[B("ysb"), B("ysq")], writes=[B("ysb")])
                add("dve", lambda e: e.tensor_tensor(out=rwo[s], in0=ysb, in1=gate[s], op=ALU.mult), reads=[B("ysb"), B("gate", s)], writes=[B("rwo", s)])
                add("sp", lambda e: e.dma_start(out=rwo_d[c * 128:(c + 1) * 128, :], in_=rwo[s]), reads=[B("rwo", s)], dma=True)
            plist = []

            def padd(*a, **k):
                plist.append((a, k))

            plim = {"n": 0}
            PLIM = int(os.environ.get('P2LIM', '100000'))

            def pull(n):
                for _ in range(n):
                    if not plist:
                        return
                    if n < 10 ** 8 and plim["n"] >= PLIM:
                        return
                    plim["n"] += 1
                    a, k = plist.pop(0)
                    add(*a, **k)

            pro(0, padd)
            pull(10 ** 9)
            for c in range(NT):
                if c + 1 < NT:
                    pro(c + 1, padd)
                if os.environ.get('P2NOIL'):
                    pull(10 ** 9)
                plim["n"] = 0
                post(c, pull)
                pull(10 ** 9)
            S_.barrier()

        def phase3(L):
            A.off = P0
            NSL = 6
            PD = 3
            qk = A.alloc((8, S), BF16)
            vaug = A.alloc((NT, NH, HD + 1), BF16)
            mbig = A.alloc((MBW,), BF16)
            esb = [A.alloc((512,), BF16) for _ in range(NSL)]
            psb = [A.alloc((512,), BF16) for _ in range(NSL)]
            osb = [A.alloc((4, 512), F32) for _ in range(2)]
            rec = [A.alloc((4,), F32) for _ in range(2)]
            for m in range(8):
                add("sp", lambda e, m=m: e.dma_start(out=qk[:, m, :], in_=qkT_d[m * 128:(m + 1) * 128, :]), writes=[B("qk", m)], dma=True)
            for t in range(NT):
                add("sp", lambda e, t=t: e.dma_start(out=vaug[:, t, :, 0:HD], in_=vbf_d[t * 128:(t + 1) * 128, :].rearrange("p (h c) -> p h c", c=HD)),
                    writes=[B("vaug", t)], dma=True)
            add("pool", lambda e: e.dma_start(out=mbig, in_=mbig_d), writes=[B("mbig")], dma=True)
            add("pool", lambda e: e.memset(vaug[:, :, :, HD:HD + 1], 1.0), writes=[B("vones")])
            Bqk = [B("qk", m) for m in range(8)]
            units = []
            for sb in range(S // 512):
                q0 = sb * 512
                kt_lo = max(0, (q0 - 2048) // 128)
                kt_hi = (q0 + 511) // 128
                for h in range(NH):
                    for kt in range(kt_lo, kt_hi + 1):
                        units.append((sb, h, kt, kt == kt_lo, kt == kt_hi))

            def front(u, idx):
                sb, h, kt, first, last = u
                q0 = sb * 512
                ph = (h % 2) * 64
                Dd = q0 - kt * 128
                bs = nb(0, 6)
                es = idx % NSL
                add("pe", lambda e: e.matmul(pbank[bs][:, :], qk[ph:ph + 64, 4 + h // 2, kt * 128:(kt + 1) * 128], qk[ph:ph + 64, h // 2, q0:q0 + 512],
                                             start=True, stop=True),
                    reads=[Bqk[h // 2], Bqk[4 + h // 2]], writes=[Bbank[bs]])
                add("act", lambda e: e.activation(out=esb[es], in_=pbank[bs][:, :], func=AF.Exp, scale=1.0 / 8.0),
                    reads=[Bbank[bs]], writes=[B("esb", es)])
                add("dve", lambda e: e.tensor_tensor(out=psb[es], in0=esb[es], in1=mbig[:, Dd + 384:Dd + 384 + 512], op=ALU.mult),
                    reads=[B("esb", es), B("mbig")], writes=[B("psb", es)])

            def back(u, idx):
                sb, h, kt, first, last = u
                q0 = sb * 512
                Dd = q0 - kt * 128
                es = idx % NSL
                bacc = 6 + (h % 2)
                os_ = sb % 2
                qss = [qs for qs in range(4) if (Dd + qs * 128 + 127 >= 0) and (Dd + qs * 128 - 127 <= 2048)]

                def mmpv(e):
                    ins = None
                    for qs in qss:
                        ins = e.matmul(pbank[bacc][:, qs * 65:(qs + 1) * 65], psb[es][:, qs * 128:(qs + 1) * 128], vaug[:, kt, h, :],
                                       start=(first and qs == qss[0]), stop=last, skip_group_check=True)
                    return ins
                add("pe", mmpv, reads=[B("psb", es), B("vaug", kt), B("vones")], writes=[Bbank[bacc]])
                if last:
                    accv = pbank[bacc][:, 0:260].rearrange("p (q c) -> p q c", c=65)
                    add("dve", lambda e: e.reciprocal(rec[os_].unsqueeze(2), accv[:, :, 64:65]), reads=[Bbank[bacc]], writes=[B("rec", os_)])
                    add("dve", lambda e: e.tensor_tensor(out=osb[os_][:, :, h * 64:(h + 1) * 64], in0=accv[:, :, 0:64],
                                                         in1=rec[os_].unsqueeze(2).broadcast_to([128, 4, 64]), op=ALU.mult),
                        reads=[Bbank[bacc], B("rec", os_)], writes=[B("osb", os_)])
                    if h == NH - 1:
                        add("sp", lambda e: e.dma_start(out=atto_d[q0:q0 + 512, :].rearrange("(q p) c -> p q c", p=128), in_=osb[os_]),
                            reads=[B("osb", os_)], dma=True)

            n = len(units)
            for idx in range(n + PD):
                if idx < n:
                    front(units[idx], idx)
                if idx >= PD:
                    back(units[idx - PD], idx - PD)
            S_.barrier()

        def phase4(L, xsrc):
            A.off = P0
            wo = A.alloc((8, D), BF16)
            gpo = A.alloc((D,), F32)
            aog = A.alloc((512,), F32)
            at = [A.alloc((512,), F32) for _ in range(2)]
            cat = [A.alloc((D,), BF16) for _ in range(2)]
            catT = [A.alloc((8, 128), BF16) for _ in range(2)]
            xt = [A.alloc((D,), F32) for _ in range(2)]
            xo = [A.alloc((D,), F32) for _ in range(2)]
            junk = A.alloc((D,), BF16)
            st4 = [A.alloc((8,), F32) for _ in range(2)]
            for kc in range(8):
                add("pool", lambda e, kc=kc: e.dma_start(out=wo[:, kc, :], in_=w_out[L, kc * 128:(kc + 1) * 128, :]), writes=[B("wo", kc)], dma=True)
            add("sp", lambda e: e.dma_start(out=gpo, in_=gpost_d[L, 0:1, :].partition_broadcast(128)), writes=[B("gpo")], dma=True)
            add("sp", lambda e: e.dma_start(out=aog, in_=rowp_d[L:L + 1, 8 * 512:9 * 512].partition_broadcast(128)), writes=[B("aog")], dma=True)
            Bwo = [B("wo", kc) for kc in range(8)]
            for t in range(NT):
                s = t % 2
                add("sp", lambda e, s=s, t=t: e.dma_start(out=at[s], in_=atto_d[t * 128:(t + 1) * 128, :]), writes=[B("at", s)], dma=True)
                add("sp", lambda e, s=s, t=t: e.dma_start(out=cat[s][:, 512:1024], in_=rwo_d[t * 128:(t + 1) * 128, :]), writes=[B("catr", s)], dma=True)
                add("sp", lambda e, s=s, t=t: e.dma_start(out=xt[s], in_=xsrc[t * 128:(t + 1) * 128, :]), writes=[B("xt", s)], dma=True)
                Bst = B("st4", s)
                add("act", lambda e, s=s: e.activation(out=junk[:, 0:512], in_=at[s], func=AF.Square, accum_out=st4[s][:, 0:1]), reads=[B("at", s)], writes=[Bst])
                add("act", lambda e, s=s: e.activation(out=st4[s][:, 1:2], in_=st4[s][:, 0:1], func=AF.Sqrt, bias=EPS, scale=1.0 / DA), reads=[Bst], writes=[Bst])
                add("dve", lambda e, s=s: e.reciprocal(st4[s][:, 2:3], st4[s][:, 1:2]), reads=[Bst], writes=[Bst])
                add("dve", lambda e, s=s: e.scalar_tensor_tensor(out=cat[s][:, 0:512], in0=at[s], scalar=st4[s][:, 2:3], in1=aog, op0=ALU.mult, op1=ALU.mult),
                    reads=[B("at", s), Bst, B("aog")], writes=[B("cata", s)])
                tb = nb(0, 2)

                def tr(e, s=s, tb=tb):
                    pv = bank_bf(tb)
                    for kc in range(8):
                        ins = e.transpose(pv[:, kc * 128:(kc + 1) * 128], cat[s][:, kc * 128:(kc + 1) * 128], ident_b)
                    return ins
                add("pe", tr, reads=[B("cata", s), B("catr", s), B("identb")], writes=[Bbank[tb]])
                add("act", lambda e, s=s, tb=tb: e.copy(catT[s].rearrange("p k t -> p (k t)"), bank_bf(tb)), reads=[Bbank[tb]], writes=[B("catT", s)])
                bks = []
                for n in range(2):
                    bk = nb(2, 8)
                    bks.append(bk)

                    def mmo(e, s=s, bk=bk, n=n):
                        for kc in range(8):
                            ins = e.matmul(pbank[bk][:, :], catT[s][:, kc, :], wo[:, kc, n * 512:(n + 1) * 512], start=(kc == 0), stop=(kc == 7))
                        return ins
                    add("pe", mmo, reads=Bwo + [B("catT", s)], writes=[Bbank[bk]])
                    add("act", lambda e, s=s, bk=bk, n=n: e.activation(out=junk[:, 0:512], in_=pbank[bk][:, :], func=AF.Square, accum_out=st4[s][:, 3 + n:4 + n]),
                        reads=[Bbank[bk]], writes=[B("st4b", s, n)])
                Bst2 = B("st4c", s)
                add("dve", lambda e, s=s: e.tensor_tensor(out=st4[s][:, 5:6], in0=st4[s][:, 3:4], in1=st4[s][:, 4:5], op=ALU.add),
                    reads=[B("st4b", s, 0), B("st4b", s, 1)], writes=[Bst2])
                add("act", lambda e, s=s: e.activation(out=st4[s][:, 6:7], in_=st4[s][:, 5:6], func=AF.Sqrt, bias=EPS, scale=1.0 / D), reads=[Bst2], writes=[Bst2])
                add("dve", lambda e, s=s: e.reciprocal(st4[s][:, 7:8], st4[s][:, 6:7]), reads=[Bst2], writes=[Bst2])
                for n in range(2):
                    bk = bks[n]
                    add("dve", lambda e, s=s, bk=bk, n=n: e.scalar_tensor_tensor(out=xo[s][:, n * 512:(n + 1) * 512], in0=pbank[bk][:, :], scalar=st4[s][:, 7:8],
                                                                              in1=gpo[:, n * 512:(n + 1) * 512], op0=ALU.mult, op1=ALU.mult),
                        reads=[Bbank[bk], Bst2, B("gpo")], writes=[B("xo", s, n)])
                add("pool", lambda e, s=s: e.tensor_tensor(out=xo[s], in0=xo[s], in1=xt[s], op=ALU.add),
                    reads=[B("xo", s, 0), B("xo", s, 1), B("xt", s)], writes=[B("xo", s, 0), B("xo", s, 1)])
                add("sp", lambda e, s=s, t=t: e.dma_start(out=xmid_d[t * 128:(t + 1) * 128, :], in_=xo[s]), reads=[B("xo", s, 0), B("xo", s, 1)], dma=True)
            S_.barrier()

        def phase5(L, xdst, outbufs):
            A.off = P0
            TB = 256
            NB5 = S // TB
            wu = A.alloc((8, DFF), BF16)
            wd = A.alloc((32, D), BF16)
            gpre = A.alloc((8,), F32)
            gpo = A.alloc((D,), F32)
            xt = [A.alloc((2, D), F32) for _ in range(2)]
            xn = [A.alloc((D,), BF16) for _ in range(2)]
            hT = [A.alloc((8, TB), BF16) for _ in range(2)]
            aT = A.alloc((32, TB), BF16)
            rl = [A.alloc((TB,), BF16) for _ in range(3)]
            xo = [A.alloc((D,), F32) for _ in range(2)]
            junk = A.alloc((D,), BF16)
            st5 = [A.alloc((8,), F32) for _ in range(2)]
            for kc in range(8):
                add("pool", lambda e, kc=kc: e.dma_start(out=wu[:, kc, :], in_=w_up[L, kc * 128:(kc + 1) * 128, :]), writes=[B("wu", kc)], dma=True)
            for g in range(4):
                add("pool", lambda e, g=g: e.dma_start(out=wd[:, g * 8:(g + 1) * 8, :], in_=w_dn[L, g * 1024:(g + 1) * 1024, :].rearrange("(m p) d -> p m d", p=128)),
                    writes=[B("wd", g)], dma=True)
            add("sp", lambda e: e.dma_start(out=gpre, in_=gpre_d[L, 1]), writes=[B("gpre")], dma=True)
            add("sp", lambda e: e.dma_start(out=gpo, in_=gpost_d[L, 1:2, :].partition_broadcast(128)), writes=[B("gpo")], dma=True)
            Bwu = [B("wu", kc) for kc in range(8)]
            Bwd = [B("wd", g) for g in range(4)]
            cnt = 0
            for b in range(NB5):
                bs = b % 2
                BhT = B("hT", bs)
                for ts in range(2):
                    t = 2 * b + ts
                    s = t % 2
                    Bx = B("xt", bs, ts)
                    add("sp", lambda e, bs=bs, ts=ts, t=t: e.dma_start(out=xt[bs][:, ts, :], in_=xmid_d[t * 128:(t + 1) * 128, :]), writes=[Bx], dma=True)
                    Bst = B("st5", s)
                    add("act", lambda e, bs=bs, ts=ts, s=s: e.activation(out=junk, in_=xt[bs][:, ts, :], func=AF.Square, accum_out=st5[s][:, 0:1]), reads=[Bx], writes=[Bst])
                    add("act", lambda e, s=s: e.activation(out=st5[s][:, 1:2], in_=st5[s][:, 0:1], func=AF.Sqrt, bias=EPS, scale=1.0 / D), reads=[Bst], writes=[Bst])
                    add("dve", lambda e, s=s: e.reciprocal(st5[s][:, 2:3], st5[s][:, 1:2]), reads=[Bst], writes=[Bst])
                    add("dve", lambda e, bs=bs, ts=ts, s=s: e.tensor_scalar(xn[s], xt[bs][:, ts, :], st5[s][:, 2:3], None, ALU.mult), reads=[Bx, Bst], writes=[B("xn", s)])
                    tb = nb(0, 2)

                    def tr(e, s=s, tb=tb):
                        pv = bank_bf(tb)
                        for kc in range(8):
                            ins = e.transpose(pv[:, kc * 128:(kc + 1) * 128], xn[s][:, kc * 128:(kc + 1) * 128], ident_b)
                        return ins
                    add("pe", tr, reads=[B("xn", s), B("identb")], writes=[Bbank[tb]])
                    add("dve", lambda e, tb=tb, bs=bs, ts=ts: e.tensor_tensor(
                        out=hT[bs][:, :, ts * 128:(ts + 1) * 128], in0=bank_bf(tb).rearrange("p (k t) -> p k t", t=128),
                        in1=gpre.unsqueeze(2).broadcast_to([128, 8, 128]), op=ALU.mult),
                        reads=[Bbank[tb], B("gpre")], writes=[BhT])
                for m2 in range(16):
                    bk = nb(2, 6)

                    def mmu(e, bk=bk, m2=m2, bs=bs):
                        for j in range(2):
                            m = 2 * m2 + j
                            for kc in range(8):
                                ins = e.matmul(pbank[bk][:, j * TB:(j + 1) * TB], wu[:, kc, m * 128:(m + 1) * 128], hT[bs][:, kc, :], start=(kc == 0), stop=(kc == 7))
                        return ins
                    add("pe", mmu, reads=Bwu + [BhT], writes=[Bbank[bk]])
                    for j in range(2):
                        m = 2 * m2 + j
                        rs = cnt % 3
                        cnt += 1
                        add("act", lambda e, bk=bk, j=j, rs=rs: e.activation(out=rl[rs], in_=pbank[bk][:, j * TB:(j + 1) * TB], func=AF.Relu),
                            reads=[Bbank[bk]], writes=[B("rl", rs)])
                        eng = "pool" if m % 2 else "dve"
                        add(eng, lambda e, m=m, rs=rs: e.tensor_tensor(out=aT[:, m, :], in0=rl[rs], in1=rl[rs], op=ALU.mult),
                            reads=[B("rl", rs)], writes=[B("aT", m)])
                BaT = [B("aT", m) for m in range(32)]
                for ts in range(2):
                    t = 2 * b + ts
                    s = t % 2
                    bks = []
                    for n in range(2):
                        bk = nb(6, 8) if False else (6 + n)
                        bks.append(bk)

                        def mmd(e, bk=bk, ts=ts, n=n):
                            for m in range(32):
                                ins = e.matmul(pbank[bk][:, :], aT[:, m, ts * 128:(ts + 1) * 128], wd[:, m, n * 512:(n + 1) * 512], start=(m == 0), stop=(m == 31))
                            return ins
                        add("pe", mmd, reads=Bwd + BaT, writes=[Bbank[bk]])
                        add("act", lambda e, s=s, bk=bk, n=n: e.activation(out=junk[:, 0:512], in_=pbank[bk][:, :], func=AF.Square, accum_out=st5[s][:, 3 + n:4 + n]),
                            reads=[Bbank[bk]], writes=[B("st5b", s, n)])
                    Bst2 = B("st5c", s)
                    add("dve", lambda e, s=s: e.tensor_tensor(out=st5[s][:, 5:6], in0=st5[s][:, 3:4], in1=st5[s][:, 4:5], op=ALU.add),
                        reads=[B("st5b", s, 0), B("st5b", s, 1)], writes=[Bst2])
                    add("act", lambda e, s=s: e.activation(out=st5[s][:, 6:7], in_=st5[s][:, 5:6], func=AF.Sqrt, bias=EPS, scale=1.0 / D), reads=[Bst2], writes=[Bst2])
                    add("dve", lambda e, s=s: e.reciprocal(st5[s][:, 7:8], st5[s][:, 6:7]), reads=[Bst2], writes=[Bst2])
                    for n in range(2):
                        bk = bks[n]
                        add("dve", lambda e, s=s, bk=bk, n=n: e.scalar_tensor_tensor(out=xo[s][:, n * 512:(n + 1) * 512], in0=pbank[bk][:, :], scalar=st5[s][:, 7:8],
                                                                                  in1=gpo[:, n * 512:(n + 1) * 512], op0=ALU.mult, op1=ALU.mult),
                            reads=[Bbank[bk], Bst2, B("gpo")], writes=[B("xo", s, n)])
                    add("pool", lambda e, s=s, bs=bs, ts=ts: e.tensor_tensor(out=xo[s], in0=xo[s], in1=xt[bs][:, ts, :], op=ALU.add),
                        reads=[B("xo", s, 0), B("xo", s, 1), B("xt", bs, ts)], writes=[B("xo", s, 0), B("xo", s, 1)])
                    ob = B("outd", L, t)
                    outbufs.append(ob)
                    add("sp", lambda e, s=s, t=t: e.dma_start(out=xdst[t * 128:(t + 1) * 128, :], in_=xo[s]), reads=[B("xo", s, 0), B("xo", s, 1)], writes=[ob], dma=True)
            S_.barrier()

        outbufs = []
        for L in range(depth):
            xsrc = x_in if L == 0 else xres_d
            xdst = out_d if L == depth - 1 else xres_d
            if phases is None or 1 in phases:
                phase1(L, xsrc)
            if phases is None or 2 in phases:
                phase2(L)
            if phases is None or 3 in phases:
                phase3(L)
            if phases is None or 4 in phases:
                phase4(L, xsrc)
            if phases is None or 5 in phases:
                phase5(L, xdst, outbufs)
        S_.emit()
    return nc


def _count_mask():
    ki = np.arange(128)[:, None]
    xx = np.arange(MBW)[None, :]
    d = xx - 384 - ki
    c = ((d >= 0) & (d <= 128)).astype(np.float32)
    c += ((d >= 0) & (d <= 512) & (d % 4 == 0)).astype(np.float32)
    c += ((d >= 0) & (d <= 2048) & (d % 16 == 0)).astype(np.float32)
    return np.ascontiguousarray(c, dtype=np.float32)


def _const_masks():
    p = np.arange(128)[:, None]
    f = np.arange(128)[None, :]
    ident = (p == f).astype(np.float32)
    ut = (p <= f).astype(np.float32)
    strict = (p < f).astype(np.float32)
    incl = (p <= f).astype(np.float32)
    low = (f < p).astype(np.float32)
    ones = np.ones((128, 128), np.float32)
    bd16 = ((p // 16) == (f // 16)).astype(np.float32)
    lowbd = low * bd16
    mms = []
    for bs in (16, 32, 64):
        mu_ = (((p // bs) % 2 == 0) & ((f // bs) == (p // bs) + 1)).astype(np.float32)
        mms += [mu_, np.ascontiguousarray(mu_.T)]
    return np.ascontiguousarray(np.concatenate([ident, ut, strict, incl, strict, incl, low, ones, -bd16, -lowbd] + mms, axis=1))


def pack_inputs(depth, norm_mix_pre, norm_mix_post, norm_ffn_pre, norm_ffn_post, w_in_first, w_in_rest,
                mu_shift, mu_shift_mv, attn_out_gain, decay_w0, decay_up, aaa_a0, aaa_up, mv_v0, mv_up,
                gate_up, k_k, k_a, r_k, gn_w, gn_b, w_out, w_ffn_up, w_ffn_down):
    f = np.float32
    w_in = np.zeros((depth, D, NCOLS), f)
    w_in[0, :, :w_in_first.shape[1]] = w_in_first
    for i in range(1, depth):
        w_in[i] = w_in_rest[i - 1]
    gpre = np.zeros((depth, 2, 128, 8), f)
    gpost = np.zeros((depth, 2, D), f)
    mu = np.zeros((depth, NZ), f)
    rowp = np.zeros((depth, 9, 512), f)
    lora = np.zeros((depth, 4, 96, 512), f)
    for i in range(depth):
        gpre[i, 0] = np.asarray(norm_mix_pre[i]).reshape(8, 128).T
        gpre[i, 1] = np.asarray(norm_ffn_pre[i]).reshape(8, 128).T
        gpost[i, 0] = norm_mix_post[i]
        gpost[i, 1] = norm_ffn_post[i]
        mu[i, :1696] = mu_shift[i]
        rowp[i, 0] = decay_w0[i]
        rowp[i, 1] = aaa_a0[i]
        rowp[i, 3] = k_k[i]
        rowp[i, 4] = k_a[i]
        rowp[i, 5] = np.asarray(r_k[i]).reshape(512)
        rowp[i, 6] = gn_w[i]
        rowp[i, 7] = gn_b[i]
        rowp[i, 8] = attn_out_gain[i]
        lora[i, 0, :32] = decay_up[i]
        lora[i, 1, :32] = aaa_up[i]
        lora[i, 3, :96] = gate_up[i]
        if i > 0:
            mu[i, 1696:] = mu_shift_mv[i - 1]
            rowp[i, 2] = mv_v0[i - 1]
            lora[i, 2, :32] = mv_up[i - 1]
    return {
        "w_in": w_in, "w_out": np.ascontiguousarray(w_out[:depth], f), "w_up": np.ascontiguousarray(w_ffn_up[:depth], f),
        "w_dn": np.ascontiguousarray(w_ffn_down[:depth], f), "gpre": gpre, "gpost": gpost, "mu": mu,
        "rowp": rowp.reshape(depth, 9 * 512), "lora": lora, "cmask": _const_masks(), "mbig": _count_mask(),
    }


_CACHE = {}


def kernel(x, **params):
    x = np.asarray(x, np.float32)
    Bn, S, _ = x.shape
    depth = 4
    params = {k: np.asarray(v, np.float32) for k, v in params.items()}
    shared = pack_inputs(depth, **params)
    key = (S, depth)
    if key not in _CACHE:
        _CACHE[key] = build_program(S, depth)
    nc = _CACHE[key]
    in_maps = []
    for b in range(Bn):
        m = dict(shared)
        m["x"] = np.ascontiguousarray(x[b])
        in_maps.append(m)
    res = run_bass_kernel_spmd(nc, in_maps, core_ids=list(range(Bn)))
    return np.stack([np.asarray(r["out"], np.float32) for r in res.results], axis=0)
```

```python
import contextlib
import numpy as np
import ml_dtypes
import concourse.bass as bass
import concourse.mybir as mybir
from concourse.bass_utils import run_bass_kernel_spmd

F32 = mybir.dt.float32
BF16 = mybir.dt.bfloat16
ALU = mybir.AluOpType
AF = mybir.ActivationFunctionType
AX = mybir.AxisListType

D = 1024
DA = 512
DR = 512
NH = 8
HD = 64
NCOLS = 3264
NZ = 1728
DFF = 4096
EPS = 1e-6
GN_EPS = 64e-5
C0 = float(np.exp(-0.5))
MBW = 2944
N_CORES = 8

ENGS = ("pe", "act", "dve", "pool", "sp")
N_DMA_SEMS = 40
import os
P2STOP = int(os.environ.get('P2STOP', '9'))


class Buf:
    __slots__ = ("name", "last_w", "readers", "excl")

    def __init__(self, name):
        self.name = name
        self.last_w = None
        self.readers = {}
        self.excl = False


class Op:
    __slots__ = ("eng", "fn", "deps", "sem", "val", "is_dma")

    def __init__(self, eng, fn, is_dma):
        self.eng = eng
        self.fn = fn
        self.deps = set()
        self.sem = None
        self.val = None
        self.is_dma = is_dma


class Sched:
    def __init__(self, nc):
        self.nc = nc
        self.ops = []
        self.bufs = {}
        self.dma_rr = 0
        self.dma_last = [None] * N_DMA_SEMS
        self.dma_cnt = [0] * N_DMA_SEMS
        self.last_op = {e: None for e in ENGS}
        self.dma_since = []
        self.sync_same_engine = True

    def B(self, *key):
        b = self.bufs.get(key)
        if b is None:
            b = Buf(key)
            self.bufs[key] = b
        return b

    def add(self, eng, fn, reads=(), writes=(), dma=False):
        op = Op(eng, fn, dma)
        deps = op.deps
        for b in reads:
            if b.last_w is not None:
                deps.add(b.last_w)
            if b.excl:
                for k, r in b.readers.items():
                    if k != eng:
                        deps.add(r)
        for b in writes:
            if b.last_w is not None:
                deps.add(b.last_w)
            for r in b.readers.values():
                deps.add(r)
        for b in reads:
            b.readers[id(op) if dma else eng] = op
        for b in writes:
            b.last_w = op
            b.readers = {}
        deps.discard(op)
        if dma:
            k = self.dma_rr
            self.dma_rr = (k + 1) % N_DMA_SEMS
            prev = self.dma_last[k]
            if prev is not None:
                deps.add(prev)
            self.dma_cnt[k] += 16
            op.sem = k
            op.val = self.dma_cnt[k]
            self.dma_last[k] = op
            self.dma_since.append(op)
        else:
            self.last_op[eng] = op
        self.ops.append(op)
        return op

    def barrier(self):
        lasts = [o for o in self.last_op.values() if o is not None] + list(self.dma_since)
        for e in ENGS:
            op = Op(e, None, False)
            op.deps = set(lasts)
            self.ops.append(op)
        self.dma_since = []
        for b in self.bufs.values():
            b.last_w = None
            b.readers = {}

    def _skip(self, d, op):
        if d.is_dma or op.is_dma or d.eng != op.eng:
            return False
        return d.eng == "pe" or not self.sync_same_engine

    def emit(self):
        nc = self.nc
        with contextlib.ExitStack() as st:
            esem = {e: st.enter_context(nc.semaphore("s_" + e)) for e in ENGS}
            dsem = [st.enter_context(nc.semaphore("d%d" % i)) for i in range(N_DMA_SEMS)]
            needed = set()
            for op in self.ops:
                for d in op.deps:
                    if d.is_dma or self._skip(d, op):
                        continue
                    needed.add(d)
            cnt = {e: 0 for e in ENGS}
            for op in self.ops:
                if op.is_dma:
                    op.sem = dsem[op.sem]
                else:
                    op.sem = esem[op.eng]
                    if op in needed:
                        cnt[op.eng] += 1
                        op.val = cnt[op.eng]
            per = {e: [op for op in self.ops if op.eng == e] for e in ENGS}
            block = st.enter_context(nc.Block())

            def run(engname, eng):
                waited = {}
                for op in per[engname]:
                    for d in op.deps:
                        if self._skip(d, op):
                            continue
                        key = id(d.sem)
                        if waited.get(key, 0) >= d.val:
                            continue
                        eng.wait_ge(d.sem, d.val)
                        waited[key] = d.val
                    if op.fn is None:
                        continue
                    ins = op.fn(eng)
                    if op.is_dma:
                        ins.then_inc(op.sem, 16)
                    elif op in needed:
                        ins.then_inc(op.sem, 1)

            @block.tensor
            def _(e):
                run("pe", e)

            @block.scalar
            def _(e):
                run("act", e)

            @block.vector
            def _(e):
                run("dve", e)

            @block.gpsimd
            def _(e):
                run("pool", e)

            @block.sync
            def _(e):
                run("sp", e)


class Arena:
    def __init__(self, tens, nwords):
        self.t = tens
        self.n = nwords
        self.off = 0

    def alloc(self, free_shape, dtype):
        n = int(np.prod(free_shape))
        words = n if dtype == F32 else (n + 1) // 2
        words = (words + 7) // 8 * 8
        assert self.off + words <= self.n, ("arena overflow", self.off, words, self.n)
        ap = self.t[:, self.off:self.off + words]
        self.off += words
        if dtype != F32:
            ap = ap.bitcast(dtype)
        ap = ap[:, 0:n]
        if len(free_shape) == 2:
            ap = ap.rearrange("p (a b) -> p a b", b=free_shape[1])
        elif len(free_shape) == 3:
            ap = ap.rearrange("p (a b c) -> p a b c", b=free_shape[1], c=free_shape[2])
        return ap


def build_program(S, depth, debug=False, phases=None):
    NT = S // 128
    nc = bass.Bass("TRN2", target_bir_lowering=False)
    okind = "ExternalOutput" if debug else "Internal"

    def din(name, shape, dt=F32):
        return nc.dram_tensor(name, list(shape), dt, kind="ExternalInput").ap()

    def dscr(name, shape, dt=F32):
        return nc.dram_tensor(name, list(shape), dt, kind=okind).ap()

    x_in = din("x", [S, D])
    w_in = din("w_in", [depth, D, NCOLS])
    w_out = din("w_out", [depth, D, D])
    w_up = din("w_up", [depth, D, DFF])
    w_dn = din("w_dn", [depth, DFF, D])
    gpre_d = din("gpre", [depth, 2, 128, 8])
    gpost_d = din("gpost", [depth, 2, D])
    mu_d = din("mu", [depth, NZ])
    rowp_d = din("rowp", [depth, 9 * 512])
    lora_d = din("lora", [depth, 4, 96, 512])
    cmask_d = din("cmask", [128, 2048])
    mbig_d = din("mbig", [128, MBW])
    out_d = nc.dram_tensor("out", [S, D], F32, kind="ExternalOutput").ap()

    qkT_d = dscr("qkT", [D, S], BF16)
    vbf_d = dscr("vbf", [S, DA], BF16)
    zr_d = dscr("zr", [S, NZ])
    vfirst_d = dscr("vfirst", [S, DR])
    atto_d = dscr("atto", [S, DA])
    rwo_d = dscr("rwo", [S, DR], BF16)
    xmid_d = dscr("xmid", [S, D])
    xres_d = dscr("xres", [S, D])

    S_ = Sched(nc)
    add = S_.add
    B = S_.B
    taps = {}

    def tap(name, ap, bufs):
        if not debug or name in taps:
            return
        tdt = nc.dram_tensor("tap_" + name, list(ap.shape), ap.dtype, kind="ExternalOutput").ap()
        taps[name] = tdt
        add("sp", lambda e: e.dma_start(out=tdt, in_=ap), reads=bufs, dma=True)

    with contextlib.ExitStack() as st:
        ARENA_WORDS = 50 * 1024
        arena_t = st.enter_context(nc.sbuf_tensor("arena", [128, ARENA_WORDS], F32))
        A = Arena(arena_t, ARENA_WORDS)
        pbank = [st.enter_context(nc.psum_tensor("pb%d" % i, [128, 512], F32)) for i in range(8)]
        Bbank = [B("bank", i) for i in range(8)]
        for b_ in Bbank:
            b_.excl = True

        def bank_bf(i):
            return pbank[i][:, :].bitcast(BF16)

        cmask = A.alloc((2048,), F32)
        ident_f = cmask[:, 0:128]
        ut_f = cmask[:, 128:256]
        mask4 = cmask[:, 256:768]
        mlow = cmask[:, 768:896]
        ones_f = cmask[:, 896:1024]
        bd16 = cmask[:, 1024:1152]
        lowbd = cmask[:, 1152:1280]
        ident_b = A.alloc((128,), BF16)
        Bc = B("consts")
        add("sp", lambda e: e.dma_start(out=cmask, in_=cmask_d), writes=[Bc], dma=True)
        add("dve", lambda e: e.tensor_copy(ident_b, ident_f), reads=[Bc], writes=[B("identb")])
        ut_b = A.alloc((128,), BF16)
        ones_b = A.alloc((128,), BF16)
        add("dve", lambda e: e.tensor_copy(ut_b, ut_f), reads=[Bc], writes=[B("utb")])
        add("dve", lambda e: e.tensor_copy(ones_b, ones_f), reads=[Bc], writes=[B("onesb")])
        P0 = A.off
        S_.barrier()

        rot = {}

        def nb(lo=0, hi=8):
            i = rot.get((lo, hi), 0)
            rot[(lo, hi)] = i + 1
            return lo + i % (hi - lo)

        def rms_rstd(eng_sq_in, width, stat, Bin, Bstat, tag):
            pass

        def phase1(L, xsrc):
            A.off = P0
            NB = S // 512
            win = A.alloc((8, NCOLS), BF16)
            gpre = A.alloc((8,), F32)
            xt = [A.alloc((D,), F32) for _ in range(2)]
            junk = A.alloc((D,), BF16)
            stt_ = [A.alloc((4,), F32) for _ in range(2)]
            xn = [A.alloc((D,), BF16) for _ in range(2)]
            hT = [A.alloc((8, 512), BF16) for _ in range(2)]
            qks = [A.alloc((512,), BF16) for _ in range(3)]
            zt = [A.alloc((NZ,), F32) for _ in range(2)]
            vt = [A.alloc((512,), BF16) for _ in range(2)]
            for kc in range(8):
                add("pool", lambda e, kc=kc: e.dma_start(out=win[:, kc, :], in_=w_in[L, kc * 128:(kc + 1) * 128, :]),
                    writes=[B("win", kc)], dma=True)
            add("sp", lambda e: e.dma_start(out=gpre, in_=gpre_d[L, 0]), writes=[B("gpre")], dma=True)
            Bwin = [B("win", kc) for kc in range(8)]
            ev = {"i": 0}

            def evac(out, in_, reads, writes):
                ev["i"] += 1
                if ev["i"] % 2:
                    add("act", lambda e: e.copy(out, in_), reads=reads, writes=writes)
                else:
                    add("dve", lambda e: e.tensor_copy(out, in_), reads=reads, writes=writes)

            def p1front(b):
                hs = b % 2
                BhT = B("hT", hs)
                for ts in range(4):
                    t = 4 * b + ts
                    s = t % 2
                    add("sp", lambda e, s=s, t=t: e.dma_start(out=xt[s], in_=xsrc[t * 128:(t + 1) * 128, :]),
                        writes=[B("xt", s)], dma=True)
                    add("act", lambda e, s=s: e.activation(out=junk, in_=xt[s], func=AF.Square, accum_out=stt_[s][:, 0:1]),
                        reads=[B("xt", s)], writes=[B("st0", s)])
                    add("act", lambda e, s=s: e.activation(out=stt_[s][:, 1:2], in_=stt_[s][:, 0:1], func=AF.Sqrt, bias=EPS, scale=1.0 / D),
                        reads=[B("st0", s)], writes=[B("st1", s)])
                    add("dve", lambda e, s=s: e.reciprocal(stt_[s][:, 2:3], stt_[s][:, 1:2]),
                        reads=[B("st1", s)], writes=[B("st2", s)])
                    add("dve", lambda e, s=s: e.tensor_scalar(xn[s], xt[s], stt_[s][:, 2:3], None, ALU.mult),
                        reads=[B("xt", s), B("st2", s)], writes=[B("xn", s)])
                    tb = nb(0, 2)

                    def tr(e, s=s, tb=tb):
                        pv = bank_bf(tb)
                        for kc in range(8):
                            ins = e.transpose(pv[:, kc * 128:(kc + 1) * 128], xn[s][:, kc * 128:(kc + 1) * 128], ident_b)
                        return ins
                    add("pe", tr, reads=[B("xn", s), B("identb")], writes=[Bbank[tb]])
                    add("dve", lambda e, tb=tb, hs=hs, ts=ts: e.tensor_tensor(
                        out=hT[hs][:, :, ts * 128:(ts + 1) * 128],
                        in0=bank_bf(tb).rearrange("p (k t) -> p k t", t=128),
                        in1=gpre.unsqueeze(2).broadcast_to([128, 8, 128]), op=ALU.mult),
                        reads=[Bbank[tb], B("gpre")], writes=[BhT])
            def p1back(b):
                hs = b % 2
                BhT = B("hT", hs)
                for m in range(8):
                    bk = nb(2, 8)

                    def mmf(e, m=m, bk=bk, hs=hs):
                        for kc in range(8):
                            ins = e.matmul(pbank[bk][:, :], win[:, kc, m * 128:(m + 1) * 128], hT[hs][:, kc, :],
                                           start=(kc == 0), stop=(kc == 7))
                        return ins
                    add("pe", mmf, reads=Bwin + [BhT], writes=[Bbank[bk]])
                    qs = (b * 8 + m) % 3
                    evac(qks[qs], pbank[bk][:, :], [Bbank[bk]], [B("qks", qs)])
                    add("sp", lambda e, m=m, b=b, qs=qs: e.dma_start(
                        out=qkT_d[m * 128:(m + 1) * 128, b * 512:(b + 1) * 512], in_=qks[qs]),
                        reads=[B("qks", qs)], dma=True)
                for ts in range(4):
                    t = 4 * b + ts
                    zs = t % 2
                    for (c0, c1) in ((1024, 1536), (1536, 2048), (2048, 2560), (2560, 3072), (3072, 3264)):
                        bk = nb(2, 8)
                        w = c1 - c0

                        def mmt(e, bk=bk, hs=hs, ts=ts, c0=c0, c1=c1, w=w):
                            for kc in range(8):
                                ins = e.matmul(pbank[bk][:, 0:w], hT[hs][:, kc, ts * 128:(ts + 1) * 128], win[:, kc, c0:c1],
                                               start=(kc == 0), stop=(kc == 7))
                            return ins
                        add("pe", mmt, reads=Bwin + [BhT], writes=[Bbank[bk]])
                        if c0 == 1024:
                            evac(vt[zs], pbank[bk][:, 0:512], [Bbank[bk]], [B("vt", zs)])
                        else:
                            evac(zt[zs][:, c0 - 1536:c1 - 1536], pbank[bk][:, 0:w], [Bbank[bk]], [B("zt", zs)])
                    add("sp", lambda e, t=t, zs=zs: e.dma_start(out=vbf_d[t * 128:(t + 1) * 128, :], in_=vt[zs]),
                        reads=[B("vt", zs)], dma=True)
                    add("sp", lambda e, t=t, zs=zs: e.dma_start(out=zr_d[t * 128:(t + 1) * 128, :], in_=zt[zs]),
                        reads=[B("zt", zs)], dma=True)
            p1front(0)
            for b in range(NB):
                if b + 1 < NB:
                    p1front(b + 1)
                p1back(b)
            S_.barrier()

        def phase2(L):
            A.off = P0
            mu = A.alloc((NZ,), F32)
            rowp = A.alloc((9, 512), F32)
            lora = A.alloc((4, 512), BF16)
            ST = [A.alloc((NH, HD), F32) for _ in range(2)]
            STb = [A.alloc((NH, HD), BF16) for _ in range(2)]
            cur = [A.alloc((NZ,), F32) for _ in range(2)]
            prv = [A.alloc((NZ,), F32) for _ in range(2)]
            vf = [A.alloc((512,), F32) for _ in range(2)]
            lo = A.alloc((192,), BF16)
            loT = A.alloc((4, 128), BF16)
            tmpa = A.alloc((512,), F32)
            tmpb = A.alloc((512,), F32)
            sg = A.alloc((512,), F32)
            sgh = A.alloc((512,), BF16)
            sgl = A.alloc((512,), BF16)
            cs = A.alloc((512,), F32)
            g_in = A.alloc((512,), F32)
            g_inv = A.alloc((512,), F32)
            g_prev = A.alloc((512,), F32)
            g_end = A.alloc((512,), F32)
            gCTs = [A.alloc((NH,), F32) for _ in range(2)]
            a_sb = A.alloc((512,), F32)
            gate = [A.alloc((512,), F32) for _ in range(2)]
            v2 = [A.alloc((512,), F32) for _ in range(2)]
            kk = A.alloc((512,), F32)
            kap = A.alloc((512,), F32)
            kmod = A.alloc((512,), F32)
            b_sb = A.alloc((512,), F32)
            sm = A.alloc((64,), F32)
            smt = A.alloc((64,), F32)
            rkc = [A.alloc((NH,), F32) for _ in range(2)]
            tm = [A.alloc((7, 512), BF16) for _ in range(2)]
            XT = [A.alloc((NH, 4, 128), BF16) for _ in range(2)]
            NG = 8
            AT = [A.alloc((512,), BF16) for _ in range(NG)]
            Q = [[A.alloc((4, 128), BF16) for _ in range(2)] for _ in range(NG)]
            ZT = [A.alloc((2, 128), BF16) for _ in range(NG)]
            GG = [A.alloc((256,), BF16) for _ in range(NG)]
            Lm = [A.alloc((128,), BF16) for _ in range(NG)]
            tmpm = [A.alloc((256,), F32) for _ in range(NG)]
            nW1 = [A.alloc((HD,), BF16) for _ in range(NG)]
            KU = [A.alloc((128,), BF16) for _ in range(NG)]
            Mc = [A.alloc((HD,), BF16) for _ in range(NG)]
            RhT = [A.alloc((128,), BF16) for _ in range(NG)]
            ysb = A.alloc((512,), F32)
            ysq = A.alloc((512,), F32)
            rwo = [A.alloc((512,), BF16) for _ in range(2)]

            add("sp", lambda e: e.dma_start(out=mu, in_=mu_d[L:L + 1, :].partition_broadcast(128)), writes=[B("mu")], dma=True)
            add("sp", lambda e: e.dma_start(out=rowp.rearrange("p a b -> p (a b)"), in_=rowp_d[L:L + 1, :].partition_broadcast(128)),
                writes=[B("rowp")], dma=True)
            for q in range(4):
                add("pool", lambda e, q=q: e.dma_start(out=lora[0:96, q, :], in_=lora_d[L, q]), writes=[B("lora", q)], dma=True)
            add("pool", lambda e: e.memset(ST[0].rearrange("p a b -> p (a b)"), 0.0), writes=[B("ST", 0)])
            add("pool", lambda e: e.memset(STb[0].rearrange("p a b -> p (a b)"), 0.0), writes=[B("STb", 0)])
            add("pool", lambda e: e.memset(STb[1].rearrange("p a b -> p (a b)"), 0.0), writes=[B("STb", 1)])
            for s_ in range(2):
                add("pool", lambda e, s_=s_: e.memset(XT[s_].rearrange("p a b c -> p (a b c)"), 0.0), writes=[B("XT", s_)])
            for i_ in range(NG):
                add("pool", lambda e, i_=i_: e.memset(Mc[i_], 0.0), writes=[B("Mc", i_)])
                add("pool", lambda e, i_=i_: e.memset(RhT[i_], 0.0), writes=[B("RhT", i_)])
            w0_bc, a0_bc, mv0_bc, kk_bc, ka_bc, rk_bc, gnw_bc, gnb_bc = [rowp[:, i, :] for i in range(8)]
            Brow = B("rowp")

            h3 = lambda ap: ap.rearrange("p (h j) -> p h j", j=HD)

            LV = 1 if os.environ.get('P2FORCE') else L
            def pro(c, padd):
                s = c % 2
                Bcur, Bprv = B("cur", s), B("prv", s)
                gCT = gCTs[s]
                padd("sp", lambda e, s=s, c=c: e.dma_start(out=cur[s], in_=zr_d[c * 128:(c + 1) * 128, :]), writes=[Bcur], dma=True)
                if c == 0:
                    padd("pool", lambda e, s=s: e.memset(prv[s][0:1, :], 0.0), writes=[Bprv])
                    padd("sp", lambda e, s=s: e.dma_start(out=prv[s][1:128, :], in_=zr_d[0:127, :]), writes=[Bprv], dma=True)
                else:
                    padd("sp", lambda e, s=s, c=c: e.dma_start(out=prv[s], in_=zr_d[c * 128 - 1:c * 128 + 127, :]), writes=[Bprv], dma=True)
                if LV > 0:
                    padd("sp", lambda e, s=s, c=c: e.dma_start(out=vf[s], in_=vfirst_d[c * 128:(c + 1) * 128, :]), writes=[B("vf", s)], dma=True)
                padd("pool", lambda e, s=s: e.tensor_tensor(out=prv[s], in0=prv[s], in1=cur[s], op=ALU.subtract), reads=[Bcur, Bprv], writes=[Bprv])
                padd("dve", lambda e, s=s: e.tensor_tensor(out=prv[s], in0=prv[s], in1=mu, op=ALU.mult), reads=[Bprv, B("mu")], writes=[Bprv])
                padd("dve", lambda e, s=s: e.tensor_tensor(out=prv[s], in0=prv[s], in1=cur[s], op=ALU.add), reads=[Bprv, Bcur], writes=[Bprv])
                zs = prv[s]
                r_, k_, v_ = zs[:, 0:512], zs[:, 512:1024], zs[:, 1024:1536]
                Blo = B("lo")
                padd("act", lambda e, zs=zs: e.activation(out=lo[:, 0:32], in_=zs[:, 1536:1568], func=AF.Tanh), reads=[Bprv], writes=[Blo])
                padd("act", lambda e, zs=zs: e.activation(out=lo[:, 64:160], in_=zs[:, 1600:1696], func=AF.Sigmoid), reads=[Bprv], writes=[Blo])
                padd("dve", lambda e, zs=zs: e.tensor_copy(lo[:, 32:64], zs[:, 1568:1600]), reads=[Bprv], writes=[Blo])
                padd("dve", lambda e, zs=zs: e.tensor_copy(lo[:, 160:192], zs[:, 1696:1728]), reads=[Bprv], writes=[Blo])
                bk = nb(0, 7)

                def trlo(e, bk=bk):
                    pv = bank_bf(bk)
                    e.transpose(pv[0:32, 0:128], lo[:, 0:32], ident_b)
                    e.transpose(pv[0:32, 128:256], lo[:, 32:64], ident_b)
                    e.transpose(pv[0:96, 256:384], lo[:, 64:160], ident_b)
                    return e.transpose(pv[0:32, 384:512], lo[:, 160:192], ident_b)
                padd("pe", trlo, reads=[Blo, B("identb")], writes=[Bbank[bk]])
                BloT = B("loT")
                for (q, r0) in ((0, 32), (1, 32), (2, 96), (3, 32)):
                    padd("act", lambda e, bk=bk, q=q, r0=r0: e.copy(loT[0:r0, q, :], bank_bf(bk)[0:r0, q * 128:(q + 1) * 128]),
                        reads=[Bbank[bk]], writes=[BloT])
                bk = nb(0, 7)
                padd("pe", lambda e, bk=bk: e.matmul(pbank[bk][:, :], loT[0:32, 0, :], lora[0:32, 0, :], start=True, stop=True),
                    reads=[BloT, B("lora", 0)], writes=[Bbank[bk]])
                padd("dve", lambda e, bk=bk: e.tensor_tensor(out=tmpa, in0=pbank[bk][:, :], in1=w0_bc, op=ALU.add),
                    reads=[Bbank[bk], Brow], writes=[B("tmpa")])
                padd("act", lambda e: e.activation(out=sg, in_=tmpa, func=AF.Sigmoid), reads=[B("tmpa")], writes=[B("sg")])
                padd("act", lambda e: e.copy(sgh, sg), reads=[B("sg")], writes=[B("sgh")])
                padd("dve", lambda e: e.tensor_tensor(out=sgl, in0=sg, in1=sgh, op=ALU.subtract), reads=[B("sg"), B("sgh")], writes=[B("sgl")])
                bk = nb(0, 7)
                padd("pe", lambda e, bk=bk: e.matmul(pbank[bk][:, :], loT[0:32, 1, :], lora[0:32, 1, :], start=True, stop=True),
                    reads=[BloT, B("lora", 1)], writes=[Bbank[bk]])
                padd("dve", lambda e, bk=bk: e.tensor_tensor(out=tmpb, in0=pbank[bk][:, :], in1=a0_bc, op=ALU.add),
                    reads=[Bbank[bk], Brow], writes=[B("tmpb")])
                padd("act", lambda e: e.activation(out=a_sb, in_=tmpb, func=AF.Sigmoid), reads=[B("tmpb")], writes=[B("a")])
                bk = nb(0, 7)
                padd("pe", lambda e, bk=bk: e.matmul(pbank[bk][:, :], loT[0:96, 2, :], lora[0:96, 3, :], start=True, stop=True),
                    reads=[BloT, B("lora", 3)], writes=[Bbank[bk]])
                padd("act", lambda e, bk=bk, s=s: e.copy(gate[s], pbank[bk][:, :]), reads=[Bbank[bk]], writes=[B("gate", s)])
                Bv2 = B("v2", s)
                if LV == 0:
                    padd("pool", lambda e, s=s, v_=v_: e.tensor_copy(v2[s], v_), reads=[Bprv], writes=[Bv2])
                    padd("sp", lambda e, s=s, c=c: e.dma_start(out=vfirst_d[c * 128:(c + 1) * 128, :], in_=v2[s]), reads=[Bv2], dma=True)
                else:
                    bk = nb(0, 7)
                    padd("pe", lambda e, bk=bk: e.matmul(pbank[bk][:, :], loT[0:32, 3, :], lora[0:32, 2, :], start=True, stop=True),
                        reads=[BloT, B("lora", 2)], writes=[Bbank[bk]])
                    padd("dve", lambda e, bk=bk: e.tensor_tensor(out=tmpb, in0=pbank[bk][:, :], in1=mv0_bc, op=ALU.add),
                        reads=[Bbank[bk], Brow], writes=[B("tmpb")])
                    padd("act", lambda e: e.activation(out=tmpb, in_=tmpb, func=AF.Sigmoid), reads=[B("tmpb")], writes=[B("tmpb")])
                    padd("pool", lambda e, s=s, v_=v_: e.tensor_tensor(out=v2[s], in0=vf[s], in1=v_, op=ALU.subtract),
                        reads=[B("vf", s), Bprv], writes=[Bv2])
                    padd("dve", lambda e, s=s: e.tensor_tensor(out=v2[s], in0=v2[s], in1=tmpb, op=ALU.mult), reads=[Bv2, B("tmpb")], writes=[Bv2])
                    padd("dve", lambda e, s=s, v_=v_: e.tensor_tensor(out=v2[s], in0=v2[s], in1=v_, op=ALU.add), reads=[Bv2, Bprv], writes=[Bv2])
                bk_cs = nb(0, 7)
                def mmcs(e, bk=bk_cs):
                    e.matmul(pbank[bk][:, :], ut_b, sgh, start=True, stop=False)
                    return e.matmul(pbank[bk][:, :], ut_b, sgl, start=False, stop=True)
                padd("pe", mmcs, reads=[B("sgh"), B("sgl"), B("utb")], writes=[Bbank[bk_cs]])
                padd("act", lambda e, bk=bk_cs: e.copy(cs, pbank[bk][:, :]), reads=[Bbank[bk]], writes=[B("cs")])
                bk_tot = nb(0, 7)
                def mmtot(e, bk=bk_tot):
                    e.matmul(pbank[bk][:, :], ones_b, sgh, start=True, stop=False)
                    return e.matmul(pbank[bk][:, :], ones_b, sgl, start=False, stop=True)
                padd("pe", mmtot, reads=[B("sgh"), B("sgl"), B("onesb")], writes=[Bbank[bk_tot]])
                padd("act", lambda e: e.activation(out=g_in, in_=cs, func=AF.Exp, scale=-C0), reads=[B("cs")], writes=[B("g_in")])
                padd("act", lambda e: e.activation(out=g_inv, in_=cs, func=AF.Exp, scale=C0), reads=[B("cs")], writes=[B("g_inv")])
                padd("pool", lambda e: e.tensor_tensor(out=tmpa, in0=cs, in1=sg, op=ALU.subtract), reads=[B("cs"), B("sg")], writes=[B("tmpa")])
                padd("act", lambda e: e.activation(out=g_prev, in_=tmpa, func=AF.Exp, scale=-C0), reads=[B("tmpa")], writes=[B("g_prev")])
                padd("dve", lambda e, bk=bk_tot: e.tensor_tensor(out=tmpb, in0=pbank[bk][:, :], in1=cs, op=ALU.subtract),
                    reads=[Bbank[bk], B("cs")], writes=[B("tmpb")])
                padd("act", lambda e: e.activation(out=g_end, in_=tmpb, func=AF.Exp, scale=-C0), reads=[B("tmpb")], writes=[B("g_end")])
                bk = nb(0, 7)

                def mmgc(e, bk=bk):
                    for h in range(NH):
                        e.matmul(pbank[bk][0:64, h:h + 1], sgh[:, h * 64:(h + 1) * 64], ones_b[:, 0:1], start=True, stop=False)
                        ins = e.matmul(pbank[bk][0:64, h:h + 1], sgl[:, h * 64:(h + 1) * 64], ones_b[:, 0:1], start=False, stop=True)
                    return ins
                padd("pe", mmgc, reads=[B("sgh"), B("sgl"), B("onesb")], writes=[Bbank[bk]])
                padd("act", lambda e, bk=bk: e.activation(out=gCT[0:64, :], in_=pbank[bk][0:64, 0:NH], func=AF.Exp, scale=-C0),
                    reads=[Bbank[bk]], writes=[B("gCT", s)])
                Bsm = B("sm")
                padd("dve", lambda e, k_=k_: e.tensor_tensor(out=kk, in0=k_, in1=kk_bc, op=ALU.mult), reads=[Bprv, Brow], writes=[B("kk")])
                padd("pool", lambda e: e.tensor_tensor(out=tmpa, in0=kk, in1=kk, op=ALU.mult), reads=[B("kk")], writes=[B("tmpa")])
                padd("dve", lambda e: e.tensor_reduce(out=sm[:, 0:8], in_=h3(tmpa), axis=AX.X, op=ALU.add), reads=[B("tmpa")], writes=[Bsm])
                padd("act", lambda e: e.activation(out=sm[:, 8:16], in_=sm[:, 0:8], func=AF.Sqrt), reads=[Bsm], writes=[Bsm])
                padd("dve", lambda e: e.tensor_scalar(sm[:, 8:16], sm[:, 8:16], 1e-12, None, ALU.max), reads=[Bsm], writes=[Bsm])
                padd("dve", lambda e: e.reciprocal(sm[:, 16:24], sm[:, 8:16]), reads=[Bsm], writes=[Bsm])
                padd("dve", lambda e: e.tensor_tensor(out=h3(kap), in0=h3(kk), in1=sm[:, 16:24].unsqueeze(2).broadcast_to([128, NH, HD]), op=ALU.mult),
                    reads=[B("kk"), Bsm], writes=[B("kap")])
                padd("dve", lambda e: e.scalar_tensor_tensor(out=kmod, in0=a_sb, scalar=-1.0, in1=ka_bc, op0=ALU.add, op1=ALU.mult),
                    reads=[B("a"), Brow], writes=[B("kmod")])
                padd("dve", lambda e, k_=k_: e.scalar_tensor_tensor(out=kmod, in0=kmod, scalar=1.0, in1=k_, op0=ALU.add, op1=ALU.mult),
                    reads=[B("kmod"), Bprv], writes=[B("kmod")])
                padd("pool", lambda e: e.tensor_tensor(out=b_sb, in0=kap, in1=a_sb, op=ALU.mult), reads=[B("kap"), B("a")], writes=[B("b")])
                padd("pool", lambda e, r_=r_: e.tensor_tensor(out=tmpa, in0=r_, in1=kmod, op=ALU.mult), reads=[Bprv, B("kmod")], writes=[B("tmpa")])
                padd("dve", lambda e: e.tensor_tensor(out=tmpa, in0=tmpa, in1=rk_bc, op=ALU.mult), reads=[B("tmpa"), Brow], writes=[B("tmpa")])
                padd("dve", lambda e, s=s: e.tensor_reduce(out=rkc[s], in_=h3(tmpa), axis=AX.X, op=ALU.add), reads=[B("tmpa")], writes=[B("rkc", s)])
                Btm = B("tm", s)
                T = tm[s]
                padd("dve", lambda e, T=T, r_=r_: e.tensor_tensor(out=T[:, 0, :], in0=r_, in1=g_in, op=ALU.mult), reads=[Bprv, B("g_in")], writes=[B("tm0", s)])
                padd("pool", lambda e, T=T: e.tensor_tensor(out=T[:, 1, :], in0=kap, in1=g_prev, op=ALU.mult), reads=[B("kap"), B("g_prev")], writes=[B("tm1", s)])
                padd("dve", lambda e, T=T: e.tensor_tensor(out=T[:, 2, :], in0=b_sb, in1=g_inv, op=ALU.mult), reads=[B("b"), B("g_inv")], writes=[B("tm2", s)])
                padd("pool", lambda e, T=T: e.tensor_tensor(out=T[:, 3, :], in0=kmod, in1=g_inv, op=ALU.mult), reads=[B("kmod"), B("g_inv")], writes=[B("tm3", s)])
                padd("dve", lambda e, T=T: e.tensor_tensor(out=T[:, 4, :], in0=b_sb, in1=g_end, op=ALU.mult), reads=[B("b"), B("g_end")], writes=[B("tm4", s)])
                padd("pool", lambda e, T=T: e.tensor_tensor(out=T[:, 5, :], in0=kmod, in1=g_end, op=ALU.mult), reads=[B("kmod"), B("g_end")], writes=[B("tm5", s)])
                padd("act", lambda e, T=T, s=s: e.copy(T[:, 6, :], v2[s]), reads=[Bv2], writes=[B("tm6", s)])
                BXT = B("XT", s)
                for (kind, src) in ((0, 1), (1, 0), (2, 2), (3, 3)):
                    bk = nb(0, 7)

                    def trx(e, bk=bk, T=T, src=src):
                        pv = bank_bf(bk)
                        for h in range(NH):
                            ins = e.transpose(pv[0:64, h * 128:(h + 1) * 128], T[:, src, h * 64:(h + 1) * 64], ident_b)
                        return ins
                    padd("pe", trx, reads=[B("tm%d" % src, s), B("identb")], writes=[Bbank[bk]])
                    eng = "act" if kind % 2 else "dve"
                    if eng == "act":
                        padd("act", lambda e, bk=bk, kind=kind, s=s: e.copy(XT[s][0:64, :, kind, :], bank_bf(bk)[0:64, :].rearrange("p (h t) -> p h t", t=128)),
                            reads=[Bbank[bk]], writes=[BXT])
                    else:
                        padd("dve", lambda e, bk=bk, kind=kind, s=s: e.tensor_copy(XT[s][0:64, :, kind, :], bank_bf(bk)[0:64, :].rearrange("p (h t) -> p h t", t=128)),
                            reads=[Bbank[bk]], writes=[BXT])
            def post(c, pull):
                s = c % 2
                T = tm[s]
                X = XT[s]
                BXT = B("XT", s)
                Bv2 = B("v2", s)
                gCT = gCTs[s]
                BgCT = B("gCT", s)
                STo, STn = ST[c % 2], ST[(c + 1) % 2]
                BSTo, BSTn = B("ST", c % 2), B("ST", (c + 1) % 2)
                STbo, STbn = STb[c % 2], STb[(c + 1) % 2]
                BSTbo, BSTbn = B("STb", c % 2), B("STb", (c + 1) % 2)
                bkY = 7
                PK = int(os.environ.get('P2PK', '0'))
                heads = list(range(NH))
                for i in heads:
                    h = i
                    bk = nb(0, 7)

                    def mma(e, bk=bk, h=h):
                        e.matmul(pbank[bk][:, 0:256], X[:, h, 2, :], X[:, h, 0:2, :], start=True, stop=True)
                        return e.matmul(pbank[bk][:, 256:512], X[:, h, 3, :], X[:, h, 0:2, :], start=True, stop=True)
                    add("pe", mma, reads=[BXT], writes=[Bbank[bk]])
                    add("dve", lambda e, bk=bk, i=i: e.tensor_tensor(out=AT[i], in0=pbank[bk][:, :], in1=mask4, op=ALU.mult),
                        reads=[Bbank[bk], Bc], writes=[B("AT", i)])
                    bk2 = nb(0, 7)
                    add("pe", lambda e, bk2=bk2, h=h: e.matmul(pbank[bk2][:, 0:128], X[:, h, 0, :], X[:, h, 2, :], start=True, stop=True),
                        reads=[BXT], writes=[Bbank[bk2]])
                    add("dve", lambda e, bk2=bk2, i=i: e.tensor_tensor(out=Q[i][0][:, 2, :], in0=pbank[bk2][:, 0:128], in1=lowbd, op=ALU.mult),
                        reads=[Bbank[bk2], Bc], writes=[B("Qw", i, 0)])
                    add("act", lambda e, bk2=bk2, i=i: e.copy(tmpm[i][:, 0:128], pbank[bk2][:, 0:128]), reads=[Bbank[bk2]], writes=[B("tmpm", i)])
                    add("pool", lambda e, i=i: e.tensor_tensor(out=Lm[i], in0=tmpm[i][:, 0:128], in1=mlow, op=ALU.mult),
                        reads=[B("tmpm", i), Bc], writes=[B("Lm", i)])
                    add("pool", lambda e, i=i: e.tensor_tensor(out=Q[i][0][:, 0, :], in0=AT[i][:, 0:128], in1=bd16, op=ALU.mult),
                        reads=[B("AT", i), Bc], writes=[B("Qy", i, 0)])
                    add("pool", lambda e, i=i: e.tensor_tensor(out=Q[i][1][:, 1:4:2, :], in0=Q[i][0][:, 0:3:2, :], in1=ident_f.unsqueeze(1).broadcast_to([128, 2, 128]), op=ALU.add),
                        reads=[B("Qy", i, 0), B("Qw", i, 0), Bc], writes=[B("Qzt", i, 1)])
                pull(PK)
                for lev in range(0, 4):
                    p_, n_ = lev % 2, (lev + 1) % 2
                    for i in heads:
                        bk = nb(0, 7)
                        Qp, Qn = Q[i][p_], Q[i][n_]
                        if lev == 0:
                            def mml(e, bk=bk, Qp=Qp):
                                e.matmul(pbank[bk][:, 0:128], Qp[:, 2, :], Qp[:, 0, :], start=True, stop=True)
                                return e.matmul(pbank[bk][:, 128:256], Qp[:, 0, :], Qp[:, 2, :], start=True, stop=True)
                            add("pe", mml, reads=[B("Qy", i, p_), B("Qw", i, p_)], writes=[Bbank[bk]])
                            add("act", lambda e, bk=bk, Qn=Qn: e.copy(Qn[:, 0:3:2, :], pbank[bk][:, 0:256].rearrange("p (a b) -> p a b", b=128)),
                                reads=[Bbank[bk]], writes=[B("Qy", i, n_), B("Qw", i, n_)])
                        elif lev < 3:
                            def mml(e, bk=bk, Qp=Qp):
                                e.matmul(pbank[bk][:, 0:256], Qp[:, 2, :], Qp[:, 0:2, :], start=True, stop=True)
                                return e.matmul(pbank[bk][:, 256:512], Qp[:, 0, :], Qp[:, 2:4, :], start=True, stop=True)
                            add("pe", mml, reads=[B("Qy", i, p_), B("Qw", i, p_), B("Qzt", i, p_)], writes=[Bbank[bk]])
                            pv4 = pbank[bk][:, :].rearrange("p (a b) -> p a b", b=128)
                            add("act", lambda e, pv4=pv4, Qn=Qn: e.copy(Qn[:, 0:3:2, :], pv4[:, 0:3:2, :]),
                                reads=[Bbank[bk]], writes=[B("Qy", i, n_), B("Qw", i, n_)])
                            add("dve", lambda e, pv4=pv4, Qn=Qn, Qp=Qp: e.tensor_tensor(out=Qn[:, 1:4:2, :], in0=pv4[:, 1:4:2, :], in1=Qp[:, 1:4:2, :], op=ALU.add),
                                reads=[Bbank[bk], B("Qzt", i, p_)], writes=[B("Qzt", i, n_)])
                        else:
                            def mml(e, bk=bk, Qp=Qp):
                                e.matmul(pbank[bk][:, 0:128], Qp[:, 2, :], Qp[:, 1, :], start=True, stop=True)
                                return e.matmul(pbank[bk][:, 128:256], Qp[:, 0, :], Qp[:, 3, :], start=True, stop=True)
                            add("pe", mml, reads=[B("Qy", i, p_), B("Qw", i, p_), B("Qzt", i, p_)], writes=[Bbank[bk]])
                            add("dve", lambda e, bk=bk, i=i, Qp=Qp: e.tensor_tensor(out=ZT[i], in0=pbank[bk][:, 0:256].rearrange("p (a b) -> p a b", b=128),
                                                                                  in1=Qp[:, 1:4:2, :], op=ALU.add),
                                reads=[Bbank[bk], B("Qzt", i, p_)], writes=[B("ZT", i)])
                    pull(PK)
                for mi in range(3):
                    MMm = cmask[:, 1280 + 256 * mi:1536 + 256 * mi]
                    bks = {}
                    for i in heads:
                        bk = nb(0, 7)

                        def mmg(e, bk=bk, i=i):
                            e.matmul(pbank[bk][:, 0:128], Lm[i], ZT[i][:, 0, :], start=True, stop=True)
                            return e.matmul(pbank[bk][:, 128:256], AT[i][:, 0:128], ZT[i][:, 1, :], start=True, stop=True)
                        add("pe", mmg, reads=[B("Lm", i), B("AT", i), B("ZT", i)], writes=[Bbank[bk]])
                        add("act", lambda e, bk=bk, i=i: e.copy(GG[i], pbank[bk][:, 0:256]), reads=[Bbank[bk]], writes=[B("GG", i)])
                    pull(PK // 2)
                    for i in heads:
                        bk = nb(0, 7)

                        def mmh(e, bk=bk, i=i):
                            e.matmul(pbank[bk][:, 0:128], ZT[i][:, 1, :], GG[i][:, 0:128], start=True, stop=True)
                            return e.matmul(pbank[bk][:, 128:256], ZT[i][:, 0, :], GG[i][:, 128:256], start=True, stop=True)
                        add("pe", mmh, reads=[B("GG", i), B("ZT", i)], writes=[Bbank[bk]])
                        add("dve", lambda e, bk=bk, i=i, MMm=MMm: e.tensor_tensor(out=tmpm[i], in0=pbank[bk][:, 0:256], in1=MMm, op=ALU.mult),
                            reads=[Bbank[bk], Bc], writes=[B("tmpm", i)])
                        add("pool", lambda e, i=i: e.tensor_tensor(out=ZT[i].rearrange("p a b -> p (a b)"), in0=ZT[i].rearrange("p a b -> p (a b)"), in1=tmpm[i], op=ALU.subtract),
                            reads=[B("ZT", i), B("tmpm", i)], writes=[B("ZT", i)])
                    pull(PK // 2)
                for i in heads:
                    h = i
                    vb_h = T[:, 6, h * 64:(h + 1) * 64]
                    bk = nb(0, 7)
                    add("pe", lambda e, bk=bk, i=i, vb_h=vb_h: e.matmul(pbank[bk][:, 0:64], AT[i][:, 256:384], vb_h, start=True, stop=True),
                        reads=[B("AT", i), B("tm6", s)], writes=[Bbank[bk]])
                    add("act", lambda e, bk=bk, i=i: e.mul(nW1[i], pbank[bk][:, 0:64], -1.0), reads=[Bbank[bk]], writes=[B("nW1", i)])
                pull(PK)
                for i in heads:
                    h = i
                    Zf = ZT[i][:, 0, :]
                    bk = nb(0, 7)

                    def mmku(e, bk=bk, i=i, h=h, Zf=Zf):
                        e.matmul(pbank[bk][:, 0:64], Zf, T[:, 1, h * 64:(h + 1) * 64], start=True, stop=True)
                        return e.matmul(pbank[bk][:, 64:128], Zf, nW1[i], start=True, stop=True)
                    add("pe", mmku, reads=[B("ZT", i), B("tm1", s), B("nW1", i)], writes=[Bbank[bk]])
                    add("dve", lambda e, bk=bk, i=i: e.tensor_copy(KU[i], pbank[bk][:, 0:128]), reads=[Bbank[bk]], writes=[B("KU", i)])
                pull(PK)
                for i in heads:
                    h = i
                    bk = nb(0, 7)

                    def mmmr(e, bk=bk, i=i, h=h):
                        e.matmul(pbank[bk][0:64, 0:64], KU[i][:, 0:64], T[:, 4, h * 64:(h + 1) * 64], start=True, stop=True)
                        return e.matmul(pbank[bk][0:64, 64:192], KU[i][:, 0:64], AT[i][:, 128:256], start=True, stop=True)
                    add("pe", mmmr, reads=[B("KU", i), B("tm4", s), B("AT", i)], writes=[Bbank[bk]])
                    add("act", lambda e, bk=bk, i=i: e.mul(Mc[i][0:64, :], pbank[bk][0:64, 0:64], -1.0), reads=[Bbank[bk]], writes=[B("Mc", i)])
                    add("dve", lambda e, bk=bk, i=i, h=h: e.tensor_tensor(out=RhT[i][0:64, :], in0=X[0:64, h, 1, :], in1=pbank[bk][0:64, 64:192], op=ALU.subtract),
                        reads=[Bbank[bk], BXT], writes=[B("RhT", i)])
                pull(PK)
                for i in heads:
                    h = i
                    vb_h = T[:, 6, h * 64:(h + 1) * 64]

                    def mmy(e, i=i, h=h, vb_h=vb_h):
                        o = pbank[bkY][:, h * 64:(h + 1) * 64]
                        e.matmul(o, RhT[i][:, :], STbo[:, h, :], start=True, stop=False)
                        e.matmul(o, AT[i][:, 128:256], KU[i][:, 64:128], start=False, stop=False)
                        return e.matmul(o, AT[i][:, 384:512], vb_h, start=False, stop=True)
                    add("pe", mmy, reads=[B("RhT", i), BSTbo, B("AT", i), B("KU", i), B("tm6", s)], writes=[Bbank[bkY]])
                    bk = nb(0, 7)

                    def mms(e, bk=bk, i=i, h=h, vb_h=vb_h):
                        o = pbank[bk][0:64, 0:64]
                        e.matmul(o, T[:, 4, h * 64:(h + 1) * 64], KU[i][:, 64:128], start=True, stop=False)
                        e.matmul(o, T[:, 5, h * 64:(h + 1) * 64], vb_h, start=False, stop=False)
                        return e.matmul(o, Mc[i][:, :], STbo[:, h, :], start=False, stop=True)
                    add("pe", mms, reads=[B("tm4", s), B("tm5", s), B("tm6", s), B("KU", i), B("Mc", i), BSTbo], writes=[Bbank[bk]])
                    add("dve", lambda e, bk=bk, h=h: e.scalar_tensor_tensor(out=STn[0:64, h, :], in0=STo[0:64, h, :], scalar=gCT[0:64, h:h + 1],
                                                                          in1=pbank[bk][0:64, 0:64], op0=ALU.mult, op1=ALU.add),
                        reads=[Bbank[bk], BSTo, BgCT], writes=[B("STh", (c + 1) % 2, h)])
                    add("act", lambda e, h=h: e.copy(STbn[0:64, h, :], STn[0:64, h, :]), reads=[B("STh", (c + 1) % 2, h)], writes=[BSTbn, BSTn])
                pull(PK)
                Bsm = B("smt")
                sm = smt
                add("act", lambda e: e.copy(ysb, pbank[bkY][:, :]), reads=[Bbank[bkY]], writes=[B("ysb")])
                add("pool", lambda e: e.tensor_tensor(out=ysq, in0=ysb, in1=ysb, op=ALU.mult), reads=[B("ysb")], writes=[B("ysq")])
                add("dve", lambda e: e.tensor_reduce(out=sm[:, 24:32], in_=h3(ysb), axis=AX.X, op=ALU.add), reads=[B("ysb")], writes=[Bsm])
                add("dve", lambda e: e.tensor_reduce(out=sm[:, 32:40], in_=h3(ysq), axis=AX.X, op=ALU.add), reads=[B("ysq")], writes=[Bsm])
                add("dve", lambda e: e.tensor_scalar(sm[:, 24:32], sm[:, 24:32], 1.0 / HD, None, ALU.mult), reads=[Bsm], writes=[Bsm])
                add("dve", lambda e: e.tensor_tensor(out=sm[:, 40:48], in0=sm[:, 24:32], in1=sm[:, 24:32], op=ALU.mult), reads=[Bsm], writes=[Bsm])
                add("dve", lambda e: e.scalar_tensor_tensor(out=sm[:, 32:40], in0=sm[:, 32:40], scalar=1.0 / HD, in1=sm[:, 40:48], op0=ALU.mult, op1=ALU.subtract),
                    reads=[Bsm], writes=[Bsm])
                add("act", lambda e: e.activation(out=sm[:, 40:48], in_=sm[:, 32:40], func=AF.Sqrt, bias=GN_EPS, scale=1.0), reads=[Bsm], writes=[Bsm])
                add("dve", lambda e: e.reciprocal(sm[:, 48:56], sm[:, 40:48]), reads=[Bsm], writes=[Bsm])
                add("dve", lambda e: e.tensor_tensor(out=h3(ysb), in0=h3(ysb), in1=sm[:, 24:32].unsqueeze(2).broadcast_to([128, NH, HD]), op=ALU.subtract),
                    reads=[B("ysb"), Bsm], writes=[B("ysb")])
                add("dve", lambda e: e.tensor_tensor(out=h3(ysb), in0=h3(ysb), in1=sm[:, 48:56].unsqueeze(2).broadcast_to([128, NH, HD]), op=ALU.mult),
                    reads=[B("ysb"), Bsm], writes=[B("ysb")])
                add("pool", lambda e: e.tensor_tensor(out=ysb, in0=ysb, in1=gnw_bc, op=ALU.mult), reads=[B("ysb"), Brow], writes=[B("ysb")])
                add("pool", lambda e: e.tensor_tensor(out=ysb, in0=ysb, in1=gnb_bc, op=ALU.add), reads=[B("ysb"), Brow], writes=[B("ysb")])
                add("dve", lambda e: e.tensor_tensor(out=h3(ysq), in0=h3(v2[s]), in1=rkc[s].unsqueeze(2).broadcast_to([128, NH, HD]), op=ALU.mult),
                    reads=[Bv2, B("rkc", s)], writes=[B("ysq")])
                add("pool", lambda e: e.tensor_tensor(out=ysb, in0=ysb, in1=ysq, op=ALU.add), reads=[B("ysb"), B("ysq")], writes=[B("ysb")])
                add("dve", lambda e: e.tensor_tensor(out=rwo[s], in0=ysb, in1=gate[s], op=ALU.mult), reads=[B("ysb"), B("gate", s)], writes=[B("rwo", s)])
                add("sp", lambda e: e.dma_start(out=rwo_d[c * 128:(c + 1) * 128, :], in_=rwo[s]), reads=[B("rwo", s)], dma=True)
            plist = []

            def padd(*a, **k):
                plist.append((a, k))

            plim = {"n": 0}
            PLIM = int(os.environ.get('P2LIM', '100000'))

            def pull(n):
                for _ in range(n):
                    if not plist:
                        return
                    if n < 10 ** 8 and plim["n"] >= PLIM:
                        return
                    plim["n"] += 1
                    a, k = plist.pop(0)
                    add(*a, **k)

            pro(0, padd)
            pull(10 ** 9)
            for c in range(NT):
                if c + 1 < NT:
                    pro(c + 1, padd)
                if os.environ.get('P2NOIL'):
                    pull(10 ** 9)
                plim["n"] = 0
                post(c, pull)
                pull(10 ** 9)
            S_.barrier()

        def phase3(L):
            A.off = P0
            NSL = 6
            PD = 3
            qk = A.alloc((8, S), BF16)
            vaug = A.alloc((NT, NH, HD + 1), BF16)
            mbig = A.alloc((MBW,), BF16)
            esb = [A.alloc((512,), BF16) for _ in range(NSL)]
            psb = [A.alloc((512,), BF16) for _ in range(NSL)]
            osb = [A.alloc((4, 512), F32) for _ in range(2)]
            rec = [A.alloc((4,), F32) for _ in range(2)]
            for m in range(8):
                add("sp", lambda e, m=m: e.dma_start(out=qk[:, m, :], in_=qkT_d[m * 128:(m + 1) * 128, :]), writes=[B("qk", m)], dma=True)
            for t in range(NT):
                add("sp", lambda e, t=t: e.dma_start(out=vaug[:, t, :, 0:HD], in_=vbf_d[t * 128:(t + 1) * 128, :].rearrange("p (h c) -> p h c", c=HD)),
                    writes=[B("vaug", t)], dma=True)
            add("pool", lambda e: e.dma_start(out=mbig, in_=mbig_d), writes=[B("mbig")], dma=True)
            add("pool", lambda e: e.memset(vaug[:, :, :, HD:HD + 1], 1.0), writes=[B("vones")])
            Bqk = [B("qk", m) for m in range(8)]
            units = []
            for sb in range(S // 512):
                q0 = sb * 512
                kt_lo = max(0, (q0 - 2048) // 128)
                kt_hi = (q0 + 511) // 128
                for h in range(NH):
                    for kt in range(kt_lo, kt_hi + 1):
                        units.append((sb, h, kt, kt == kt_lo, kt == kt_hi))

            def front(u, idx):
                sb, h, kt, first, last = u
                q0 = sb * 512
                ph = (h % 2) * 64
                Dd = q0 - kt * 128
                bs = nb(0, 6)
                es = idx % NSL
                add("pe", lambda e: e.matmul(pbank[bs][:, :], qk[ph:ph + 64, 4 + h // 2, kt * 128:(kt + 1) * 128], qk[ph:ph + 64, h // 2, q0:q0 + 512],
                                             start=True, stop=True),
                    reads=[Bqk[h // 2], Bqk[4 + h // 2]], writes=[Bbank[bs]])
                add("act", lambda e: e.activation(out=esb[es], in_=pbank[bs][:, :], func=AF.Exp, scale=1.0 / 8.0),
                    reads=[Bbank[bs]], writes=[B("esb", es)])
                add("dve", lambda e: e.tensor_tensor(out=psb[es], in0=esb[es], in1=mbig[:, Dd + 384:Dd + 384 + 512], op=ALU.mult),
                    reads=[B("esb", es), B("mbig")], writes=[B("psb", es)])

            def back(u, idx):
                sb, h, kt, first, last = u
                q0 = sb * 512
                Dd = q0 - kt * 128
                es = idx % NSL
                bacc = 6 + (h % 2)
                os_ = sb % 2
                qss = [qs for qs in range(4) if (Dd + qs * 128 + 127 >= 0) and (Dd + qs * 128 - 127 <= 2048)]

                def mmpv(e):
                    ins = None
                    for qs in qss:
                        ins = e.matmul(pbank[bacc][:, qs * 65:(qs + 1) * 65], psb[es][:, qs * 128:(qs + 1) * 128], vaug[:, kt, h, :],
                                       start=(first and qs == qss[0]), stop=last, skip_group_check=True)
                    return ins
                add("pe", mmpv, reads=[B("psb", es), B("vaug", kt), B("vones")], writes=[Bbank[bacc]])
                if last:
                    accv = pbank[bacc][:, 0:260].rearrange("p (q c) -> p q c", c=65)
                    add("dve", lambda e: e.reciprocal(rec[os_].unsqueeze(2), accv[:, :, 64:65]), reads=[Bbank[bacc]], writes=[B("rec", os_)])
                    add("dve", lambda e: e.tensor_tensor(out=osb[os_][:, :, h * 64:(h + 1) * 64], in0=accv[:, :, 0:64],
                                                         in1=rec[os_].unsqueeze(2).broadcast_to([128, 4, 64]), op=ALU.mult),
                        reads=[Bbank[bacc], B("rec", os_)], writes=[B("osb", os_)])
                    if h == NH - 1:
                        add("sp", lambda e: e.dma_start(out=atto_d[q0:q0 + 512, :].rearrange("(q p) c -> p q c", p=128), in_=osb[os_]),
                            reads=[B("osb", os_)], dma=True)

            n = len(units)
            for idx in range(n + PD):
                if idx < n:
                    front(units[idx], idx)
                if idx >= PD:
                    back(units[idx - PD], idx - PD)
            S_.barrier()

        def phase4(L, xsrc):
            A.off = P0
            wo = A.alloc((8, D), BF16)
            gpo = A.alloc((D,), F32)
            aog = A.alloc((512,), F32)
            at = [A.alloc((512,), F32) for _ in range(2)]
            cat = [A.alloc((D,), BF16) for _ in range(2)]
            catT = [A.alloc((8, 128), BF16) for _ in range(2)]
            xt = [A.alloc((D,), F32) for _ in range(2)]
            xo = [A.alloc((D,), F32) for _ in range(2)]
            junk = A.alloc((D,), BF16)
            st4 = [A.alloc((8,), F32) for _ in range(2)]
            for kc in range(8):
                add("pool", lambda e, kc=kc: e.dma_start(out=wo[:, kc, :], in_=w_out[L, kc * 128:(kc + 1) * 128, :]), writes=[B("wo", kc)], dma=True)
            add("sp", lambda e: e.dma_start(out=gpo, in_=gpost_d[L, 0:1, :].partition_broadcast(128)), writes=[B("gpo")], dma=True)
            add("sp", lambda e: e.dma_start(out=aog, in_=rowp_d[L:L + 1, 8 * 512:9 * 512].partition_broadcast(128)), writes=[B("aog")], dma=True)
            Bwo = [B("wo", kc) for kc in range(8)]
            def p4front(t):
                s = t % 2
                add("sp", lambda e, s=s, t=t: e.dma_start(out=at[s], in_=atto_d[t * 128:(t + 1) * 128, :]), writes=[B("at", s)], dma=True)
                add("sp", lambda e, s=s, t=t: e.dma_start(out=cat[s][:, 512:1024], in_=rwo_d[t * 128:(t + 1) * 128, :]), writes=[B("catr", s)], dma=True)
                add("sp", lambda e, s=s, t=t: e.dma_start(out=xt[s], in_=xsrc[t * 128:(t + 1) * 128, :]), writes=[B("xt", s)], dma=True)
                Bst = B("st4", s)
                add("act", lambda e, s=s: e.activation(out=junk[:, 0:512], in_=at[s], func=AF.Square, accum_out=st4[s][:, 0:1]), reads=[B("at", s)], writes=[Bst])
                add("act", lambda e, s=s: e.activation(out=st4[s][:, 1:2], in_=st4[s][:, 0:1], func=AF.Sqrt, bias=EPS, scale=1.0 / DA), reads=[Bst], writes=[Bst])
                add("dve", lambda e, s=s: e.reciprocal(st4[s][:, 2:3], st4[s][:, 1:2]), reads=[Bst], writes=[Bst])
                add("dve", lambda e, s=s: e.scalar_tensor_tensor(out=cat[s][:, 0:512], in0=at[s], scalar=st4[s][:, 2:3], in1=aog, op0=ALU.mult, op1=ALU.mult),
                    reads=[B("at", s), Bst, B("aog")], writes=[B("cata", s)])
                tb = nb(0, 2)

                def tr(e, s=s, tb=tb):
                    pv = bank_bf(tb)
                    for kc in range(8):
                        ins = e.transpose(pv[:, kc * 128:(kc + 1) * 128], cat[s][:, kc * 128:(kc + 1) * 128], ident_b)
                    return ins
                add("pe", tr, reads=[B("cata", s), B("catr", s), B("identb")], writes=[Bbank[tb]])
                add("act", lambda e, s=s, tb=tb: e.copy(catT[s].rearrange("p k t -> p (k t)"), bank_bf(tb)), reads=[Bbank[tb]], writes=[B("catT", s)])
            def p4back(t):
                s = t % 2
                Bst = B("st4", s)
                bks = []
                for n in range(2):
                    bk = nb(2, 8)
                    bks.append(bk)

                    def mmo(e, s=s, bk=bk, n=n):
                        for kc in range(8):
                            ins = e.matmul(pbank[bk][:, :], catT[s][:, kc, :], wo[:, kc, n * 512:(n + 1) * 512], start=(kc == 0), stop=(kc == 7))
                        return ins
                    add("pe", mmo, reads=Bwo + [B("catT", s)], writes=[Bbank[bk]])
                    add("act", lambda e, s=s, bk=bk, n=n: e.activation(out=junk[:, 0:512], in_=pbank[bk][:, :], func=AF.Square, accum_out=st4[s][:, 3 + n:4 + n]),
                        reads=[Bbank[bk]], writes=[B("st4b", s, n)])
                Bst2 = B("st4c", s)
                add("dve", lambda e, s=s: e.tensor_tensor(out=st4[s][:, 5:6], in0=st4[s][:, 3:4], in1=st4[s][:, 4:5], op=ALU.add),
                    reads=[B("st4b", s, 0), B("st4b", s, 1)], writes=[Bst2])
                add("act", lambda e, s=s: e.activation(out=st4[s][:, 6:7], in_=st4[s][:, 5:6], func=AF.Sqrt, bias=EPS, scale=1.0 / D), reads=[Bst2], writes=[Bst2])
                add("dve", lambda e, s=s: e.reciprocal(st4[s][:, 7:8], st4[s][:, 6:7]), reads=[Bst2], writes=[Bst2])
                for n in range(2):
                    bk = bks[n]
                    add("dve", lambda e, s=s, bk=bk, n=n: e.scalar_tensor_tensor(out=xo[s][:, n * 512:(n + 1) * 512], in0=pbank[bk][:, :], scalar=st4[s][:, 7:8],
                                                                              in1=gpo[:, n * 512:(n + 1) * 512], op0=ALU.mult, op1=ALU.mult),
                        reads=[Bbank[bk], Bst2, B("gpo")], writes=[B("xo", s, n)])
                add("pool", lambda e, s=s: e.tensor_tensor(out=xo[s], in0=xo[s], in1=xt[s], op=ALU.add),
                    reads=[B("xo", s, 0), B("xo", s, 1), B("xt", s)], writes=[B("xo", s, 0), B("xo", s, 1)])
                add("sp", lambda e, s=s, t=t: e.dma_start(out=xmid_d[t * 128:(t + 1) * 128, :], in_=xo[s]), reads=[B("xo", s, 0), B("xo", s, 1)], dma=True)
            p4front(0)
            for t in range(NT):
                if t + 1 < NT:
                    p4front(t + 1)
                p4back(t)
            S_.barrier()

        def phase5(L, xdst, outbufs):
            A.off = P0
            TB = 256
            NB5 = S // TB
            wu = A.alloc((8, DFF), BF16)
            wd = A.alloc((32, D), BF16)
            gpre = A.alloc((8,), F32)
            gpo = A.alloc((D,), F32)
            xt = [A.alloc((2, D), F32) for _ in range(2)]
            xn = [A.alloc((D,), BF16) for _ in range(2)]
            hT = [A.alloc((8, TB), BF16) for _ in range(2)]
            aT = A.alloc((32, TB), BF16)
            rl = [A.alloc((TB,), BF16) for _ in range(3)]
            xo = [A.alloc((D,), F32) for _ in range(2)]
            junk = A.alloc((D,), BF16)
            st5 = [A.alloc((8,), F32) for _ in range(2)]
            for kc in range(8):
                add("pool", lambda e, kc=kc: e.dma_start(out=wu[:, kc, :], in_=w_up[L, kc * 128:(kc + 1) * 128, :]), writes=[B("wu", kc)], dma=True)
            for g in range(4):
                add("pool", lambda e, g=g: e.dma_start(out=wd[:, g * 8:(g + 1) * 8, :], in_=w_dn[L, g * 1024:(g + 1) * 1024, :].rearrange("(m p) d -> p m d", p=128)),
                    writes=[B("wd", g)], dma=True)
            add("sp", lambda e: e.dma_start(out=gpre, in_=gpre_d[L, 1]), writes=[B("gpre")], dma=True)
            add("sp", lambda e: e.dma_start(out=gpo, in_=gpost_d[L, 1:2, :].partition_broadcast(128)), writes=[B("gpo")], dma=True)
            Bwu = [B("wu", kc) for kc in range(8)]
            Bwd = [B("wd", g) for g in range(4)]
            cnt = 0
            def p5front(b):
                bs = b % 2
                BhT = B("hT", bs)
                for ts in range(2):
                    t = 2 * b + ts
                    s = t % 2
                    Bx = B("xt", bs, ts)
                    add("sp", lambda e, bs=bs, ts=ts, t=t: e.dma_start(out=xt[bs][:, ts, :], in_=xmid_d[t * 128:(t + 1) * 128, :]), writes=[Bx], dma=True)
                    Bst = B("st5", s)
                    add("act", lambda e, bs=bs, ts=ts, s=s: e.activation(out=junk, in_=xt[bs][:, ts, :], func=AF.Square, accum_out=st5[s][:, 0:1]), reads=[Bx], writes=[Bst])
                    add("act", lambda e, s=s: e.activation(out=st5[s][:, 1:2], in_=st5[s][:, 0:1], func=AF.Sqrt, bias=EPS, scale=1.0 / D), reads=[Bst], writes=[Bst])
                    add("dve", lambda e, s=s: e.reciprocal(st5[s][:, 2:3], st5[s][:, 1:2]), reads=[Bst], writes=[Bst])
                    add("dve", lambda e, bs=bs, ts=ts, s=s: e.tensor_scalar(xn[s], xt[bs][:, ts, :], st5[s][:, 2:3], None, ALU.mult), reads=[Bx, Bst], writes=[B("xn", s)])
                    tb = nb(0, 2)

                    def tr(e, s=s, tb=tb):
                        pv = bank_bf(tb)
                        for kc in range(8):
                            ins = e.transpose(pv[:, kc * 128:(kc + 1) * 128], xn[s][:, kc * 128:(kc + 1) * 128], ident_b)
                        return ins
                    add("pe", tr, reads=[B("xn", s), B("identb")], writes=[Bbank[tb]])
                    add("dve", lambda e, tb=tb, bs=bs, ts=ts: e.tensor_tensor(
                        out=hT[bs][:, :, ts * 128:(ts + 1) * 128], in0=bank_bf(tb).rearrange("p (k t) -> p k t", t=128),
                        in1=gpre.unsqueeze(2).broadcast_to([128, 8, 128]), op=ALU.mult),
                        reads=[Bbank[tb], B("gpre")], writes=[BhT])
            def p5back(b):
                bs = b % 2
                BhT = B("hT", bs)
                cnt = cnt5[0]
                for m2 in range(16):
                    bk = nb(2, 6)

                    def mmu(e, bk=bk, m2=m2, bs=bs):
                        for j in range(2):
                            m = 2 * m2 + j
                            for kc in range(8):
                                ins = e.matmul(pbank[bk][:, j * TB:(j + 1) * TB], wu[:, kc, m * 128:(m + 1) * 128], hT[bs][:, kc, :], start=(kc == 0), stop=(kc == 7))
                        return ins
                    add("pe", mmu, reads=Bwu + [BhT], writes=[Bbank[bk]])
                    for j in range(2):
                        m = 2 * m2 + j
                        rs = cnt % 3
                        cnt += 1
                        cnt5[0] = cnt
                        add("act", lambda e, bk=bk, j=j, rs=rs: e.activation(out=rl[rs], in_=pbank[bk][:, j * TB:(j + 1) * TB], func=AF.Relu),
                            reads=[Bbank[bk]], writes=[B("rl", rs)])
                        eng = "pool" if m % 2 else "dve"
                        add(eng, lambda e, m=m, rs=rs: e.tensor_tensor(out=aT[:, m, :], in0=rl[rs], in1=rl[rs], op=ALU.mult),
                            reads=[B("rl", rs)], writes=[B("aT", m)])
                BaT = [B("aT", m) for m in range(32)]
                for ts in range(2):
                    t = 2 * b + ts
                    s = t % 2
                    bks = []
                    for n in range(2):
                        bk = nb(6, 8) if False else (6 + n)
                        bks.append(bk)

                        def mmd(e, bk=bk, ts=ts, n=n):
                            for m in range(32):
                                ins = e.matmul(pbank[bk][:, :], aT[:, m, ts * 128:(ts + 1) * 128], wd[:, m, n * 512:(n + 1) * 512], start=(m == 0), stop=(m == 31))
                            return ins
                        add("pe", mmd, reads=Bwd + BaT, writes=[Bbank[bk]])
                        add("act", lambda e, s=s, bk=bk, n=n: e.activation(out=junk[:, 0:512], in_=pbank[bk][:, :], func=AF.Square, accum_out=st5[s][:, 3 + n:4 + n]),
                            reads=[Bbank[bk]], writes=[B("st5b", s, n)])
                    Bst2 = B("st5c", s)
                    add("dve", lambda e, s=s: e.tensor_tensor(out=st5[s][:, 5:6], in0=st5[s][:, 3:4], in1=st5[s][:, 4:5], op=ALU.add),
                        reads=[B("st5b", s, 0), B("st5b", s, 1)], writes=[Bst2])
                    add("act", lambda e, s=s: e.activation(out=st5[s][:, 6:7], in_=st5[s][:, 5:6], func=AF.Sqrt, bias=EPS, scale=1.0 / D), reads=[Bst2], writes=[Bst2])
                    add("dve", lambda e, s=s: e.reciprocal(st5[s][:, 7:8], st5[s][:, 6:7]), reads=[Bst2], writes=[Bst2])
                    for n in range(2):
                        bk = bks[n]
                        add("dve", lambda e, s=s, bk=bk, n=n: e.scalar_tensor_tensor(out=xo[s][:, n * 512:(n + 1) * 512], in0=pbank[bk][:, :], scalar=st5[s][:, 7:8],
                                                                                  in1=gpo[:, n * 512:(n + 1) * 512], op0=ALU.mult, op1=ALU.mult),
                            reads=[Bbank[bk], Bst2, B("gpo")], writes=[B("xo", s, n)])
                    add("pool", lambda e, s=s, bs=bs, ts=ts: e.tensor_tensor(out=xo[s], in0=xo[s], in1=xt[bs][:, ts, :], op=ALU.add),
                        reads=[B("xo", s, 0), B("xo", s, 1), B("xt", bs, ts)], writes=[B("xo", s, 0), B("xo", s, 1)])
                    ob = B("outd", L, t)
                    outbufs.append(ob)
                    add("sp", lambda e, s=s, t=t: e.dma_start(out=xdst[t * 128:(t + 1) * 128, :], in_=xo[s]), reads=[B("xo", s, 0), B("xo", s, 1)], writes=[ob], dma=True)
            cnt5 = [0]
            p5front(0)
            for b in range(NB5):
                if b + 1 < NB5:
                    p5front(b + 1)
                p5back(b)
            S_.barrier()

        outbufs = []
        for L in range(depth):
            xsrc = x_in if L == 0 else xres_d
            xdst = out_d if L == depth - 1 else xres_d
            if phases is None or 1 in phases:
                phase1(L, xsrc)
            if phases is None or 2 in phases:
                phase2(L)
            if phases is None or 3 in phases:
                phase3(L)
            if phases is None or 4 in phases:
                phase4(L, xsrc)
            if phases is None or 5 in phases:
                phase5(L, xdst, outbufs)
        S_.emit()
    return nc


def _count_mask():
    ki = np.arange(128)[:, None]
    xx = np.arange(MBW)[None, :]
    d = xx - 384 - ki
    c = ((d >= 0) & (d <= 128)).astype(np.float32)
    c += ((d >= 0) & (d <= 512) & (d % 4 == 0)).astype(np.float32)
    c += ((d >= 0) & (d <= 2048) & (d % 16 == 0)).astype(np.float32)
    return np.ascontiguousarray(c, dtype=np.float32)


def _const_masks():
    p = np.arange(128)[:, None]
    f = np.arange(128)[None, :]
    ident = (p == f).astype(np.float32)
    ut = (p <= f).astype(np.float32)
    strict = (p < f).astype(np.float32)
    incl = (p <= f).astype(np.float32)
    low = (f < p).astype(np.float32)
    ones = np.ones((128, 128), np.float32)
    bd16 = ((p // 16) == (f // 16)).astype(np.float32)
    lowbd = low * bd16
    mms = []
    for bs in (16, 32, 64):
        mu_ = (((p // bs) % 2 == 0) & ((f // bs) == (p // bs) + 1)).astype(np.float32)
        mms += [mu_, np.ascontiguousarray(mu_.T)]
    return np.ascontiguousarray(np.concatenate([ident, ut, strict, incl, strict, incl, low, ones, -bd16, -lowbd] + mms, axis=1))


def pack_inputs(depth, norm_mix_pre, norm_mix_post, norm_ffn_pre, norm_ffn_post, w_in_first, w_in_rest,
                mu_shift, mu_shift_mv, attn_out_gain, decay_w0, decay_up, aaa_a0, aaa_up, mv_v0, mv_up,
                gate_up, k_k, k_a, r_k, gn_w, gn_b, w_out, w_ffn_up, w_ffn_down):
    f = np.float32
    w_in = np.zeros((depth, D, NCOLS), f)
    w_in[0, :, :w_in_first.shape[1]] = w_in_first
    for i in range(1, depth):
        w_in[i] = w_in_rest[i - 1]
    gpre = np.zeros((depth, 2, 128, 8), f)
    gpost = np.zeros((depth, 2, D), f)
    mu = np.zeros((depth, NZ), f)
    rowp = np.zeros((depth, 9, 512), f)
    lora = np.zeros((depth, 4, 96, 512), f)
    for i in range(depth):
        gpre[i, 0] = np.asarray(norm_mix_pre[i]).reshape(8, 128).T
        gpre[i, 1] = np.asarray(norm_ffn_pre[i]).reshape(8, 128).T
        gpost[i, 0] = norm_mix_post[i]
        gpost[i, 1] = norm_ffn_post[i]
        mu[i, :1696] = mu_shift[i]
        rowp[i, 0] = decay_w0[i]
        rowp[i, 1] = aaa_a0[i]
        rowp[i, 3] = k_k[i]
        rowp[i, 4] = k_a[i]
        rowp[i, 5] = np.asarray(r_k[i]).reshape(512)
        rowp[i, 6] = gn_w[i]
        rowp[i, 7] = gn_b[i]
        rowp[i, 8] = attn_out_gain[i]
        lora[i, 0, :32] = decay_up[i]
        lora[i, 1, :32] = aaa_up[i]
        lora[i, 3, :96] = gate_up[i]
        if i > 0:
            mu[i, 1696:] = mu_shift_mv[i - 1]
            rowp[i, 2] = mv_v0[i - 1]
            lora[i, 2, :32] = mv_up[i - 1]
    return {
        "w_in": w_in, "w_out": np.ascontiguousarray(w_out[:depth], f), "w_up": np.ascontiguousarray(w_ffn_up[:depth], f),
        "w_dn": np.ascontiguousarray(w_ffn_down[:depth], f), "gpre": gpre, "gpost": gpost, "mu": mu,
        "rowp": rowp.reshape(depth, 9 * 512), "lora": lora, "cmask": _const_masks(), "mbig": _count_mask(),
    }


_CACHE = {}


def kernel(x, **params):
    x = np.asarray(x, np.float32)
    Bn, S, _ = x.shape
    depth = 4
    params = {k: np.asarray(v, np.float32) for k, v in params.items()}
    shared = pack_inputs(depth, **params)
    key = (S, depth)
    if key not in _CACHE:
        _CACHE[key] = build_program(S, depth)
    nc = _CACHE[key]
    in_maps = []
    for b in range(Bn):
        m = dict(shared)
        m["x"] = np.ascontiguousarray(x[b])
        in_maps.append(m)
    res = run_bass_kernel_spmd(nc, in_maps, core_ids=list(range(Bn)))
    return np.stack([np.asarray(r["out"], np.float32) for r in res.results], axis=0)
```

```python
import contextlib
import numpy as np
import ml_dtypes
import concourse.bass as bass
import concourse.mybir as mybir
from concourse.bass_utils import run_bass_kernel_spmd

F32 = mybir.dt.float32
BF16 = mybir.dt.bfloat16
ALU = mybir.AluOpType
AF = mybir.ActivationFunctionType
AX = mybir.AxisListType

D = 1024
DA = 512
DR = 512
NH = 8
HD = 64
NCOLS = 3264
NZ = 1728
DFF = 4096
EPS = 1e-6
GN_EPS = 64e-5
C0 = float(np.exp(-0.5))
MBW = 2944
N_CORES = 8

ENGS = ("pe", "act", "dve", "pool", "sp")
N_DMA_SEMS = 40
import os
P2STOP = int(os.environ.get('P2STOP', '9'))


class Buf:
    __slots__ = ("name", "last_w", "readers", "excl")

    def __init__(self, name):
        self.name = name
        self.last_w = None
        self.readers = {}
        self.excl = False


class Op:
    __slots__ = ("eng", "fn", "deps", "sem", "val", "is_dma")

    def __init__(self, eng, fn, is_dma):
        self.eng = eng
        self.fn = fn
        self.deps = set()
        self.sem = None
        self.val = None
        self.is_dma = is_dma


class Sched:
    def __init__(self, nc):
        self.nc = nc
        self.ops = []
        self.bufs = {}
        self.dma_rr = 0
        self.dma_last = [None] * N_DMA_SEMS
        self.dma_cnt = [0] * N_DMA_SEMS
        self.last_op = {e: None for e in ENGS}
        self.dma_since = []
        self.sync_same_engine = True

    def B(self, *key):
        b = self.bufs.get(key)
        if b is None:
            b = Buf(key)
            self.bufs[key] = b
        return b

    def add(self, eng, fn, reads=(), writes=(), dma=False):
        op = Op(eng, fn, dma)
        deps = op.deps
        for b in reads:
            if b.last_w is not None:
                deps.add(b.last_w)
            if b.excl:
                for k, r in b.readers.items():
                    if k != eng:
                        deps.add(r)
        for b in writes:
            if b.last_w is not None:
                deps.add(b.last_w)
            for r in b.readers.values():
                deps.add(r)
        for b in reads:
            b.readers[id(op) if dma else eng] = op
        for b in writes:
            b.last_w = op
            b.readers = {}
        deps.discard(op)
        if dma:
            k = self.dma_rr
            self.dma_rr = (k + 1) % N_DMA_SEMS
            prev = self.dma_last[k]
            if prev is not None:
                deps.add(prev)
            self.dma_cnt[k] += 16
            op.sem = k
            op.val = self.dma_cnt[k]
            self.dma_last[k] = op
            self.dma_since.append(op)
        else:
            self.last_op[eng] = op
        self.ops.append(op)
        return op

    def barrier(self):
        lasts = [o for o in self.last_op.values() if o is not None] + list(self.dma_since)
        for e in ENGS:
            op = Op(e, None, False)
            op.deps = set(lasts)
            self.ops.append(op)
        self.dma_since = []
        for b in self.bufs.values():
            b.last_w = None
            b.readers = {}

    def _skip(self, d, op):
        if d.is_dma or op.is_dma or d.eng != op.eng:
            return False
        return d.eng == "pe" or not self.sync_same_engine

    def emit(self):
        nc = self.nc
        with contextlib.ExitStack() as st:
            esem = {e: st.enter_context(nc.semaphore("s_" + e)) for e in ENGS}
            dsem = [st.enter_context(nc.semaphore("d%d" % i)) for i in range(N_DMA_SEMS)]
            needed = set()
            for op in self.ops:
                for d in op.deps:
                    if d.is_dma or self._skip(d, op):
                        continue
                    needed.add(d)
            cnt = {e: 0 for e in ENGS}
            for op in self.ops:
                if op.is_dma:
                    op.sem = dsem[op.sem]
                else:
                    op.sem = esem[op.eng]
                    if op in needed:
                        cnt[op.eng] += 1
                        op.val = cnt[op.eng]
            per = {e: [op for op in self.ops if op.eng == e] for e in ENGS}
            block = st.enter_context(nc.Block())

            def run(engname, eng):
                waited = {}
                for op in per[engname]:
                    for d in op.deps:
                        if self._skip(d, op):
                            continue
                        key = id(d.sem)
                        if waited.get(key, 0) >= d.val:
                            continue
                        eng.wait_ge(d.sem, d.val)
                        waited[key] = d.val
                    if op.fn is None:
                        continue
                    ins = op.fn(eng)
                    if op.is_dma:
                        ins.then_inc(op.sem, 16)
                    elif op in needed:
                        ins.then_inc(op.sem, 1)

            @block.tensor
            def _(e):
                run("pe", e)

            @block.scalar
            def _(e):
                run("act", e)

            @block.vector
            def _(e):
                run("dve", e)

            @block.gpsimd
            def _(e):
                run("pool", e)

            @block.sync
            def _(e):
                run("sp", e)


class Arena:
    def __init__(self, tens, nwords):
        self.t = tens
        self.n = nwords
        self.off = 0

    def alloc(self, free_shape, dtype):
        n = int(np.prod(free_shape))
        words = n if dtype == F32 else (n + 1) // 2
        words = (words + 7) // 8 * 8
        assert self.off + words <= self.n, ("arena overflow", self.off, words, self.n)
        ap = self.t[:, self.off:self.off + words]
        self.off += words
        if dtype != F32:
            ap = ap.bitcast(dtype)
        ap = ap[:, 0:n]
        if len(free_shape) == 2:
            ap = ap.rearrange("p (a b) -> p a b", b=free_shape[1])
        elif len(free_shape) == 3:
            ap = ap.rearrange("p (a b c) -> p a b c", b=free_shape[1], c=free_shape[2])
        return ap


def build_program(S, depth, debug=False, phases=None):
    NT = S // 128
    nc = bass.Bass("TRN2", target_bir_lowering=False)
    okind = "ExternalOutput" if debug else "Internal"

    def din(name, shape, dt=F32):
        return nc.dram_tensor(name, list(shape), dt, kind="ExternalInput").ap()

    def dscr(name, shape, dt=F32):
        return nc.dram_tensor(name, list(shape), dt, kind=okind).ap()

    x_in = din("x", [S, D])
    w_in = din("w_in", [depth, D, NCOLS])
    w_out = din("w_out", [depth, D, D])
    w_up = din("w_up", [depth, D, DFF])
    w_dn = din("w_dn", [depth, DFF, D])
    gpre_d = din("gpre", [depth, 2, 128, 8])
    gpost_d = din("gpost", [depth, 2, D])
    mu_d = din("mu", [depth, NZ])
    rowp_d = din("rowp", [depth, 9 * 512])
    lora_d = din("lora", [depth, 4, 96, 512])
    cmask_d = din("cmask", [128, 2048])
    mbig_d = din("mbig", [128, MBW])
    out_d = nc.dram_tensor("out", [S, D], F32, kind="ExternalOutput").ap()

    qkT_d = dscr("qkT", [D, S], BF16)
    vbf_d = dscr("vbf", [S, DA], BF16)
    zr_d = dscr("zr", [S, NZ])
    vfirst_d = dscr("vfirst", [S, DR])
    atto_d = dscr("atto", [S, DA])
    rwo_d = dscr("rwo", [S, DR], BF16)
    xmid_d = dscr("xmid", [S, D])
    xres_d = dscr("xres", [S, D])

    S_ = Sched(nc)
    add = S_.add
    B = S_.B
    taps = {}

    def tap(name, ap, bufs):
        if not debug or name in taps:
            return
        tdt = nc.dram_tensor("tap_" + name, list(ap.shape), ap.dtype, kind="ExternalOutput").ap()
        taps[name] = tdt
        add("sp", lambda e: e.dma_start(out=tdt, in_=ap), reads=bufs, dma=True)

    with contextlib.ExitStack() as st:
        ARENA_WORDS = 50 * 1024
        arena_t = st.enter_context(nc.sbuf_tensor("arena", [128, ARENA_WORDS], F32))
        A = Arena(arena_t, ARENA_WORDS)
        pbank = [st.enter_context(nc.psum_tensor("pb%d" % i, [128, 512], F32)) for i in range(8)]
        Bbank = [B("bank", i) for i in range(8)]
        for b_ in Bbank:
            b_.excl = True

        def bank_bf(i):
            return pbank[i][:, :].bitcast(BF16)

        cmask = A.alloc((2048,), F32)
        ident_f = cmask[:, 0:128]
        ut_f = cmask[:, 128:256]
        mask4 = cmask[:, 256:768]
        mlow = cmask[:, 768:896]
        ones_f = cmask[:, 896:1024]
        bd16 = cmask[:, 1024:1152]
        lowbd = cmask[:, 1152:1280]
        ident_b = A.alloc((128,), BF16)
        Bc = B("consts")
        add("sp", lambda e: e.dma_start(out=cmask, in_=cmask_d), writes=[Bc], dma=True)
        add("dve", lambda e: e.tensor_copy(ident_b, ident_f), reads=[Bc], writes=[B("identb")])
        ut_b = A.alloc((128,), BF16)
        ones_b = A.alloc((128,), BF16)
        add("dve", lambda e: e.tensor_copy(ut_b, ut_f), reads=[Bc], writes=[B("utb")])
        add("dve", lambda e: e.tensor_copy(ones_b, ones_f), reads=[Bc], writes=[B("onesb")])
        P0 = A.off
        S_.barrier()

        rot = {}

        def nb(lo=0, hi=8):
            i = rot.get((lo, hi), 0)
            rot[(lo, hi)] = i + 1
            return lo + i % (hi - lo)

        def rms_rstd(eng_sq_in, width, stat, Bin, Bstat, tag):
            pass

        def phase1(L, xsrc):
            A.off = P0
            NB = S // 512
            win = A.alloc((8, NCOLS), BF16)
            gpre = A.alloc((8,), F32)
            xt = [A.alloc((D,), F32) for _ in range(2)]
            junk = A.alloc((D,), BF16)
            stt_ = [A.alloc((4,), F32) for _ in range(2)]
            xn = [A.alloc((D,), BF16) for _ in range(2)]
            hT = [A.alloc((8, 512), BF16) for _ in range(2)]
            qks = [A.alloc((512,), BF16) for _ in range(3)]
            zt = [A.alloc((NZ,), F32) for _ in range(2)]
            vt = [A.alloc((512,), BF16) for _ in range(2)]
            for kc in range(8):
                add("pool", lambda e, kc=kc: e.dma_start(out=win[:, kc, :], in_=w_in[L, kc * 128:(kc + 1) * 128, :]),
                    writes=[B("win", kc)], dma=True)
            add("sp", lambda e: e.dma_start(out=gpre, in_=gpre_d[L, 0]), writes=[B("gpre")], dma=True)
            Bwin = [B("win", kc) for kc in range(8)]
            ev = {"i": 0}

            def evac(out, in_, reads, writes):
                ev["i"] += 1
                if ev["i"] % 2:
                    add("act", lambda e: e.copy(out, in_), reads=reads, writes=writes)
                else:
                    add("dve", lambda e: e.tensor_copy(out, in_), reads=reads, writes=writes)

            def p1front(b):
                hs = b % 2
                BhT = B("hT", hs)
                for ts in range(4):
                    t = 4 * b + ts
                    s = t % 2
                    add("sp", lambda e, s=s, t=t: e.dma_start(out=xt[s], in_=xsrc[t * 128:(t + 1) * 128, :]),
                        writes=[B("xt", s)], dma=True)
                    add("act", lambda e, s=s: e.activation(out=junk, in_=xt[s], func=AF.Square, accum_out=stt_[s][:, 0:1]),
                        reads=[B("xt", s)], writes=[B("st0", s)])
                    add("act", lambda e, s=s: e.activation(out=stt_[s][:, 1:2], in_=stt_[s][:, 0:1], func=AF.Sqrt, bias=EPS, scale=1.0 / D),
                        reads=[B("st0", s)], writes=[B("st1", s)])
                    add("dve", lambda e, s=s: e.reciprocal(stt_[s][:, 2:3], stt_[s][:, 1:2]),
                        reads=[B("st1", s)], writes=[B("st2", s)])
                    add("dve", lambda e, s=s: e.tensor_scalar(xn[s], xt[s], stt_[s][:, 2:3], None, ALU.mult),
                        reads=[B("xt", s), B("st2", s)], writes=[B("xn", s)])
                    tb = nb(0, 2)

                    def tr(e, s=s, tb=tb):
                        pv = bank_bf(tb)
                        for kc in range(8):
                            ins = e.transpose(pv[:, kc * 128:(kc + 1) * 128], xn[s][:, kc * 128:(kc + 1) * 128], ident_b)
                        return ins
                    add("pe", tr, reads=[B("xn", s), B("identb")], writes=[Bbank[tb]])
                    add("dve", lambda e, tb=tb, hs=hs, ts=ts: e.tensor_tensor(
                        out=hT[hs][:, :, ts * 128:(ts + 1) * 128],
                        in0=bank_bf(tb).rearrange("p (k t) -> p k t", t=128),
                        in1=gpre.unsqueeze(2).broadcast_to([128, 8, 128]), op=ALU.mult),
                        reads=[Bbank[tb], B("gpre")], writes=[BhT])
            def p1back(b):
                hs = b % 2
                BhT = B("hT", hs)
                for m in range(8):
                    bk = nb(2, 8)

                    def mmf(e, m=m, bk=bk, hs=hs):
                        for kc in range(8):
                            ins = e.matmul(pbank[bk][:, :], win[:, kc, m * 128:(m + 1) * 128], hT[hs][:, kc, :],
                                           start=(kc == 0), stop=(kc == 7))
                        return ins
                    add("pe", mmf, reads=Bwin + [BhT], writes=[Bbank[bk]])
                    qs = (b * 8 + m) % 3
                    evac(qks[qs], pbank[bk][:, :], [Bbank[bk]], [B("qks", qs)])
                    add("sp", lambda e, m=m, b=b, qs=qs: e.dma_start(
                        out=qkT_d[m * 128:(m + 1) * 128, b * 512:(b + 1) * 512], in_=qks[qs]),
                        reads=[B("qks", qs)], dma=True)
                for ts in range(4):
                    t = 4 * b + ts
                    zs = t % 2
                    for (c0, c1) in ((1024, 1536), (1536, 2048), (2048, 2560), (2560, 3072), (3072, 3264)):
                        bk = nb(2, 8)
                        w = c1 - c0

                        def mmt(e, bk=bk, hs=hs, ts=ts, c0=c0, c1=c1, w=w):
                            for kc in range(8):
                                ins = e.matmul(pbank[bk][:, 0:w], hT[hs][:, kc, ts * 128:(ts + 1) * 128], win[:, kc, c0:c1],
                                               start=(kc == 0), stop=(kc == 7))
                            return ins
                        add("pe", mmt, reads=Bwin + [BhT], writes=[Bbank[bk]])
                        if c0 == 1024:
                            evac(vt[zs], pbank[bk][:, 0:512], [Bbank[bk]], [B("vt", zs)])
                        else:
                            evac(zt[zs][:, c0 - 1536:c1 - 1536], pbank[bk][:, 0:w], [Bbank[bk]], [B("zt", zs)])
                    add("sp", lambda e, t=t, zs=zs: e.dma_start(out=vbf_d[t * 128:(t + 1) * 128, :], in_=vt[zs]),
                        reads=[B("vt", zs)], dma=True)
                    add("sp", lambda e, t=t, zs=zs: e.dma_start(out=zr_d[t * 128:(t + 1) * 128, :], in_=zt[zs]),
                        reads=[B("zt", zs)], dma=True)
            p1front(0)
            for b in range(NB):
                if b + 1 < NB:
                    p1front(b + 1)
                p1back(b)
            S_.barrier()

        def phase2(L):
            A.off = P0
            mu = A.alloc((NZ,), F32)
            rowp = A.alloc((9, 512), F32)
            lora = A.alloc((4, 512), BF16)
            ST = [A.alloc((NH, HD), F32) for _ in range(2)]
            STb = [A.alloc((NH, HD), BF16) for _ in range(2)]
            cur = [A.alloc((NZ,), F32) for _ in range(2)]
            prv = [A.alloc((NZ,), F32) for _ in range(2)]
            vf = [A.alloc((512,), F32) for _ in range(2)]
            lo = A.alloc((192,), BF16)
            loT = A.alloc((4, 128), BF16)
            tmpa = A.alloc((512,), F32)
            tmpb = A.alloc((512,), F32)
            sg = A.alloc((512,), F32)
            sgh = A.alloc((512,), BF16)
            sgl = A.alloc((512,), BF16)
            cs = A.alloc((512,), F32)
            g_in = A.alloc((512,), F32)
            g_inv = A.alloc((512,), F32)
            g_prev = A.alloc((512,), F32)
            g_end = A.alloc((512,), F32)
            gCTs = [A.alloc((NH,), F32) for _ in range(2)]
            a_sb = A.alloc((512,), F32)
            gate = [A.alloc((512,), F32) for _ in range(2)]
            v2 = [A.alloc((512,), F32) for _ in range(2)]
            kk = A.alloc((512,), F32)
            kap = A.alloc((512,), F32)
            kmod = A.alloc((512,), F32)
            b_sb = A.alloc((512,), F32)
            sm = A.alloc((64,), F32)
            smt = A.alloc((64,), F32)
            rkc = [A.alloc((NH,), F32) for _ in range(2)]
            tm = [A.alloc((7, 512), BF16) for _ in range(2)]
            XT = [A.alloc((NH, 4, 128), BF16) for _ in range(2)]
            NG = 8
            AT = [A.alloc((512,), BF16) for _ in range(NG)]
            Q = [[A.alloc((4, 128), BF16) for _ in range(2)] for _ in range(NG)]
            ZT = [A.alloc((2, 128), BF16) for _ in range(NG)]
            GG = [A.alloc((256,), BF16) for _ in range(NG)]
            Lm = [A.alloc((128,), BF16) for _ in range(NG)]
            tmpm = [A.alloc((256,), F32) for _ in range(NG)]
            nW1 = [A.alloc((HD,), BF16) for _ in range(NG)]
            KU = [A.alloc((128,), BF16) for _ in range(NG)]
            Mc = [A.alloc((HD,), BF16) for _ in range(NG)]
            RhT = [A.alloc((128,), BF16) for _ in range(NG)]
            ysb = A.alloc((512,), F32)
            ysq = A.alloc((512,), F32)
            rwo = [A.alloc((512,), BF16) for _ in range(2)]

            add("sp", lambda e: e.dma_start(out=mu, in_=mu_d[L:L + 1, :].partition_broadcast(128)), writes=[B("mu")], dma=True)
            add("sp", lambda e: e.dma_start(out=rowp.rearrange("p a b -> p (a b)"), in_=rowp_d[L:L + 1, :].partition_broadcast(128)),
                writes=[B("rowp")], dma=True)
            for q in range(4):
                add("pool", lambda e, q=q: e.dma_start(out=lora[0:96, q, :], in_=lora_d[L, q]), writes=[B("lora", q)], dma=True)
            add("pool", lambda e: e.memset(ST[0].rearrange("p a b -> p (a b)"), 0.0), writes=[B("ST", 0)])
            add("pool", lambda e: e.memset(STb[0].rearrange("p a b -> p (a b)"), 0.0), writes=[B("STb", 0)])
            add("pool", lambda e: e.memset(STb[1].rearrange("p a b -> p (a b)"), 0.0), writes=[B("STb", 1)])
            for s_ in range(2):
                add("pool", lambda e, s_=s_: e.memset(XT[s_].rearrange("p a b c -> p (a b c)"), 0.0), writes=[B("XT", s_)])
            for i_ in range(NG):
                add("pool", lambda e, i_=i_: e.memset(Mc[i_], 0.0), writes=[B("Mc", i_)])
                add("pool", lambda e, i_=i_: e.memset(RhT[i_], 0.0), writes=[B("RhT", i_)])
            w0_bc, a0_bc, mv0_bc, kk_bc, ka_bc, rk_bc, gnw_bc, gnb_bc = [rowp[:, i, :] for i in range(8)]
            Brow = B("rowp")

            h3 = lambda ap: ap.rearrange("p (h j) -> p h j", j=HD)

            LV = 1 if os.environ.get('P2FORCE') else L
            def pro(c, padd):
                s = c % 2
                Bcur, Bprv = B("cur", s), B("prv", s)
                gCT = gCTs[s]
                padd("sp", lambda e, s=s, c=c: e.dma_start(out=cur[s], in_=zr_d[c * 128:(c + 1) * 128, :]), writes=[Bcur], dma=True)
                if c == 0:
                    padd("pool", lambda e, s=s: e.memset(prv[s][0:1, :], 0.0), writes=[Bprv])
                    padd("sp", lambda e, s=s: e.dma_start(out=prv[s][1:128, :], in_=zr_d[0:127, :]), writes=[Bprv], dma=True)
                else:
                    padd("sp", lambda e, s=s, c=c: e.dma_start(out=prv[s], in_=zr_d[c * 128 - 1:c * 128 + 127, :]), writes=[Bprv], dma=True)
                if LV > 0:
                    padd("sp", lambda e, s=s, c=c: e.dma_start(out=vf[s], in_=vfirst_d[c * 128:(c + 1) * 128, :]), writes=[B("vf", s)], dma=True)
                padd("pool", lambda e, s=s: e.tensor_tensor(out=prv[s], in0=prv[s], in1=cur[s], op=ALU.subtract), reads=[Bcur, Bprv], writes=[Bprv])
                padd("dve", lambda e, s=s: e.tensor_tensor(out=prv[s], in0=prv[s], in1=mu, op=ALU.mult), reads=[Bprv, B("mu")], writes=[Bprv])
                padd("dve", lambda e, s=s: e.tensor_tensor(out=prv[s], in0=prv[s], in1=cur[s], op=ALU.add), reads=[Bprv, Bcur], writes=[Bprv])
                zs = prv[s]
                r_, k_, v_ = zs[:, 0:512], zs[:, 512:1024], zs[:, 1024:1536]
                Blo = B("lo")
                padd("act", lambda e, zs=zs: e.activation(out=lo[:, 0:32], in_=zs[:, 1536:1568], func=AF.Tanh), reads=[Bprv], writes=[Blo])
                padd("act", lambda e, zs=zs: e.activation(out=lo[:, 64:160], in_=zs[:, 1600:1696], func=AF.Sigmoid), reads=[Bprv], writes=[Blo])
                padd("dve", lambda e, zs=zs: e.tensor_copy(lo[:, 32:64], zs[:, 1568:1600]), reads=[Bprv], writes=[Blo])
                padd("dve", lambda e, zs=zs: e.tensor_copy(lo[:, 160:192], zs[:, 1696:1728]), reads=[Bprv], writes=[Blo])
                bk = nb(0, 7)

                def trlo(e, bk=bk):
                    pv = bank_bf(bk)
                    e.transpose(pv[0:32, 0:128], lo[:, 0:32], ident_b)
                    e.transpose(pv[0:32, 128:256], lo[:, 32:64], ident_b)
                    e.transpose(pv[0:96, 256:384], lo[:, 64:160], ident_b)
                    return e.transpose(pv[0:32, 384:512], lo[:, 160:192], ident_b)
                padd("pe", trlo, reads=[Blo, B("identb")], writes=[Bbank[bk]])
                BloT = B("loT")
                for (q, r0) in ((0, 32), (1, 32), (2, 96), (3, 32)):
                    padd("act", lambda e, bk=bk, q=q, r0=r0: e.copy(loT[0:r0, q, :], bank_bf(bk)[0:r0, q * 128:(q + 1) * 128]),
                        reads=[Bbank[bk]], writes=[BloT])
                bk = nb(0, 7)
                padd("pe", lambda e, bk=bk: e.matmul(pbank[bk][:, :], loT[0:32, 0, :], lora[0:32, 0, :], start=True, stop=True),
                    reads=[BloT, B("lora", 0)], writes=[Bbank[bk]])
                padd("dve", lambda e, bk=bk: e.tensor_tensor(out=tmpa, in0=pbank[bk][:, :], in1=w0_bc, op=ALU.add),
                    reads=[Bbank[bk], Brow], writes=[B("tmpa")])
                padd("act", lambda e: e.activation(out=sg, in_=tmpa, func=AF.Sigmoid), reads=[B("tmpa")], writes=[B("sg")])
                padd("act", lambda e: e.copy(sgh, sg), reads=[B("sg")], writes=[B("sgh")])
                padd("dve", lambda e: e.tensor_tensor(out=sgl, in0=sg, in1=sgh, op=ALU.subtract), reads=[B("sg"), B("sgh")], writes=[B("sgl")])
                bk = nb(0, 7)
                padd("pe", lambda e, bk=bk: e.matmul(pbank[bk][:, :], loT[0:32, 1, :], lora[0:32, 1, :], start=True, stop=True),
                    reads=[BloT, B("lora", 1)], writes=[Bbank[bk]])
                padd("dve", lambda e, bk=bk: e.tensor_tensor(out=tmpb, in0=pbank[bk][:, :], in1=a0_bc, op=ALU.add),
                    reads=[Bbank[bk], Brow], writes=[B("tmpb")])
                padd("act", lambda e: e.activation(out=a_sb, in_=tmpb, func=AF.Sigmoid), reads=[B("tmpb")], writes=[B("a")])
                bk = nb(0, 7)
                padd("pe", lambda e, bk=bk: e.matmul(pbank[bk][:, :], loT[0:96, 2, :], lora[0:96, 3, :], start=True, stop=True),
                    reads=[BloT, B("lora", 3)], writes=[Bbank[bk]])
                padd("act", lambda e, bk=bk, s=s: e.copy(gate[s], pbank[bk][:, :]), reads=[Bbank[bk]], writes=[B("gate", s)])
                Bv2 = B("v2", s)
                if LV == 0:
                    padd("pool", lambda e, s=s, v_=v_: e.tensor_copy(v2[s], v_), reads=[Bprv], writes=[Bv2])
                    padd("sp", lambda e, s=s, c=c: e.dma_start(out=vfirst_d[c * 128:(c + 1) * 128, :], in_=v2[s]), reads=[Bv2], dma=True)
                else:
                    bk = nb(0, 7)
                    padd("pe", lambda e, bk=bk: e.matmul(pbank[bk][:, :], loT[0:32, 3, :], lora[0:32, 2, :], start=True, stop=True),
                        reads=[BloT, B("lora", 2)], writes=[Bbank[bk]])
                    padd("dve", lambda e, bk=bk: e.tensor_tensor(out=tmpb, in0=pbank[bk][:, :], in1=mv0_bc, op=ALU.add),
                        reads=[Bbank[bk], Brow], writes=[B("tmpb")])
                    padd("act", lambda e: e.activation(out=tmpb, in_=tmpb, func=AF.Sigmoid), reads=[B("tmpb")], writes=[B("tmpb")])
                    padd("pool", lambda e, s=s, v_=v_: e.tensor_tensor(out=v2[s], in0=vf[s], in1=v_, op=ALU.subtract),
                        reads=[B("vf", s), Bprv], writes=[Bv2])
                    padd("dve", lambda e, s=s: e.tensor_tensor(out=v2[s], in0=v2[s], in1=tmpb, op=ALU.mult), reads=[Bv2, B("tmpb")], writes=[Bv2])
                    padd("dve", lambda e, s=s, v_=v_: e.tensor_tensor(out=v2[s], in0=v2[s], in1=v_, op=ALU.add), reads=[Bv2, Bprv], writes=[Bv2])
                bk_cs = nb(0, 7)
                def mmcs(e, bk=bk_cs):
                    e.matmul(pbank[bk][:, :], ut_b, sgh, start=True, stop=False)
                    return e.matmul(pbank[bk][:, :], ut_b, sgl, start=False, stop=True)
                padd("pe", mmcs, reads=[B("sgh"), B("sgl"), B("utb")], writes=[Bbank[bk_cs]])
                padd("act", lambda e, bk=bk_cs: e.copy(cs, pbank[bk][:, :]), reads=[Bbank[bk]], writes=[B("cs")])
                bk_tot = nb(0, 7)
                def mmtot(e, bk=bk_tot):
                    e.matmul(pbank[bk][:, :], ones_b, sgh, start=True, stop=False)
                    return e.matmul(pbank[bk][:, :], ones_b, sgl, start=False, stop=True)
                padd("pe", mmtot, reads=[B("sgh"), B("sgl"), B("onesb")], writes=[Bbank[bk_tot]])
                padd("act", lambda e: e.activation(out=g_in, in_=cs, func=AF.Exp, scale=-C0), reads=[B("cs")], writes=[B("g_in")])
                padd("act", lambda e: e.activation(out=g_inv, in_=cs, func=AF.Exp, scale=C0), reads=[B("cs")], writes=[B("g_inv")])
                padd("pool", lambda e: e.tensor_tensor(out=tmpa, in0=cs, in1=sg, op=ALU.subtract), reads=[B("cs"), B("sg")], writes=[B("tmpa")])
                padd("act", lambda e: e.activation(out=g_prev, in_=tmpa, func=AF.Exp, scale=-C0), reads=[B("tmpa")], writes=[B("g_prev")])
                padd("dve", lambda e, bk=bk_tot: e.tensor_tensor(out=tmpb, in0=pbank[bk][:, :], in1=cs, op=ALU.subtract),
                    reads=[Bbank[bk], B("cs")], writes=[B("tmpb")])
                padd("act", lambda e: e.activation(out=g_end, in_=tmpb, func=AF.Exp, scale=-C0), reads=[B("tmpb")], writes=[B("g_end")])
                bk = nb(0, 7)

                def mmgc(e, bk=bk):
                    for h in range(NH):
                        e.matmul(pbank[bk][0:64, h:h + 1], sgh[:, h * 64:(h + 1) * 64], ones_b[:, 0:1], start=True, stop=False)
                        ins = e.matmul(pbank[bk][0:64, h:h + 1], sgl[:, h * 64:(h + 1) * 64], ones_b[:, 0:1], start=False, stop=True)
                    return ins
                padd("pe", mmgc, reads=[B("sgh"), B("sgl"), B("onesb")], writes=[Bbank[bk]])
                padd("act", lambda e, bk=bk: e.activation(out=gCT[0:64, :], in_=pbank[bk][0:64, 0:NH], func=AF.Exp, scale=-C0),
                    reads=[Bbank[bk]], writes=[B("gCT", s)])
                Bsm = B("sm")
                padd("dve", lambda e, k_=k_: e.tensor_tensor(out=kk, in0=k_, in1=kk_bc, op=ALU.mult), reads=[Bprv, Brow], writes=[B("kk")])
                padd("pool", lambda e: e.tensor_tensor(out=tmpa, in0=kk, in1=kk, op=ALU.mult), reads=[B("kk")], writes=[B("tmpa")])
                padd("dve", lambda e: e.tensor_reduce(out=sm[:, 0:8], in_=h3(tmpa), axis=AX.X, op=ALU.add), reads=[B("tmpa")], writes=[Bsm])
                padd("act", lambda e: e.activation(out=sm[:, 8:16], in_=sm[:, 0:8], func=AF.Sqrt), reads=[Bsm], writes=[Bsm])
                padd("dve", lambda e: e.tensor_scalar(sm[:, 8:16], sm[:, 8:16], 1e-12, None, ALU.max), reads=[Bsm], writes=[Bsm])
                padd("dve", lambda e: e.reciprocal(sm[:, 16:24], sm[:, 8:16]), reads=[Bsm], writes=[Bsm])
                padd("dve", lambda e: e.tensor_tensor(out=h3(kap), in0=h3(kk), in1=sm[:, 16:24].unsqueeze(2).broadcast_to([128, NH, HD]), op=ALU.mult),
                    reads=[B("kk"), Bsm], writes=[B("kap")])
                padd("dve", lambda e: e.scalar_tensor_tensor(out=kmod, in0=a_sb, scalar=-1.0, in1=ka_bc, op0=ALU.add, op1=ALU.mult),
                    reads=[B("a"), Brow], writes=[B("kmod")])
                padd("dve", lambda e, k_=k_: e.scalar_tensor_tensor(out=kmod, in0=kmod, scalar=1.0, in1=k_, op0=ALU.add, op1=ALU.mult),
                    reads=[B("kmod"), Bprv], writes=[B("kmod")])
                padd("pool", lambda e: e.tensor_tensor(out=b_sb, in0=kap, in1=a_sb, op=ALU.mult), reads=[B("kap"), B("a")], writes=[B("b")])
                padd("pool", lambda e, r_=r_: e.tensor_tensor(out=tmpa, in0=r_, in1=kmod, op=ALU.mult), reads=[Bprv, B("kmod")], writes=[B("tmpa")])
                padd("dve", lambda e: e.tensor_tensor(out=tmpa, in0=tmpa, in1=rk_bc, op=ALU.mult), reads=[B("tmpa"), Brow], writes=[B("tmpa")])
                padd("dve", lambda e, s=s: e.tensor_reduce(out=rkc[s], in_=h3(tmpa), axis=AX.X, op=ALU.add), reads=[B("tmpa")], writes=[B("rkc", s)])
                Btm = B("tm", s)
                T = tm[s]
                padd("dve", lambda e, T=T, r_=r_: e.tensor_tensor(out=T[:, 0, :], in0=r_, in1=g_in, op=ALU.mult), reads=[Bprv, B("g_in")], writes=[B("tm0", s)])
                padd("pool", lambda e, T=T: e.tensor_tensor(out=T[:, 1, :], in0=kap, in1=g_prev, op=ALU.mult), reads=[B("kap"), B("g_prev")], writes=[B("tm1", s)])
                padd("dve", lambda e, T=T: e.tensor_tensor(out=T[:, 2, :], in0=b_sb, in1=g_inv, op=ALU.mult), reads=[B("b"), B("g_inv")], writes=[B("tm2", s)])
                padd("pool", lambda e, T=T: e.tensor_tensor(out=T[:, 3, :], in0=kmod, in1=g_inv, op=ALU.mult), reads=[B("kmod"), B("g_inv")], writes=[B("tm3", s)])
                padd("dve", lambda e, T=T: e.tensor_tensor(out=T[:, 4, :], in0=b_sb, in1=g_end, op=ALU.mult), reads=[B("b"), B("g_end")], writes=[B("tm4", s)])
                padd("pool", lambda e, T=T: e.tensor_tensor(out=T[:, 5, :], in0=kmod, in1=g_end, op=ALU.mult), reads=[B("kmod"), B("g_end")], writes=[B("tm5", s)])
                padd("act", lambda e, T=T, s=s: e.copy(T[:, 6, :], v2[s]), reads=[Bv2], writes=[B("tm6", s)])
                BXT = B("XT", s)
                for (kind, src) in ((0, 1), (1, 0), (2, 2), (3, 3)):
                    bk = nb(0, 7)

                    def trx(e, bk=bk, T=T, src=src):
                        pv = bank_bf(bk)
                        for h in range(NH):
                            ins = e.transpose(pv[0:64, h * 128:(h + 1) * 128], T[:, src, h * 64:(h + 1) * 64], ident_b)
                        return ins
                    padd("pe", trx, reads=[B("tm%d" % src, s), B("identb")], writes=[Bbank[bk]])
                    eng = "act" if kind % 2 else "dve"
                    if eng == "act":
                        padd("act", lambda e, bk=bk, kind=kind, s=s: e.copy(XT[s][0:64, :, kind, :], bank_bf(bk)[0:64, :].rearrange("p (h t) -> p h t", t=128)),
                            reads=[Bbank[bk]], writes=[BXT])
                    else:
                        padd("dve", lambda e, bk=bk, kind=kind, s=s: e.tensor_copy(XT[s][0:64, :, kind, :], bank_bf(bk)[0:64, :].rearrange("p (h t) -> p h t", t=128)),
                            reads=[Bbank[bk]], writes=[BXT])
            def post(c, pull):
                s = c % 2
                T = tm[s]
                X = XT[s]
                BXT = B("XT", s)
                Bv2 = B("v2", s)
                gCT = gCTs[s]
                BgCT = B("gCT", s)
                STo, STn = ST[c % 2], ST[(c + 1) % 2]
                BSTo, BSTn = B("ST", c % 2), B("ST", (c + 1) % 2)
                STbo, STbn = STb[c % 2], STb[(c + 1) % 2]
                BSTbo, BSTbn = B("STb", c % 2), B("STb", (c + 1) % 2)
                bkY = 7
                PK = int(os.environ.get('P2PK', '0'))
                heads = list(range(NH))
                for i in heads:
                    h = i
                    bk = nb(0, 7)

                    def mma(e, bk=bk, h=h):
                        e.matmul(pbank[bk][:, 0:256], X[:, h, 2, :], X[:, h, 0:2, :], start=True, stop=True)
                        return e.matmul(pbank[bk][:, 256:512], X[:, h, 3, :], X[:, h, 0:2, :], start=True, stop=True)
                    add("pe", mma, reads=[BXT], writes=[Bbank[bk]])
                    add("dve", lambda e, bk=bk, i=i: e.tensor_tensor(out=AT[i], in0=pbank[bk][:, :], in1=mask4, op=ALU.mult),
                        reads=[Bbank[bk], Bc], writes=[B("AT", i)])
                    bk2 = nb(0, 7)
                    add("pe", lambda e, bk2=bk2, h=h: e.matmul(pbank[bk2][:, 0:128], X[:, h, 0, :], X[:, h, 2, :], start=True, stop=True),
                        reads=[BXT], writes=[Bbank[bk2]])
                    add("dve", lambda e, bk2=bk2, i=i: e.tensor_tensor(out=Q[i][0][:, 2, :], in0=pbank[bk2][:, 0:128], in1=lowbd, op=ALU.mult),
                        reads=[Bbank[bk2], Bc], writes=[B("Qw", i, 0)])
                    add("act", lambda e, bk2=bk2, i=i: e.copy(tmpm[i][:, 0:128], pbank[bk2][:, 0:128]), reads=[Bbank[bk2]], writes=[B("tmpm", i)])
                    add("pool", lambda e, i=i: e.tensor_tensor(out=Lm[i], in0=tmpm[i][:, 0:128], in1=mlow, op=ALU.mult),
                        reads=[B("tmpm", i), Bc], writes=[B("Lm", i)])
                    add("pool", lambda e, i=i: e.tensor_tensor(out=Q[i][0][:, 0, :], in0=AT[i][:, 0:128], in1=bd16, op=ALU.mult),
                        reads=[B("AT", i), Bc], writes=[B("Qy", i, 0)])
                    add("pool", lambda e, i=i: e.tensor_tensor(out=Q[i][1][:, 1:4:2, :], in0=Q[i][0][:, 0:3:2, :], in1=ident_f.unsqueeze(1).broadcast_to([128, 2, 128]), op=ALU.add),
                        reads=[B("Qy", i, 0), B("Qw", i, 0), Bc], writes=[B("Qzt", i, 1)])
                pull(PK)
                for lev in range(0, 4):
                    p_, n_ = lev % 2, (lev + 1) % 2
                    for i in heads:
                        bk = nb(0, 7)
                        Qp, Qn = Q[i][p_], Q[i][n_]
                        if lev == 0:
                            def mml(e, bk=bk, Qp=Qp):
                                e.matmul(pbank[bk][:, 0:128], Qp[:, 2, :], Qp[:, 0, :], start=True, stop=True)
                                return e.matmul(pbank[bk][:, 128:256], Qp[:, 0, :], Qp[:, 2, :], start=True, stop=True)
                            add("pe", mml, reads=[B("Qy", i, p_), B("Qw", i, p_)], writes=[Bbank[bk]])
                            add("act", lambda e, bk=bk, Qn=Qn: e.copy(Qn[:, 0:3:2, :], pbank[bk][:, 0:256].rearrange("p (a b) -> p a b", b=128)),
                                reads=[Bbank[bk]], writes=[B("Qy", i, n_), B("Qw", i, n_)])
                        elif lev < 3:
                            def mml(e, bk=bk, Qp=Qp):
                                e.matmul(pbank[bk][:, 0:256], Qp[:, 2, :], Qp[:, 0:2, :], start=True, stop=True)
                                return e.matmul(pbank[bk][:, 256:512], Qp[:, 0, :], Qp[:, 2:4, :], start=True, stop=True)
                            add("pe", mml, reads=[B("Qy", i, p_), B("Qw", i, p_), B("Qzt", i, p_)], writes=[Bbank[bk]])
                            pv4 = pbank[bk][:, :].rearrange("p (a b) -> p a b", b=128)
                            add("act", lambda e, pv4=pv4, Qn=Qn: e.copy(Qn[:, 0:3:2, :], pv4[:, 0:3:2, :]),
                                reads=[Bbank[bk]], writes=[B("Qy", i, n_), B("Qw", i, n_)])
                            add("dve", lambda e, pv4=pv4, Qn=Qn, Qp=Qp: e.tensor_tensor(out=Qn[:, 1:4:2, :], in0=pv4[:, 1:4:2, :], in1=Qp[:, 1:4:2, :], op=ALU.add),
                                reads=[Bbank[bk], B("Qzt", i, p_)], writes=[B("Qzt", i, n_)])
                        else:
                            def mml(e, bk=bk, Qp=Qp):
                                e.matmul(pbank[bk][:, 0:128], Qp[:, 2, :], Qp[:, 1, :], start=True, stop=True)
                                return e.matmul(pbank[bk][:, 128:256], Qp[:, 0, :], Qp[:, 3, :], start=True, stop=True)
                            add("pe", mml, reads=[B("Qy", i, p_), B("Qw", i, p_), B("Qzt", i, p_)], writes=[Bbank[bk]])
                            add("dve", lambda e, bk=bk, i=i, Qp=Qp: e.tensor_tensor(out=ZT[i], in0=pbank[bk][:, 0:256].rearrange("p (a b) -> p a b", b=128),
                                                                                  in1=Qp[:, 1:4:2, :], op=ALU.add),
                                reads=[Bbank[bk], B("Qzt", i, p_)], writes=[B("ZT", i)])
                    pull(PK)
                for mi in range(3):
                    MMm = cmask[:, 1280 + 256 * mi:1536 + 256 * mi]
                    bks = {}
                    for i in heads:
                        bk = nb(0, 7)

                        def mmg(e, bk=bk, i=i):
                            e.matmul(pbank[bk][:, 0:128], Lm[i], ZT[i][:, 0, :], start=True, stop=True)
                            return e.matmul(pbank[bk][:, 128:256], AT[i][:, 0:128], ZT[i][:, 1, :], start=True, stop=True)
                        add("pe", mmg, reads=[B("Lm", i), B("AT", i), B("ZT", i)], writes=[Bbank[bk]])
                        add("act", lambda e, bk=bk, i=i: e.copy(GG[i], pbank[bk][:, 0:256]), reads=[Bbank[bk]], writes=[B("GG", i)])
                    pull(PK // 2)
                    for i in heads:
                        bk = nb(0, 7)

                        def mmh(e, bk=bk, i=i):
                            e.matmul(pbank[bk][:, 0:128], ZT[i][:, 1, :], GG[i][:, 0:128], start=True, stop=True)
                            return e.matmul(pbank[bk][:, 128:256], ZT[i][:, 0, :], GG[i][:, 128:256], start=True, stop=True)
                        add("pe", mmh, reads=[B("GG", i), B("ZT", i)], writes=[Bbank[bk]])
                        add("dve", lambda e, bk=bk, i=i, MMm=MMm: e.tensor_tensor(out=tmpm[i], in0=pbank[bk][:, 0:256], in1=MMm, op=ALU.mult),
                            reads=[Bbank[bk], Bc], writes=[B("tmpm", i)])
                        add("pool", lambda e, i=i: e.tensor_tensor(out=ZT[i].rearrange("p a b -> p (a b)"), in0=ZT[i].rearrange("p a b -> p (a b)"), in1=tmpm[i], op=ALU.subtract),
                            reads=[B("ZT", i), B("tmpm", i)], writes=[B("ZT", i)])
                    pull(PK // 2)
                for i in heads:
                    h = i
                    vb_h = T[:, 6, h * 64:(h + 1) * 64]
                    bk = nb(0, 7)
                    add("pe", lambda e, bk=bk, i=i, vb_h=vb_h: e.matmul(pbank[bk][:, 0:64], AT[i][:, 256:384], vb_h, start=True, stop=True),
                        reads=[B("AT", i), B("tm6", s)], writes=[Bbank[bk]])
                    add("act", lambda e, bk=bk, i=i: e.mul(nW1[i], pbank[bk][:, 0:64], -1.0), reads=[Bbank[bk]], writes=[B("nW1", i)])
                pull(PK)
                for i in heads:
                    h = i
                    Zf = ZT[i][:, 0, :]
                    bk = nb(0, 7)

                    def mmku(e, bk=bk, i=i, h=h, Zf=Zf):
                        e.matmul(pbank[bk][:, 0:64], Zf, T[:, 1, h * 64:(h + 1) * 64], start=True, stop=True)
                        return e.matmul(pbank[bk][:, 64:128], Zf, nW1[i], start=True, stop=True)
                    add("pe", mmku, reads=[B("ZT", i), B("tm1", s), B("nW1", i)], writes=[Bbank[bk]])
                    add("dve", lambda e, bk=bk, i=i: e.tensor_copy(KU[i], pbank[bk][:, 0:128]), reads=[Bbank[bk]], writes=[B("KU", i)])
                pull(PK)
                for i in heads:
                    h = i
                    bk = nb(0, 7)

                    def mmmr(e, bk=bk, i=i, h=h):
                        e.matmul(pbank[bk][0:64, 0:64], KU[i][:, 0:64], T[:, 4, h * 64:(h + 1) * 64], start=True, stop=True)
                        return e.matmul(pbank[bk][0:64, 64:192], KU[i][:, 0:64], AT[i][:, 128:256], start=True, stop=True)
                    add("pe", mmmr, reads=[B("KU", i), B("tm4", s), B("AT", i)], writes=[Bbank[bk]])
                    add("act", lambda e, bk=bk, i=i: e.mul(Mc[i][0:64, :], pbank[bk][0:64, 0:64], -1.0), reads=[Bbank[bk]], writes=[B("Mc", i)])
                    add("dve", lambda e, bk=bk, i=i, h=h: e.tensor_tensor(out=RhT[i][0:64, :], in0=X[0:64, h, 1, :], in1=pbank[bk][0:64, 64:192], op=ALU.subtract),
                        reads=[Bbank[bk], BXT], writes=[B("RhT", i)])
                pull(PK)
                for i in heads:
                    h = i
                    vb_h = T[:, 6, h * 64:(h + 1) * 64]

                    def mmy(e, i=i, h=h, vb_h=vb_h):
                        o = pbank[bkY][:, h * 64:(h + 1) * 64]
                        e.matmul(o, RhT[i][:, :], STbo[:, h, :], start=True, stop=False)
                        e.matmul(o, AT[i][:, 128:256], KU[i][:, 64:128], start=False, stop=False)
                        return e.matmul(o, AT[i][:, 384:512], vb_h, start=False, stop=True)
                    add("pe", mmy, reads=[B("RhT", i), BSTbo, B("AT", i), B("KU", i), B("tm6", s)], writes=[Bbank[bkY]])
                    bk = nb(0, 7)

                    def mms(e, bk=bk, i=i, h=h, vb_h=vb_h):
                        o = pbank[bk][0:64, 0:64]
                        e.matmul(o, T[:, 4, h * 64:(h + 1) * 64], KU[i][:, 64:128], start=True, stop=False)
                        e.matmul(o, T[:, 5, h * 64:(h + 1) * 64], vb_h, start=False, stop=False)
                        return e.matmul(o, Mc[i][:, :], STbo[:, h, :], start=False, stop=True)
                    add("pe", mms, reads=[B("tm4", s), B("tm5", s), B("tm6", s), B("KU", i), B("Mc", i), BSTbo], writes=[Bbank[bk]])
                    add("dve", lambda e, bk=bk, h=h: e.scalar_tensor_tensor(out=STn[0:64, h, :], in0=STo[0:64, h, :], scalar=gCT[0:64, h:h + 1],
                                                                          in1=pbank[bk][0:64, 0:64], op0=ALU.mult, op1=ALU.add),
                        reads=[Bbank[bk], BSTo, BgCT], writes=[B("STh", (c + 1) % 2, h)])
                    add("act", lambda e, h=h: e.copy(STbn[0:64, h, :], STn[0:64, h, :]), reads=[B("STh", (c + 1) % 2, h)], writes=[BSTbn, BSTn])
                pull(PK)
                Bsm = B("smt")
                sm = smt
                add("act", lambda e: e.copy(ysb, pbank[bkY][:, :]), reads=[Bbank[bkY]], writes=[B("ysb")])
                add("pool", lambda e: e.tensor_tensor(out=ysq, in0=ysb, in1=ysb, op=ALU.mult), reads=[B("ysb")], writes=[B("ysq")])
                add("dve", lambda e: e.tensor_reduce(out=sm[:, 24:32], in_=h3(ysb), axis=AX.X, op=ALU.add), reads=[B("ysb")], writes=[Bsm])
                add("dve", lambda e: e.tensor_reduce(out=sm[:, 32:40], in_=h3(ysq), axis=AX.X, op=ALU.add), reads=[B("ysq")], writes=[Bsm])
                add("dve", lambda e: e.tensor_scalar(sm[:, 24:32], sm[:, 24:32], 1.0 / HD, None, ALU.mult), reads=[Bsm], writes=[Bsm])
                add("dve", lambda e: e.tensor_tensor(out=sm[:, 40:48], in0=sm[:, 24:32], in1=sm[:, 24:32], op=ALU.mult), reads=[Bsm], writes=[Bsm])
                add("dve", lambda e: e.scalar_tensor_tensor(out=sm[:, 32:40], in0=sm[:, 32:40], scalar=1.0 / HD, in1=sm[:, 40:48], op0=ALU.mult, op1=ALU.subtract),
                    reads=[Bsm], writes=[Bsm])
                add("act", lambda e: e.activation(out=sm[:, 40:48], in_=sm[:, 32:40], func=AF.Sqrt, bias=GN_EPS, scale=1.0), reads=[Bsm], writes=[Bsm])
                add("dve", lambda e: e.reciprocal(sm[:, 48:56], sm[:, 40:48]), reads=[Bsm], writes=[Bsm])
                add("dve", lambda e: e.tensor_tensor(out=h3(ysb), in0=h3(ysb), in1=sm[:, 24:32].unsqueeze(2).broadcast_to([128, NH, HD]), op=ALU.subtract),
                    reads=[B("ysb"), Bsm], writes=[B("ysb")])
                add("dve", lambda e: e.tensor_tensor(out=h3(ysb), in0=h3(ysb), in1=sm[:, 48:56].unsqueeze(2).broadcast_to([128, NH, HD]), op=ALU.mult),
                    reads=[B("ysb"), Bsm], writes=[B("ysb")])
                add("pool", lambda e: e.tensor_tensor(out=ysb, in0=ysb, in1=gnw_bc, op=ALU.mult), reads=[B("ysb"), Brow], writes=[B("ysb")])
                add("pool", lambda e: e.tensor_tensor(out=ysb, in0=ysb, in1=gnb_bc, op=ALU.add), reads=[B("ysb"), Brow], writes=[B("ysb")])
                add("dve", lambda e: e.tensor_tensor(out=h3(ysq), in0=h3(v2[s]), in1=rkc[s].unsqueeze(2).broadcast_to([128, NH, HD]), op=ALU.mult),
                    reads=[Bv2, B("rkc", s)], writes=[B("ysq")])
                add("pool", lambda e: e.tensor_tensor(out=ysb, in0=ysb, in1=ysq, op=ALU.add), reads=[B("ysb"), B("ysq")], writes=[B("ysb")])
                add("dve", lambda e: e.tensor_tensor(out=rwo[s], in0=ysb, in1=gate[s], op=ALU.mult), reads=[B("ysb"), B("gate", s)], writes=[B("rwo", s)])
                add("sp", lambda e: e.dma_start(out=rwo_d[c * 128:(c + 1) * 128, :], in_=rwo[s]), reads=[B("rwo", s)], dma=True)
            plist = []

            def padd(*a, **k):
                plist.append((a, k))

            plim = {"n": 0}
            PLIM = int(os.environ.get('P2LIM', '100000'))

            def pull(n):
                for _ in range(n):
                    if not plist:
                        return
                    if n < 10 ** 8 and plim["n"] >= PLIM:
                        return
                    plim["n"] += 1
                    a, k = plist.pop(0)
                    add(*a, **k)

            pro(0, padd)
            pull(10 ** 9)
            for c in range(NT):
                if c + 1 < NT:
                    pro(c + 1, padd)
                if os.environ.get('P2NOIL'):
                    pull(10 ** 9)
                plim["n"] = 0
                post(c, pull)
                pull(10 ** 9)
            S_.barrier()

        def phase3(L):
            A.off = P0
            NSL = 6
            PD = 3
            qk = A.alloc((8, S), BF16)
            vaug = A.alloc((NT, NH, HD + 1), BF16)
            mbig = A.alloc((MBW,), BF16)
            esb = [A.alloc((512,), BF16) for _ in range(NSL)]
            psb = [A.alloc((512,), BF16) for _ in range(NSL)]
            osb = [A.alloc((4, 512), F32) for _ in range(2)]
            rec = [A.alloc((4,), F32) for _ in range(2)]
            for m in range(8):
                add("sp", lambda e, m=m: e.dma_start(out=qk[:, m, :], in_=qkT_d[m * 128:(m + 1) * 128, :]), writes=[B("qk", m)], dma=True)
            for t in range(NT):
                add("sp", lambda e, t=t: e.dma_start(out=vaug[:, t, :, 0:HD], in_=vbf_d[t * 128:(t + 1) * 128, :].rearrange("p (h c) -> p h c", c=HD)),
                    writes=[B("vaug", t)], dma=True)
            add("pool", lambda e: e.dma_start(out=mbig, in_=mbig_d), writes=[B("mbig")], dma=True)
            add("pool", lambda e: e.memset(vaug[:, :, :, HD:HD + 1], 1.0), writes=[B("vones")])
            Bqk = [B("qk", m) for m in range(8)]
            units = []
            for sb in range(S // 512):
                q0 = sb * 512
                kt_lo = max(0, (q0 - 2048) // 128)
                kt_hi = (q0 + 511) // 128
                for h in range(NH):
                    for kt in range(kt_lo, kt_hi + 1):
                        units.append((sb, h, kt, kt == kt_lo, kt == kt_hi))

            def front(u, idx):
                sb, h, kt, first, last = u
                q0 = sb * 512
                ph = (h % 2) * 64
                Dd = q0 - kt * 128
                bs = nb(0, 6)
                es = idx % NSL
                add("pe", lambda e: e.matmul(pbank[bs][:, :], qk[ph:ph + 64, 4 + h // 2, kt * 128:(kt + 1) * 128], qk[ph:ph + 64, h // 2, q0:q0 + 512],
                                             start=True, stop=True),
                    reads=[Bqk[h // 2], Bqk[4 + h // 2]], writes=[Bbank[bs]])
                add("act", lambda e: e.activation(out=esb[es], in_=pbank[bs][:, :], func=AF.Exp, scale=1.0 / 8.0),
                    reads=[Bbank[bs]], writes=[B("esb", es)])
                add("dve", lambda e: e.tensor_tensor(out=psb[es], in0=esb[es], in1=mbig[:, Dd + 384:Dd + 384 + 512], op=ALU.mult),
                    reads=[B("esb", es), B("mbig")], writes=[B("psb", es)])

            def back(u, idx):
                sb, h, kt, first, last = u
                q0 = sb * 512
                Dd = q0 - kt * 128
                es = idx % NSL
                bacc = 6 + (h % 2)
                os_ = sb % 2
                qss = [qs for qs in range(4) if (Dd + qs * 128 + 127 >= 0) and (Dd + qs * 128 - 127 <= 2048)]

                def mmpv(e):
                    ins = None
                    for qs in qss:
                        ins = e.matmul(pbank[bacc][:, qs * 65:(qs + 1) * 65], psb[es][:, qs * 128:(qs + 1) * 128], vaug[:, kt, h, :],
                                       start=(first and qs == qss[0]), stop=last, skip_group_check=True)
                    return ins
                add("pe", mmpv, reads=[B("psb", es), B("vaug", kt), B("vones")], writes=[Bbank[bacc]])
                if last:
                    accv = pbank[bacc][:, 0:260].rearrange("p (q c) -> p q c", c=65)
                    add("dve", lambda e: e.reciprocal(rec[os_].unsqueeze(2), accv[:, :, 64:65]), reads=[Bbank[bacc]], writes=[B("rec", os_)])
                    add("dve", lambda e: e.tensor_tensor(out=osb[os_][:, :, h * 64:(h + 1) * 64], in0=accv[:, :, 0:64],
                                                         in1=rec[os_].unsqueeze(2).broadcast_to([128, 4, 64]), op=ALU.mult),
                        reads=[Bbank[bacc], B("rec", os_)], writes=[B("osb", os_)])
                    if h == NH - 1:
                        add("sp", lambda e: e.dma_start(out=atto_d[q0:q0 + 512, :].rearrange("(q p) c -> p q c", p=128), in_=osb[os_]),
                            reads=[B("osb", os_)], dma=True)

            n = len(units)
            for idx in range(n + PD):
                if idx < n:
                    front(units[idx], idx)
                if idx >= PD:
                    back(units[idx - PD], idx - PD)
            S_.barrier()

        def phase4(L, xsrc):
            A.off = P0
            wu_n = A.alloc((8, DFF), BF16)
            wd_n = A.alloc((32, D), BF16)
            for kc in range(8):
                add("pool", lambda e, kc=kc: e.dma_start(out=wu_n[:, kc, :], in_=w_up[L, kc * 128:(kc + 1) * 128, :]), writes=[B("wu", kc)], dma=True)
            for g in range(4):
                add("pool", lambda e, g=g: e.dma_start(out=wd_n[:, g * 8:(g + 1) * 8, :], in_=w_dn[L, g * 1024:(g + 1) * 1024, :].rearrange("(m p) d -> p m d", p=128)),
                    writes=[B("wd", g)], dma=True)
            wo = A.alloc((8, D), BF16)
            wo_f = A.alloc((2, D), F32)
            gpo = A.alloc((D,), F32)
            aog = A.alloc((512,), F32)
            at = [A.alloc((512,), F32) for _ in range(2)]
            cat = [A.alloc((D,), BF16) for _ in range(2)]
            catT = [A.alloc((8, 128), BF16) for _ in range(2)]
            xt = [A.alloc((D,), F32) for _ in range(2)]
            xo = [A.alloc((D,), F32) for _ in range(2)]
            junk = A.alloc((D,), BF16)
            st4 = [A.alloc((8,), F32) for _ in range(2)]
            for kc in range(8):
                add("sp", lambda e, kc=kc: e.dma_start(out=wo_f[:, kc % 2, :], in_=w_out[L, kc * 128:(kc + 1) * 128, :]), writes=[B("wof", kc % 2)], dma=True)
                add("dve", lambda e, kc=kc: e.tensor_copy(wo[:, kc, :], wo_f[:, kc % 2, :]), reads=[B("wof", kc % 2)], writes=[B("wo", kc)])
            add("sp", lambda e: e.dma_start(out=gpo, in_=gpost_d[L, 0:1, :].partition_broadcast(128)), writes=[B("gpo")], dma=True)
            add("sp", lambda e: e.dma_start(out=aog, in_=rowp_d[L:L + 1, 8 * 512:9 * 512].partition_broadcast(128)), writes=[B("aog")], dma=True)
            Bwo = [B("wo", kc) for kc in range(8)]
            def p4front(t):
                s = t % 2
                add("sp", lambda e, s=s, t=t: e.dma_start(out=at[s], in_=atto_d[t * 128:(t + 1) * 128, :]), writes=[B("at", s)], dma=True)
                add("sp", lambda e, s=s, t=t: e.dma_start(out=cat[s][:, 512:1024], in_=rwo_d[t * 128:(t + 1) * 128, :]), writes=[B("catr", s)], dma=True)
                add("sp", lambda e, s=s, t=t: e.dma_start(out=xt[s], in_=xsrc[t * 128:(t + 1) * 128, :]), writes=[B("xt", s)], dma=True)
                Bst = B("st4", s)
                add("act", lambda e, s=s: e.activation(out=junk[:, 0:512], in_=at[s], func=AF.Square, accum_out=st4[s][:, 0:1]), reads=[B("at", s)], writes=[Bst])
                add("act", lambda e, s=s: e.activation(out=st4[s][:, 1:2], in_=st4[s][:, 0:1], func=AF.Sqrt, bias=EPS, scale=1.0 / DA), reads=[Bst], writes=[Bst])
                add("dve", lambda e, s=s: e.reciprocal(st4[s][:, 2:3], st4[s][:, 1:2]), reads=[Bst], writes=[Bst])
                add("dve", lambda e, s=s: e.scalar_tensor_tensor(out=cat[s][:, 0:512], in0=at[s], scalar=st4[s][:, 2:3], in1=aog, op0=ALU.mult, op1=ALU.mult),
                    reads=[B("at", s), Bst, B("aog")], writes=[B("cata", s)])
                tb = nb(0, 2)

                def tr(e, s=s, tb=tb):
                    pv = bank_bf(tb)
                    for kc in range(8):
                        ins = e.transpose(pv[:, kc * 128:(kc + 1) * 128], cat[s][:, kc * 128:(kc + 1) * 128], ident_b)
                    return ins
                add("pe", tr, reads=[B("cata", s), B("catr", s), B("identb")], writes=[Bbank[tb]])
                add("act", lambda e, s=s, tb=tb: e.copy(catT[s].rearrange("p k t -> p (k t)"), bank_bf(tb)), reads=[Bbank[tb]], writes=[B("catT", s)])
            def p4back(t):
                s = t % 2
                Bst = B("st4", s)
                bks = []
                for n in range(2):
                    bk = nb(2, 8)
                    bks.append(bk)

                    def mmo(e, s=s, bk=bk, n=n):
                        for kc in range(8):
                            ins = e.matmul(pbank[bk][:, :], catT[s][:, kc, :], wo[:, kc, n * 512:(n + 1) * 512], start=(kc == 0), stop=(kc == 7))
                        return ins
                    add("pe", mmo, reads=Bwo + [B("catT", s)], writes=[Bbank[bk]])
                    add("act", lambda e, s=s, bk=bk, n=n: e.activation(out=junk[:, 0:512], in_=pbank[bk][:, :], func=AF.Square, accum_out=st4[s][:, 3 + n:4 + n]),
                        reads=[Bbank[bk]], writes=[B("st4b", s, n)])
                Bst2 = B("st4c", s)
                add("dve", lambda e, s=s: e.tensor_tensor(out=st4[s][:, 5:6], in0=st4[s][:, 3:4], in1=st4[s][:, 4:5], op=ALU.add),
                    reads=[B("st4b", s, 0), B("st4b", s, 1)], writes=[Bst2])
                add("act", lambda e, s=s: e.activation(out=st4[s][:, 6:7], in_=st4[s][:, 5:6], func=AF.Sqrt, bias=EPS, scale=1.0 / D), reads=[Bst2], writes=[Bst2])
                add("dve", lambda e, s=s: e.reciprocal(st4[s][:, 7:8], st4[s][:, 6:7]), reads=[Bst2], writes=[Bst2])
                for n in range(2):
                    bk = bks[n]
                    add("dve", lambda e, s=s, bk=bk, n=n: e.scalar_tensor_tensor(out=xo[s][:, n * 512:(n + 1) * 512], in0=pbank[bk][:, :], scalar=st4[s][:, 7:8],
                                                                              in1=gpo[:, n * 512:(n + 1) * 512], op0=ALU.mult, op1=ALU.mult),
                        reads=[Bbank[bk], Bst2, B("gpo")], writes=[B("xo", s, n)])
                add("pool", lambda e, s=s: e.tensor_tensor(out=xo[s], in0=xo[s], in1=xt[s], op=ALU.add),
                    reads=[B("xo", s, 0), B("xo", s, 1), B("xt", s)], writes=[B("xo", s, 0), B("xo", s, 1)])
                add("sp", lambda e, s=s, t=t: e.dma_start(out=xmid_d[t * 128:(t + 1) * 128, :], in_=xo[s]), reads=[B("xo", s, 0), B("xo", s, 1)], dma=True)
            p4front(0)
            for t in range(NT):
                if t + 1 < NT:
                    p4front(t + 1)
                p4back(t)
            S_.barrier()

        def phase5(L, xdst, outbufs):
            A.off = P0
            TB = 256
            NB5 = S // TB
            wu = A.alloc((8, DFF), BF16)
            wd = A.alloc((32, D), BF16)
            gpre = A.alloc((8,), F32)
            gpo = A.alloc((D,), F32)
            xt = [A.alloc((2, D), F32) for _ in range(2)]
            xn = [A.alloc((D,), BF16) for _ in range(2)]
            hT = [A.alloc((8, TB), BF16) for _ in range(2)]
            aT = A.alloc((32, TB), BF16)
            rl = [A.alloc((TB,), BF16) for _ in range(3)]
            xo = [A.alloc((D,), F32) for _ in range(2)]
            junk = A.alloc((D,), BF16)
            st5 = [A.alloc((8,), F32) for _ in range(2)]
            if not (phases is None or 4 in phases):
                for kc in range(8):
                    add("pool", lambda e, kc=kc: e.dma_start(out=wu[:, kc, :], in_=w_up[L, kc * 128:(kc + 1) * 128, :]), writes=[B("wu", kc)], dma=True)
                for g in range(4):
                    add("pool", lambda e, g=g: e.dma_start(out=wd[:, g * 8:(g + 1) * 8, :], in_=w_dn[L, g * 1024:(g + 1) * 1024, :].rearrange("(m p) d -> p m d", p=128)),
                        writes=[B("wd", g)], dma=True)
            add("sp", lambda e: e.dma_start(out=gpre, in_=gpre_d[L, 1]), writes=[B("gpre")], dma=True)
            add("sp", lambda e: e.dma_start(out=gpo, in_=gpost_d[L, 1:2, :].partition_broadcast(128)), writes=[B("gpo")], dma=True)
            Bwu = [B("wu", kc) for kc in range(8)]
            Bwd = [B("wd", g) for g in range(4)]
            cnt = 0
            def p5front(b):
                bs = b % 2
                BhT = B("hT", bs)
                for ts in range(2):
                    t = 2 * b + ts
                    s = t % 2
                    Bx = B("xt", bs, ts)
                    add("sp", lambda e, bs=bs, ts=ts, t=t: e.dma_start(out=xt[bs][:, ts, :], in_=xmid_d[t * 128:(t + 1) * 128, :]), writes=[Bx], dma=True)
                    Bst = B("st5", s)
                    add("act", lambda e, bs=bs, ts=ts, s=s: e.activation(out=junk, in_=xt[bs][:, ts, :], func=AF.Square, accum_out=st5[s][:, 0:1]), reads=[Bx], writes=[Bst])
                    add("act", lambda e, s=s: e.activation(out=st5[s][:, 1:2], in_=st5[s][:, 0:1], func=AF.Sqrt, bias=EPS, scale=1.0 / D), reads=[Bst], writes=[Bst])
                    add("dve", lambda e, s=s: e.reciprocal(st5[s][:, 2:3], st5[s][:, 1:2]), reads=[Bst], writes=[Bst])
                    add("dve", lambda e, bs=bs, ts=ts, s=s: e.tensor_scalar(xn[s], xt[bs][:, ts, :], st5[s][:, 2:3], None, ALU.mult), reads=[Bx, Bst], writes=[B("xn", s)])
                    tb = nb(0, 2)

                    def tr(e, s=s, tb=tb):
                        pv = bank_bf(tb)
                        for kc in range(8):
                            ins = e.transpose(pv[:, kc * 128:(kc + 1) * 128], xn[s][:, kc * 128:(kc + 1) * 128], ident_b)
                        return ins
                    add("pe", tr, reads=[B("xn", s), B("identb")], writes=[Bbank[tb]])
                    add("dve", lambda e, tb=tb, bs=bs, ts=ts: e.tensor_tensor(
                        out=hT[bs][:, :, ts * 128:(ts + 1) * 128], in0=bank_bf(tb).rearrange("p (k t) -> p k t", t=128),
                        in1=gpre.unsqueeze(2).broadcast_to([128, 8, 128]), op=ALU.mult),
                        reads=[Bbank[tb], B("gpre")], writes=[BhT])
            def p5back(b):
                bs = b % 2
                BhT = B("hT", bs)
                cnt = cnt5[0]
                for m2 in range(16):
                    bk = nb(2, 6)

                    def mmu(e, bk=bk, m2=m2, bs=bs):
                        for j in range(2):
                            m = 2 * m2 + j
                            for kc in range(8):
                                ins = e.matmul(pbank[bk][:, j * TB:(j + 1) * TB], wu[:, kc, m * 128:(m + 1) * 128], hT[bs][:, kc, :], start=(kc == 0), stop=(kc == 7))
                        return ins
                    add("pe", mmu, reads=Bwu + [BhT], writes=[Bbank[bk]])
                    for j in range(2):
                        m = 2 * m2 + j
                        rs = cnt % 3
                        cnt += 1
                        cnt5[0] = cnt
                        add("act", lambda e, bk=bk, j=j, rs=rs: e.activation(out=rl[rs], in_=pbank[bk][:, j * TB:(j + 1) * TB], func=AF.Relu),
                            reads=[Bbank[bk]], writes=[B("rl", rs)])
                        eng = "pool" if m % 2 else "dve"
                        add(eng, lambda e, m=m, rs=rs: e.tensor_tensor(out=aT[:, m, :], in0=rl[rs], in1=rl[rs], op=ALU.mult),
                            reads=[B("rl", rs)], writes=[B("aT", m)])
                BaT = [B("aT", m) for m in range(32)]
                for ts in range(2):
                    t = 2 * b + ts
                    s = t % 2
                    bks = []
                    for n in range(2):
                        bk = nb(6, 8) if False else (6 + n)
                        bks.append(bk)

                        def mmd(e, bk=bk, ts=ts, n=n):
                            for m in range(32):
                                ins = e.matmul(pbank[bk][:, :], aT[:, m, ts * 128:(ts + 1) * 128], wd[:, m, n * 512:(n + 1) * 512], start=(m == 0), stop=(m == 31))
                            return ins
                        add("pe", mmd, reads=Bwd + BaT, writes=[Bbank[bk]])
                        add("act", lambda e, s=s, bk=bk, n=n: e.activation(out=junk[:, 0:512], in_=pbank[bk][:, :], func=AF.Square, accum_out=st5[s][:, 3 + n:4 + n]),
                            reads=[Bbank[bk]], writes=[B("st5b", s, n)])
                    Bst2 = B("st5c", s)
                    add("dve", lambda e, s=s: e.tensor_tensor(out=st5[s][:, 5:6], in0=st5[s][:, 3:4], in1=st5[s][:, 4:5], op=ALU.add),
                        reads=[B("st5b", s, 0), B("st5b", s, 1)], writes=[Bst2])
                    add("act", lambda e, s=s: e.activation(out=st5[s][:, 6:7], in_=st5[s][:, 5:6], func=AF.Sqrt, bias=EPS, scale=1.0 / D), reads=[Bst2], writes=[Bst2])
                    add("dve", lambda e, s=s: e.reciprocal(st5[s][:, 7:8], st5[s][:, 6:7]), reads=[Bst2], writes=[Bst2])
                    for n in range(2):
                        bk = bks[n]
                        add("dve", lambda e, s=s, bk=bk, n=n: e.scalar_tensor_tensor(out=xo[s][:, n * 512:(n + 1) * 512], in0=pbank[bk][:, :], scalar=st5[s][:, 7:8],
                                                                                  in1=gpo[:, n * 512:(n + 1) * 512], op0=ALU.mult, op1=ALU.mult),
                            reads=[Bbank[bk], Bst2, B("gpo")], writes=[B("xo", s, n)])
                    add("pool", lambda e, s=s, bs=bs, ts=ts: e.tensor_tensor(out=xo[s], in0=xo[s], in1=xt[bs][:, ts, :], op=ALU.add),
                        reads=[B("xo", s, 0), B("xo", s, 1), B("xt", bs, ts)], writes=[B("xo", s, 0), B("xo", s, 1)])
                    ob = B("outd", L, t)
                    outbufs.append(ob)
                    add("sp", lambda e, s=s, t=t: e.dma_start(out=xdst[t * 128:(t + 1) * 128, :], in_=xo[s]), reads=[B("xo", s, 0), B("xo", s, 1)], writes=[ob], dma=True)
            cnt5 = [0]
            p5front(0)
            for b in range(NB5):
                if b + 1 < NB5:
                    p5front(b + 1)
                p5back(b)
            S_.barrier()

        outbufs = []
        for L in range(depth):
            xsrc = x_in if L == 0 else xres_d
            xdst = out_d if L == depth - 1 else xres_d
            if phases is None or 1 in phases:
                phase1(L, xsrc)
            if phases is None or 2 in phases:
                phase2(L)
            if phases is None or 3 in phases:
                phase3(L)
            if phases is None or 4 in phases:
                phase4(L, xsrc)
            if phases is None or 5 in phases:
                phase5(L, xdst, outbufs)
        S_.emit()
    return nc


def _count_mask():
    ki = np.arange(128)[:, None]
    xx = np.arange(MBW)[None, :]
    d = xx - 384 - ki
    c = ((d >= 0) & (d <= 128)).astype(np.float32)
    c += ((d >= 0) & (d <= 512) & (d % 4 == 0)).astype(np.float32)
    c += ((d >= 0) & (d <= 2048) & (d % 16 == 0)).astype(np.float32)
    return np.ascontiguousarray(c, dtype=np.float32)


def _const_masks():
    p = np.arange(128)[:, None]
    f = np.arange(128)[None, :]
    ident = (p == f).astype(np.float32)
    ut = (p <= f).astype(np.float32)
    strict = (p < f).astype(np.float32)
    incl = (p <= f).astype(np.float32)
    low = (f < p).astype(np.float32)
    ones = np.ones((128, 128), np.float32)
    bd16 = ((p // 16) == (f // 16)).astype(np.float32)
    lowbd = low * bd16
    mms = []
    for bs in (16, 32, 64):
        mu_ = (((p // bs) % 2 == 0) & ((f // bs) == (p // bs) + 1)).astype(np.float32)
        mms += [mu_, np.ascontiguousarray(mu_.T)]
    return np.ascontiguousarray(np.concatenate([ident, ut, strict, incl, strict, incl, low, ones, -bd16, -lowbd] + mms, axis=1))


def pack_inputs(depth, norm_mix_pre, norm_mix_post, norm_ffn_pre, norm_ffn_post, w_in_first, w_in_rest,
                mu_shift, mu_shift_mv, attn_out_gain, decay_w0, decay_up, aaa_a0, aaa_up, mv_v0, mv_up,
                gate_up, k_k, k_a, r_k, gn_w, gn_b, w_out, w_ffn_up, w_ffn_down):
    f = np.float32
    w_in = np.zeros((depth, D, NCOLS), f)
    w_in[0, :, :w_in_first.shape[1]] = w_in_first
    for i in range(1, depth):
        w_in[i] = w_in_rest[i - 1]
    gpre = np.zeros((depth, 2, 128, 8), f)
    gpost = np.zeros((depth, 2, D), f)
    mu = np.zeros((depth, NZ), f)
    rowp = np.zeros((depth, 9, 512), f)
    lora = np.zeros((depth, 4, 96, 512), f)
    for i in range(depth):
        gpre[i, 0] = np.asarray(norm_mix_pre[i]).reshape(8, 128).T
        gpre[i, 1] = np.asarray(norm_ffn_pre[i]).reshape(8, 128).T
        gpost[i, 0] = norm_mix_post[i]
        gpost[i, 1] = norm_ffn_post[i]
        mu[i, :1696] = mu_shift[i]
        rowp[i, 0] = decay_w0[i]
        rowp[i, 1] = aaa_a0[i]
        rowp[i, 3] = k_k[i]
        rowp[i, 4] = k_a[i]
        rowp[i, 5] = np.asarray(r_k[i]).reshape(512)
        rowp[i, 6] = gn_w[i]
        rowp[i, 7] = gn_b[i]
        rowp[i, 8] = attn_out_gain[i]
        lora[i, 0, :32] = decay_up[i]
        lora[i, 1, :32] = aaa_up[i]
        lora[i, 3, :96] = gate_up[i]
        if i > 0:
            mu[i, 1696:] = mu_shift_mv[i - 1]
            rowp[i, 2] = mv_v0[i - 1]
            lora[i, 2, :32] = mv_up[i - 1]
    return {
        "w_in": w_in, "w_out": np.ascontiguousarray(w_out[:depth], f), "w_up": np.ascontiguousarray(w_ffn_up[:depth], f),
        "w_dn": np.ascontiguousarray(w_ffn_down[:depth], f), "gpre": gpre, "gpost": gpost, "mu": mu,
        "rowp": rowp.reshape(depth, 9 * 512), "lora": lora, "cmask": _const_masks(), "mbig": _count_mask(),
    }


_CACHE = {}


def kernel(x, **params):
    x = np.asarray(x, np.float32)
    Bn, S, _ = x.shape
    depth = 4
    params = {k: np.asarray(v, np.float32) for k, v in params.items()}
    shared = pack_inputs(depth, **params)
    key = (S, depth)
    if key not in _CACHE:
        _CACHE[key] = build_program(S, depth)
    nc = _CACHE[key]
    in_maps = []
    for b in range(Bn):
        m = dict(shared)
        m["x"] = np.ascontiguousarray(x[b])
        in_maps.append(m)
    res = run_bass_kernel_spmd(nc, in_maps, core_ids=list(range(Bn)))
    return np.stack([np.asarray(r["out"], np.float32) for r in res.results], axis=0)
```
